# Optimizing a Trainium2 kernel written in Bass

```python
import jax, jax.numpy as jnp
from jax import lax
import numpy as np

D_MODEL = 2048
BATCH = 4
SEQ = 2048
DEPTH = 1
DEC_BATCH = 128
DEC_SEQ = 4
PAST_LEN = 16384
PAGE_SIZE = 128

MEM_LEN = 256
EPS = 1e-6
CONV_WIDTH = 3
D_CONV = D_MODEL // 2
D_GLA = D_MODEL - D_CONV
GLA_HEADS = 4
GLA_DV = D_GLA // GLA_HEADS
GLA_DK = GLA_DV // 2
GLA_RANK = 16
GLA_TAU = 16.0
GLA_CHUNK = 64
X_HEADS = 4
X_HD = D_MODEL // X_HEADS
D_FF = 5632
SIZES_IN = (D_CONV, D_CONV, D_CONV, GLA_HEADS * GLA_DK, GLA_HEADS * GLA_DK,
            GLA_HEADS * GLA_DV, GLA_HEADS * GLA_DV, GLA_RANK)
N_IN = sum(SIZES_IN)

kernel_name = "hymba_conv_gla_memxattn_convffn_step"


def rmsnorm(x, g):
    xf = x.astype(jnp.float32)
    y = xf * lax.rsqrt(jnp.mean(xf * xf, axis=-1, keepdims=True) + EPS)
    return (y * g.astype(jnp.float32)).astype(x.dtype)


def causal_conv(u, prev, w):
    t = u.shape[1]
    full = jnp.concatenate([prev.astype(u.dtype), u], axis=1)
    y = sum(w[i] * full[:, i:i + t] for i in range(CONV_WIDTH))
    return y, full[:, -(CONV_WIDTH - 1):]


def gla_chunked(q, k, v, logf, s0):
    b, t, h, _ = q.shape
    dv = v.shape[-1]
    c = min(GLA_CHUNK, t)
    pad = (-t) % c
    n = (t + pad) // c

    def blocks(a):
        a = jnp.pad(a.astype(jnp.float32), ((0, 0), (0, pad), (0, 0), (0, 0)))
        return a.reshape(b, n, c, h, a.shape[-1]).transpose(1, 0, 3, 2, 4)

    qb, kb, vb, gb = blocks(q), blocks(k), blocks(v), blocks(logf)
    mask = jnp.tril(jnp.ones((c, c), bool))

    def step(s, inp):
        qc, kc, vc, gc = inp
        cum = jnp.cumsum(gc, axis=2)
        diff = cum[:, :, :, None, :] - cum[:, :, None, :, :]
        decay = jnp.exp(jnp.where(mask[:, :, None], diff, -jnp.inf))
        a = jnp.einsum('bhid,bhjd,bhijd->bhij', qc, kc, decay)
        o = (jnp.einsum('bhij,bhjv->bhiv', a, vc)
             + jnp.einsum('bhid,bhdv->bhiv', qc * jnp.exp(cum), s))
        last = cum[:, :, -1:, :]
        s = (jnp.exp(last[:, :, 0, :])[..., None] * s
             + jnp.einsum('bhjd,bhjv->bhdv', kc * jnp.exp(last - cum), vc))
        return s, o

    s, o = lax.scan(step, s0.astype(jnp.float32), (qb, kb, vb, gb))
    o = o.transpose(1, 0, 3, 2, 4).reshape(b, n * c, h, dv)[:, :t]
    return o, s


def mixer(xn, conv_prev, s0, w_in, conv_w, w_gate2, b_gate, gla_norm, w_out):
    b, t, _ = xn.shape
    proj = xn @ w_in
    offs = [0]
    for sz in SIZES_IN:
        offs.append(offs[-1] + sz)
    bg, cg, vc, q, k, v, r, g1 = [proj[..., offs[i]:offs[i + 1]] for i in range(len(SIZES_IN))]
    yc, conv_new = causal_conv(cg * vc, conv_prev, conv_w)
    conv_out = bg * yc
    q = q.reshape(b, t, GLA_HEADS, GLA_DK) * (GLA_DK ** -0.5)
    k = k.reshape(b, t, GLA_HEADS, GLA_DK)
    v = v.reshape(b, t, GLA_HEADS, GLA_DV)
    z = (g1 @ w_gate2 + b_gate).astype(jnp.float32)
    logf = (jax.nn.log_sigmoid(z) / GLA_TAU).reshape(b, t, GLA_HEADS, GLA_DK)
    o, s_new = gla_chunked(q, k, v, logf, s0)
    o = rmsnorm(o.astype(xn.dtype), gla_norm.reshape(GLA_HEADS, GLA_DV)).reshape(b, t, D_GLA)
    o = o * jax.nn.silu(r)
    out = jnp.concatenate([conv_out, o], axis=-1) @ w_out
    return out, conv_new, s_new.astype(s0.dtype)


def cross_attn(hn, mk, mv, w_q, w_o):
    b, t, _ = hn.shape
    q = (hn @ w_q).reshape(b, t, X_HEADS, X_HD)
    s = jnp.einsum('bthd,bmhd->bhtm', q, mk).astype(jnp.float32) * (X_HD ** -0.5)
    p = jax.nn.softmax(s, axis=-1).astype(mv.dtype)
    o = jnp.einsum('bhtm,bmhd->bthd', p, mv).reshape(b, t, D_MODEL)
    return o @ w_o


def conv_ffn(hn, prev, w_g, w_u, c_w, c_b, w_d):
    gc, new_prev = causal_conv(hn @ w_g, prev, c_w)
    return (jax.nn.silu(gc + c_b) * (hn @ w_u)) @ w_d, new_prev


def layer(x, conv_prev, s0, ffn_prev, mk, mv, nm, w_in, conv_w, w_gate2, b_gate, gla_norm,
          w_out, nx, w_xq, w_xo, nf, w_fg, w_fu, fc_w, fc_b, w_fd):
    m, conv_new, s_new = mixer(rmsnorm(x, nm), conv_prev, s0, w_in, conv_w, w_gate2, b_gate,
                               gla_norm, w_out)
    h = x + m
    h = h + cross_attn(rmsnorm(h, nx), mk, mv, w_xq, w_xo)
    f, ffn_new = conv_ffn(rmsnorm(h, nf), ffn_prev, w_fg, w_fu, fc_w, fc_b, w_fd)
    return h + f, conv_new, s_new, ffn_new


def setup_inputs(seed: int = 0) -> dict:
    key = jax.random.key(seed)
    ks = iter(jax.random.split(key, 40))

    def nrm(shape, scale):
        return jax.random.normal(next(ks), shape, jnp.float32) * scale

    def gain(shape):
        return 1.0 + nrm(shape, 0.02)

    L = DEPTH
    return {
        "x_prompt": nrm((BATCH, SEQ, D_MODEL), 1.0),
        "x_sample": nrm((DEC_BATCH, DEC_SEQ, D_MODEL), 1.0),
        "mem_prompt": nrm((BATCH, MEM_LEN, D_MODEL), 1.0),
        "cache_conv": nrm((L, DEC_BATCH, CONV_WIDTH - 1, D_CONV), 0.5),
        "state_gla": nrm((L, DEC_BATCH, GLA_HEADS, GLA_DK, GLA_DV), 0.3),
        "cache_ffn": nrm((L, DEC_BATCH, CONV_WIDTH - 1, D_FF), 1.0),
        "cache_mem_k": nrm((L, DEC_BATCH, MEM_LEN, X_HEADS, X_HD), 1.0),
        "cache_mem_v": nrm((L, DEC_BATCH, MEM_LEN, X_HEADS, X_HD), 1.0),
        "norm_mix": gain((L, D_MODEL)),
        "w_in": nrm((L, D_MODEL, N_IN), D_MODEL ** -0.5),
        "conv_w": nrm((L, CONV_WIDTH, D_CONV), CONV_WIDTH ** -0.5),
        "w_gate2": nrm((L, GLA_RANK, GLA_HEADS * GLA_DK), GLA_RANK ** -0.5),
        "b_gate": nrm((L, GLA_HEADS * GLA_DK), 0.1),
        "gla_norm": gain((L, D_GLA)),
        "w_out": nrm((L, D_MODEL, D_MODEL), D_MODEL ** -0.5),
        "norm_x": gain((L, D_MODEL)),
        "norm_mem": gain((L, D_MODEL)),
        "w_xq": nrm((L, D_MODEL, D_MODEL), D_MODEL ** -0.5),
        "w_xk": nrm((L, D_MODEL, D_MODEL), D_MODEL ** -0.5),
        "w_xv": nrm((L, D_MODEL, D_MODEL), D_MODEL ** -0.5),
        "w_xo": nrm((L, D_MODEL, D_MODEL), D_MODEL ** -0.5),
        "norm_ffn": gain((L, D_MODEL)),
        "w_ffn_gate": nrm((L, D_MODEL, D_FF), D_MODEL ** -0.5),
        "w_ffn_up": nrm((L, D_MODEL, D_FF), D_MODEL ** -0.5),
        "ffn_conv_w": nrm((L, CONV_WIDTH, D_FF), CONV_WIDTH ** -0.5),
        "ffn_conv_b": nrm((L, D_FF), 0.02),
        "w_ffn_down": nrm((L, D_FF, D_MODEL), D_FF ** -0.5),
        "norm_final": gain((D_MODEL,)),
    }


def reference(x_prompt, x_sample, mem_prompt, cache_conv, state_gla, cache_ffn, cache_mem_k,
              cache_mem_v, norm_mix, w_in, conv_w, w_gate2, b_gate, gla_norm, w_out, norm_x,
              norm_mem, w_xq, w_xk, w_xv, w_xo, norm_ffn, w_ffn_gate, w_ffn_up, ffn_conv_w,
              ffn_conv_b, w_ffn_down, norm_final):
    hp, hs = x_prompt, x_sample
    conv_p, gla_p, ffn_p, mk_p, mv_p = [], [], [], [], []
    conv_s, gla_s, ffn_s = [], [], []
    for l in range(DEPTH):
        shared = (norm_mix[l], w_in[l], conv_w[l], w_gate2[l], b_gate[l], gla_norm[l], w_out[l],
                  norm_x[l], w_xq[l], w_xo[l], norm_ffn[l], w_ffn_gate[l], w_ffn_up[l],
                  ffn_conv_w[l], ffn_conv_b[l], w_ffn_down[l])
        memn = rmsnorm(mem_prompt, norm_mem[l])
        mk = (memn @ w_xk[l]).reshape(BATCH, MEM_LEN, X_HEADS, X_HD)
        mv = (memn @ w_xv[l]).reshape(BATCH, MEM_LEN, X_HEADS, X_HD)
        z_conv = jnp.zeros((BATCH, CONV_WIDTH - 1, D_CONV), hp.dtype)
        z_gla = jnp.zeros((BATCH, GLA_HEADS, GLA_DK, GLA_DV), hp.dtype)
        z_ffn = jnp.zeros((BATCH, CONV_WIDTH - 1, D_FF), hp.dtype)
        hp, c1, s1, f1 = layer(hp, z_conv, z_gla, z_ffn, mk, mv, *shared)
        conv_p.append(c1); gla_p.append(s1); ffn_p.append(f1); mk_p.append(mk); mv_p.append(mv)
        hs, c2, s2, f2 = layer(hs, cache_conv[l], state_gla[l], cache_ffn[l], cache_mem_k[l],
                               cache_mem_v[l], *shared)
        conv_s.append(c2); gla_s.append(s2); ffn_s.append(f2)
    y_prompt = rmsnorm(hp, norm_final)
    y_sample = rmsnorm(hs, norm_final)
    return (y_prompt, y_sample, jnp.stack(conv_p), jnp.stack(gla_p), jnp.stack(ffn_p),
            jnp.stack(mk_p), jnp.stack(mv_p), jnp.stack(conv_s), jnp.stack(gla_s),
            jnp.stack(ffn_s))
```

```python
import os
import numpy as np
from contextlib import ExitStack
import concourse.bass as bass
import concourse.mybir as mybir
from concourse.bass_utils import run_bass_kernel_spmd

F32 = mybir.dt.float32
BF16 = mybir.dt.bfloat16
ALU = mybir.AluOpType
AF = mybir.ActivationFunctionType

NT = 1152
NPRE = 960
D = 2048
KC = 16
NSEQ = 16
DFF = 5632
FC = 44
COL = dict(bg=0, cg=1024, vc=2048, q=3072, k=3584, v=4096, r=5120, g1=6144)
MT = [(0, 512), (512, 1024), (1024, 1152)]
MTP = [(0, 512), (512, 960)]
EPS = 1e-6


class Sched:
    def __init__(self, nc, es):
        self.nc = nc
        self.engs = {'pe': nc.tensor, 'act': nc.scalar, 'dve': nc.vector, 'pool': nc.gpsimd, 'sp': nc.sync}
        self.sem = {k: es.enter_context(nc.semaphore('sem_' + k)) for k in ['pe', 'act', 'dve', 'pool']}
        self.cnt = {k: 0 for k in self.sem}
        self.ndma = 48
        self.dsem = [es.enter_context(nc.semaphore('dsem%d' % i)) for i in range(self.ndma)]
        self.dval = [0] * self.ndma
        self.dnext = {'w': 0, 'p': 0, 's': 0}
        self.seen = {e: {} for e in self.engs}
        self.lastw = {}
        self.readers = {}

    def _semof(self, k):
        return self.sem[k] if isinstance(k, str) else self.dsem[k]

    def _wait(self, eng, semkey, val):
        if self.seen[eng].get(semkey, 0) >= val:
            return
        self.seen[eng][semkey] = val
        self.engs[eng].wait_ge(self._semof(semkey), val)

    def _deps(self, eng, reads, writes):
        best = {}

        def add(t):
            if t is not None and best.get(t[0], 0) < t[1]:
                best[t[0]] = t[1]
        for k in reads:
            add(self.lastw.get(k))
        for k in writes:
            add(self.lastw.get(k))
            for s, v in self.readers.get(k, {}).items():
                add((s, v))
        for s, v in best.items():
            if s == 'pe' and eng == 'pe':
                continue
            self._wait(eng, s, v)

    def _commit(self, tok, reads, writes):
        for k in reads:
            r = self.readers.setdefault(k, {})
            if r.get(tok[0], 0) < tok[1]:
                r[tok[0]] = tok[1]
        for k in writes:
            self.lastw[k] = tok
            self.readers[k] = {}

    def op(self, eng, fn, reads=(), writes=()):
        if eng != 'pe':
            pr = [k for k in reads if k[0] == 'ps']
            if pr:
                writes = list(writes) + pr
        self._deps(eng, reads, writes)
        inst = fn()
        self.cnt[eng] += 1
        inst.then_inc(self.sem[eng], 1)
        tok = (eng, self.cnt[eng])
        self._commit(tok, reads, writes)
        return tok

    def dma(self, eng, out, in_, reads=(), writes=(), pool='g'):
        if pool != 'w':
            pool = 'p' if eng == 'pool' else 's'
        lo, hi = {'w': (0, 8), 'p': (8, 24), 's': (24, self.ndma)}[pool]
        i = lo + self.dnext[pool]
        self.dnext[pool] = (self.dnext[pool] + 1) % (hi - lo)
        if self.dval[i] > 0:
            self._wait(eng, i, self.dval[i])
        self._deps(eng, reads, writes)
        inst = self.engs[eng].dma_start(out=out, in_=in_)
        self.dval[i] += 16
        inst.then_inc(self.dsem[i], 16)
        tok = (i, self.dval[i])
        self._commit(tok, reads, writes)
        return tok

    def fence(self):
        for e in self.engs:
            for o in self.sem:
                if o != e and self.cnt[o] > 0:
                    self._wait(e, o, self.cnt[o])
            for i in range(8, self.ndma):
                if self.dval[i] > 0:
                    self._wait(e, i, self.dval[i])

    def finish(self):
        for i in range(self.ndma):
            if self.dval[i] > 0:
                self._wait('sp', i, self.dval[i])
        for o in self.sem:
            if self.cnt[o] > 0:
                self._wait('sp', o, self.cnt[o])


def build(stop=99, dbg=None):
    nc = bass.Bass("TRN2", target_bir_lowering=False)
    es = ExitStack()
    S = Sched(nc, es)
    pe, act, dve, pool, sp = nc.tensor, nc.scalar, nc.vector, nc.gpsimd, nc.sync

    def din(name, shape):
        return nc.dram_tensor(name, shape, F32, kind="ExternalInput").ap()

    def dout(name, shape):
        return nc.dram_tensor(name, shape, F32, kind="ExternalOutput").ap()

    x_all = din("x_all", [NT, D])
    x_pre = din("x_pre", [1024, D])
    mem = din("mem", [256, D])
    ovm_d = din("ovm", [128, 1])
    cconv_d = din("cache_conv", [32, 1024])
    sgla_d = din("state_gla", [NSEQ, 4, 128, 256])
    cffn_d = din("cache_ffn", [32, DFF])
    cmk_d = din("cache_mem_k", [NSEQ, 256, D])
    cmv_d = din("cache_mem_v", [NSEQ, 256, D])
    w_in = din("w_in", [D, 6160])
    w_out = din("w_out", [D, D])
    w_xq = din("w_xq", [D, D])
    w_xk = din("w_xk", [D, D])
    w_xv = din("w_xv", [D, D])
    w_xo = din("w_xo", [D, D])
    w_fg = din("w_ffn_gate", [D, DFF])
    w_fu = din("w_ffn_up", [D, DFF])
    w_fd = din("w_ffn_down", [DFF, D])
    gains_d = din("gains", [5, D])
    conv_w_d = din("conv_w", [3, 1024])
    wg2_d = din("w_gate2", [16, 512])
    bgate_d = din("b_gate", [1, 512])
    glan_d = din("gla_norm", [1, 1024])
    fcw_d = din("ffn_conv_w", [3, DFF])
    fcb_d = din("ffn_conv_b", [1, DFF])
    consts_d = din("consts", [128, 128 * 3 + 17])

    y_all = dout("y_all", [NT, D])
    o_conv_p = dout("o_conv_p", [2, 1024])
    o_gla_p = dout("o_gla_p", [4, 128, 256])
    o_ffn_p = dout("o_ffn_p", [2, DFF])
    o_mk = dout("o_mk", [256, D])
    o_mv = dout("o_mv", [256, D])
    o_conv_s = dout("o_conv_s", [32, 1024])
    o_gla_s = dout("o_gla_s", [NSEQ, 4, 128, 256])
    o_ffn_s = dout("o_ffn_s", [32, DFF])
    dbg_d = dout("dbg", [128, 16 * NT]) if dbg else None

    def sb(name, shape, dt):
        return es.enter_context(nc.sbuf_tensor("s_" + name, shape, dt))

    identb = sb("identb", [128, 128], BF16)
    identf = sb("identf", [128, 128], F32)
    maskC = sb("maskC", [128, 128], BF16)
    mask0 = sb("mask0", [128, 128], BF16)
    selm = sb("selm", [128, 17], F32)
    onesb = sb("onesb", [128, 128], BF16)
    onesf = sb("onesf", [128, 128], F32)
    gains = sb("gains", [128, 5, 16], F32)
    convw = sb("convw", [128, 3, 8], F32)
    fcw = sb("fcw", [128, 3, FC], F32)
    fcb = sb("fcb", [128, FC], F32)
    negb = sb("negb", [128, 4], F32)
    glan = sb("glan", [128, 8], F32)
    wg2 = sb("wg2", [16, 512], BF16)
    ovm = sb("ovm", [128, 1], F32)
    g1T = sb("g1T", [16, NT], BF16)
    expL = sb("expL", [128, 4, 32], F32)
    stat = sb("stat", [128, 8], F32)
    Sst = sb("Sst", [128, 4, 256], F32)
    Sbb2 = sb("Sbb", [128, 2, 1024], BF16)
    Sbb = [Sbb2[:, i, :].rearrange("p (h v) -> p h v", h=4) for i in range(2)]
    sb_cur = [0]

    stgF = sb("stgF", [128, D], F32)
    ws01 = sb("ws01", [128, 2, 16 * 256], BF16)
    WS = [ws01[:, i, :].rearrange("p (k n) -> p k n", k=16) for i in range(2)]
    WW = [ws01[:].rearrange("p a x -> p (a x)").rearrange("p (k n) -> p k n", k=16)]
    WSF = [ws01[:, i, :] for i in range(2)]
    RA = sb("RA", [128, 16, NT], BF16)
    RB = sb("RB", [128, 16, NT], BF16)

    psA = es.enter_context(nc.psum_tensor("psA", [128, 1536], F32))
    psB = es.enter_context(nc.psum_tensor("psB", [128, 1536], F32))
    psC = es.enter_context(nc.psum_tensor("psC", [128, 512], F32))
    psD = es.enter_context(nc.psum_tensor("psD", [128, 512], F32))
    PSG = [psA, psB]

    def pkeys(g, c0, c1):
        return [('ps', g, b) for b in range(c0 // 512, (c1 - 1) // 512 + 1)]

    with nc.allow_non_contiguous_dma(reason="small constant loads"):
        S.dma('pool', identb[:], consts_d[:, 0:128], writes=[('c', 'identb')])
        S.dma('sp', identf[:], consts_d[:, 0:128], writes=[('c', 'identf')])
        S.dma('pool', maskC[:], consts_d[:, 128:256], writes=[('c', 'maskC')])
        S.dma('pool', mask0[:], consts_d[:, 256:384], writes=[('c', 'mask0')])
        S.dma('sp', selm[:], consts_d[:, 384:401], writes=[('c', 'selm')])
        S.dma('sp', gains[:], gains_d.rearrange("g (c p) -> p g c", p=128), writes=[('c', 'gains')])
        S.dma('sp', convw[:], conv_w_d.rearrange("k (c p) -> p k c", p=128), writes=[('c', 'convw')])
        S.dma('sp', fcw[:], fcw_d.rearrange("k (c p) -> p k c", p=128), writes=[('c', 'fcw')])
        S.dma('sp', fcb[:], fcb_d.rearrange("o (c p) -> p (o c)", p=128), writes=[('c', 'fcb')])
        S.dma('sp', negb[:], bgate_d.rearrange("o (c p) -> p (o c)", p=128), writes=[('c', 'negb')])
        S.dma('sp', glan[:], glan_d.rearrange("o (c p) -> p (o c)", p=128), writes=[('c', 'glan')])
        S.dma('pool', wg2[:], wg2_d, writes=[('c', 'wg2')])
        S.dma('sp', ovm[:], ovm_d, writes=[('c', 'ovm')])
    S.op('dve', lambda: dve.memset(onesb[:], 1.0), writes=[('c', 'onesb')])
    S.op('dve', lambda: dve.memset(onesf[:], 1.0), writes=[('c', 'onesf')])
    S.op('dve', lambda: dve.tensor_scalar(out=negb[:], in0=negb[:], scalar1=-1.0, scalar2=None, op0=ALU.mult),
         reads=[('c', 'negb')], writes=[('c', 'negb')])
    S.op('dve', lambda: dve.memset(Sst[:], 0.0), writes=[('Sst', h_) for h_ in range(4)])
    S.op('dve', lambda: dve.memset(Sbb2[:], 0.0), writes=[('Sbb', 0), ('Sbb', 1)])

    slabs = []
    slab_state = {'issued': 0, 'used': 0, 'p1_end': None, 'rr': 0}
    resident = {}

    slab_hold = [None]

    def _free(slot):
        r = resident.get(slot)
        if r is not None and slab_hold[0] is not None and r >= slab_hold[0]:
            return False
        return r is None or r <= slab_state['used'] - 2

    def _try_issue(i):
        w_ap, r0, nk, c0, cw = slabs[i]
        src = w_ap[r0:r0 + nk * 128, c0:c0 + cw].rearrange("(kc p) n -> p kc n", p=128)
        if nk * cw > 4096:
            for ws_ in range(len(WW)):
                if _free(2 * ws_) and _free(2 * ws_ + 1):
                    resident[2 * ws_] = i
                    resident[2 * ws_ + 1] = i
                    slab_state[('slot', i)] = ws_
                    S.dma('pool', WW[ws_][:, 0:nk, 0:cw], src, writes=[('ws', 2 * ws_), ('ws', 2 * ws_ + 1)], pool='w')
                    return True
            return False
        n = len(WS)
        for d in range(n):
            sl_ = (slab_state['rr'] + d) % n
            if _free(sl_):
                slab_state['rr'] = (sl_ + 1) % n
                resident[sl_] = i
                slab_state[('slot', i)] = sl_
                dst = WS[sl_][:, 0:nk, 0:cw] if cw <= 256 else WSF[sl_][:, 0:nk * cw].rearrange("p (k n) -> p k n", k=nk)
                S.dma('pool', dst, src, writes=[('ws', sl_)], pool='w')
                return True
        return False

    def slab_issue_upto(n):
        while slab_state['issued'] < min(n, len(slabs)):
            if not _try_issue(slab_state['issued']):
                return False
            slab_state['issued'] += 1
        return True

    def next_slab():
        i = slab_state['used']
        slab_state['used'] += 1
        ok = slab_issue_upto(i + 1)
        assert ok and slab_state['issued'] >= i + 1, "no free weight slot"
        prefetch()
        return slab_state[('slot', i)]

    def prefetch():
        slab_issue_upto(slab_state['used'] + 3)

    def add_fm(w_ap, c0, ncols):
        c = c0
        while c < c0 + ncols:
            cw = min(256, c0 + ncols - c)
            slabs.append((w_ap, 0, KC, c, cw))
            c += cw

    add_fm(w_in, COL['g1'], 16)
    add_fm(w_in, COL['k'], 512)
    add_fm(w_in, COL['v'], 1024)
    add_fm(w_in, COL['g1'], 16)
    add_fm(w_in, COL['r'], 1024)
    add_fm(w_in, COL['k'], 512)
    add_fm(w_in, COL['q'], 512)
    add_fm(w_in, COL['v'], 1024)
    for j in range(4):
        for nm in ('cg', 'vc', 'bg'):
            add_fm(w_in, COL[nm] + 256 * j, 256)
    slab_state['p1_end'] = len(slabs)
    for c4 in range(4):
        slabs.append((w_out, 0, KC, c4 * 512, 512))
    add_fm(w_xq, 0, D)
    for w_ in (w_xk, w_xv):
        for c4 in range(4):
            slabs.append((w_, 0, KC, c4 * 512, 512))
    for c4 in range(4):
        slabs.append((w_xo, 0, KC, c4 * 512, 512))
    FGROUPS = [(0, 8), (8, 16), (16, 24), (24, 32), (32, 40), (40, 44)]
    for (f0, f1) in FGROUPS:
        for c in range(f0, f1):
            slabs.append((w_fg, 0, KC, c * 128, 128))
            slabs.append((w_fu, 0, KC, c * 128, 128))
        for c4 in range(4):
            slabs.append((w_fd, f0 * 128, f1 - f0, c4 * 512, 512))

    def mm_fm(slot, j, xT, mts, psg, ncol=128, xkeys=None):
        for (a, b) in mts:
            def fn(a=a, b=b):
                last = None
                for kc in range(KC):
                    last = pe.matmul(PSG[psg][0:ncol, a:b], lhsT=WS[slot][:, kc, j * 128:j * 128 + ncol],
                                     rhs=xT[:, kc, a:b], start=(kc == 0), stop=(kc == KC - 1))
                return last
            xk_ = [k for k in xkeys if a <= k[1] * 128 < b]
            S.op('pe', fn, reads=[('ws', slot)] + xk_, writes=pkeys(psg, a, b))

    xs_bufs = [RB[:, 8 + 2 * i:10 + 2 * i, :].rearrange("p a b -> p (a b)")[:, 0:2048] for i in range(4)]
    norm_ctr = [0]
    norm_alt = [True]

    def norm_T(src, srckeys, nrows, gi, dstT, col0, dstkey, xsb=None, defer=False):
        i = norm_ctr[0]
        norm_ctr[0] += 1
        s = i % len(xs_bufs)
        xs = xs_bufs[s] if xsb is None else xsb
        xkey = ('xs', s) if xsb is None else ('xs', 'm')
        ss = stat[:, 2 * s:2 * s + 1]
        rs = stat[:, 2 * s + 1:2 * s + 2]
        S.op('act', lambda: act.activation(out=xs[0:nrows, :], in_=src, func=AF.Square, accum_out=ss[0:nrows, :]),
             reads=srckeys, writes=[xkey, ('st', s)])
        S.op('act', lambda: act.activation(out=ss[0:nrows, :], in_=ss[0:nrows, :], func=AF.Sqrt, scale=1.0 / D, bias=EPS),
             reads=[('st', s)], writes=[('st', s)])
        S.op('dve', lambda: dve.reciprocal(out=rs[0:nrows, :], in_=ss[0:nrows, :]),
             reads=[('st', s)], writes=[('rs', s)])
        S.op('pool', lambda: pool.tensor_scalar(out=xs[0:nrows, :], in0=src, scalar1=rs[0:nrows, :], scalar2=0.0,
                                                op0=ALU.mult, op1=ALU.add),
             reads=srckeys + [('rs', s)], writes=[xkey])
        def part2():
            for half in range(2):
                if norm_alt[0]:
                    bsel = [[(psA[:, 0:512], ('ps', 0, 0)), (psA[:, 512:1024], ('ps', 0, 1))],
                            [(psA[:, 1024:1536], ('ps', 0, 2)), (psC[:, :], ('ps', 'C', 0))]][i % 2][half]
                    pst_ap, pk0 = bsel
                    pk = [pk0]
                    pv = pst_ap.bitcast(BF16)
                else:
                    pst = psC if half == 0 else psD
                    pk = [('ps', 'C' if half == 0 else 'D', 0)]
                    pv = pst[:].bitcast(BF16)

                def tr(half=half, pv=pv):
                    last = None
                    for c in range(8):
                        cc = half * 8 + c
                        last = pe.transpose(out=pv[:, c * 128:c * 128 + nrows],
                                            in_=xs[0:nrows, cc * 128:(cc + 1) * 128], identity=identb[0:nrows, 0:nrows])
                    return last
                S.op('pe', tr, reads=[xkey, ('c', 'identb')], writes=pk)
                gb = gains[:, gi, half * 8:half * 8 + 8].unsqueeze(2).broadcast_to([128, 8, nrows])
                src_ps = pv.rearrange("p (c t) -> p c t", c=8)[:, :, 0:nrows]
                S.op('dve', lambda half=half, gb=gb, src_ps=src_ps: dve.tensor_tensor(
                    out=dstT[:, half * 8:half * 8 + 8, col0:col0 + nrows], in0=src_ps, in1=gb, op=ALU.mult),
                    reads=pk + [('c', 'gains')], writes=[dstkey])
        if defer:
            return part2
        part2()

    es_p1 = ExitStack()
    xstage_t = es_p1.enter_context(nc.sbuf_tensor("xstage", [128, 2, D], F32))
    xstage = [xstage_t[:, 0, :], xstage_t[:, 1, :]]
    qt = es_p1.enter_context(nc.sbuf_tensor("qt", [128, 4, NT], BF16))
    kt = es_p1.enter_context(nc.sbuf_tensor("kt", [128, 4, NT], BF16))
    cums = es_p1.enter_context(nc.sbuf_tensor("cums", [128, 4, NT], F32))
    vt = cums[:].rearrange("p h t -> p (h t)").bitcast(BF16).rearrange("p (t v) -> p t v", t=9)
    scr = es_p1.enter_context(nc.sbuf_tensor("scr", [128, 3, 1152], F32))
    sbs_t = es_p1.enter_context(nc.sbuf_tensor("sbs_t", [128, 4, 1024], BF16))
    khm_t = es_p1.enter_context(nc.sbuf_tensor("khm_t", [128, 2, 512], BF16))
    cvt = es_p1.enter_context(nc.sbuf_tensor("cvt", [128, 4, NT], F32))
    CUMK = [('cums', h) for h in range(4)]
    cvf = cvt[:].rearrange("p a b -> p (a b)")
    xstage = xstage + [cvf[:, 0:D], cvf[:, D:2 * D]]

    def load_norm(x_ap, ntok, dstT, col0key):
        nt = (ntok + 127) // 128
        for t in range(nt):
            nr = min(128, ntok - t * 128)
            s = t % 4
            S.dma('sp', xstage[s][0:nr, :], x_ap[t * 128:t * 128 + nr, :], writes=[('xstage', s)])
            norm_T(xstage[s][0:nr, :], [('xstage', s)], nr, 0, dstT, t * 128, (col0key, t))

    def gates(slot, xT, mts, ntok, xk, chunks):
        mm_fm(slot, 0, xT, mts, 0, ncol=16, xkeys=xk)
        S.op('act', lambda: act.copy(out=g1T[:, 0:ntok], in_=psA[0:16, 0:ntok]),
             reads=pkeys(0, 0, ntok), writes=[('g1T',)])
        for h in range(4):
            g = h % 2

            def zf(h=h, g=g):
                last = None
                for (a, b) in mts:
                    last = pe.matmul(PSG[g][:, a:b], lhsT=wg2[:, h * 128:(h + 1) * 128], rhs=g1T[:, a:b],
                                     start=True, stop=True)
                return last
            S.op('pe', zf, reads=[('g1T',), ('c', 'wg2')], writes=pkeys(g, 0, ntok))
            S.op('act', lambda h=h, g=g: act.activation(out=scr[:, 0, 0:ntok], in_=PSG[g][:, 0:ntok], func=AF.Exp,
                                                        scale=-1.0, bias=negb[:, h:h + 1]),
                 reads=pkeys(g, 0, ntok) + [('c', 'negb')], writes=[('scr', 0)])
            S.op('act', lambda h=h: act.activation(out=scr[:, 1, 0:ntok], in_=scr[:, 0, 0:ntok], func=AF.Ln,
                                                   scale=1.0, bias=1.0),
                 reads=[('scr', 0)], writes=[('scr', 1)])
            for (a, b) in chunks:
                S.op('dve', lambda h=h, a=a, b=b: dve.tensor_tensor_scan(
                    out=cums[:, h, a:b], data0=onesf[:, 0:b - a], data1=scr[:, 1, a:b], initial=0.0,
                    op0=ALU.mult, op1=ALU.add),
                    reads=[('scr', 1), ('c', 'onesf')], writes=[('cums', h)])

    def expL_cols(c_first, stride, n, e0):
        src = cums[:, :, c_first:c_first + stride * (n - 1) + 1:stride]
        S.op('act', lambda: act.activation(out=expL[:, :, e0:e0 + n], in_=src, func=AF.Exp, scale=-1.0 / 16),
             reads=CUMK, writes=[('expL',)])

    def qk_tilde(dst, dkey, sign, scale, ntok, mts, xT, xk):
        for h in range(4):
            if h % 2 == 0:
                sl = next_slab()
            g = h % 2
            mm_fm(sl, h % 2, xT, mts, g, xkeys=xk)
            if h % 2 == 1:
                prefetch()
            S.op('act', lambda h=h: act.activation(out=scr[:, 0, 0:ntok], in_=cums[:, h, 0:ntok], func=AF.Exp,
                                                   scale=sign / 16.0),
                 reads=[('cums', h)], writes=[('scr', 0)])
            S.op('dve', lambda h=h, g=g: dve.scalar_tensor_tensor(
                out=dst[:, h, 0:ntok], in0=PSG[g][:, 0:ntok], scalar=scale, in1=scr[:, 0, 0:ntok],
                op0=ALU.mult, op1=ALU.mult),
                reads=pkeys(g, 0, ntok) + [('scr', 0)], writes=[(dkey, h)])

    def v_tok(ntok, xT, xk):
        nt = (ntok + 127) // 128
        for q4 in range(4):
            sl = next_slab()
            for t in range(nt):
                nr = min(128, ntok - t * 128)
                g = t % 2
                half = (t // 2) % 2

                def fn(t=t, nr=nr, g=g, half=half, sl=sl):
                    last = None
                    for kc in range(KC):
                        last = pe.matmul(PSG[g][0:nr, half * 512:half * 512 + 256],
                                         lhsT=xT[:, kc, t * 128:t * 128 + nr], rhs=WS[sl][:, kc, 0:256],
                                         start=(kc == 0), stop=(kc == KC - 1))
                    return last
                S.op('pe', fn, reads=[('ws', sl), xk[t]], writes=[('ps', g, half)])
                S.op('act', lambda t=t, nr=nr, g=g, half=half, q4=q4: act.copy(
                    out=vt[0:nr, t, q4 * 256:(q4 + 1) * 256], in_=PSG[g][0:nr, half * 512:half * 512 + 256]),
                    reads=[('ps', g, half)], writes=[('vt', t)] + CUMK)
            prefetch()

    kh = scr[:, 2, :].bitcast(BF16)
    khT = kh[:, 0:512].rearrange("p (h t) -> p h t", h=4)
    khat = kh[:, 512:1024].rearrange("p (h d) -> p h d", h=4)
    Am = kh[:, 1024:1536].rearrange("p (h t) -> p h t", h=4)
    pvD = psD[:].bitcast(BF16)

    def khat_T(n):
        def tr():
            last = None
            for h in range(4):
                last = pe.transpose(out=pvD[0:n, h * 128:(h + 1) * 128], in_=khT[:, h, 0:n], identity=identb[:, :])
            return last
        S.op('pe', tr, reads=[('khT',), ('c', 'identb')], writes=[('ps', 'D', 0)])
        S.op('act', lambda: act.copy(out=khat[0:n].rearrange("p h d -> p (h d)"), in_=pvD[0:n, 0:512]),
             reads=[('ps', 'D', 0)], writes=[('khat',)])

    def state_khat(c0, n, eidx):
        eb = expL[:, :, eidx:eidx + 1].broadcast_to([128, 4, n])
        S.op('dve', lambda: dve.tensor_tensor(out=khT[:, :, 0:n], in0=kt[:, :, c0:c0 + n], in1=eb, op=ALU.mult),
             reads=[('kt', h) for h in range(4)] + [('expL',)], writes=[('khT',)])
        khat_T(n)

    def state_update(lhs, lkey, n, vti, eidx, Sf, skey, pst, pstk):
        if not isinstance(pst, list):
            pst = [pst[:, h * 256:(h + 1) * 256] for h in range(4)]

        def ds():
            last = None
            for h in range(4):
                last = pe.matmul(pst[h], lhsT=lhs[0:n, h, :],
                                 rhs=vt[0:n, vti, h * 256:(h + 1) * 256], start=True, stop=True)
            return last
        S.op('pe', ds, reads=[lkey, ('vt', vti)], writes=pstk)
        for h in range(4):
            hk_ = skey + (h,)
            S.op('dve', lambda h=h: dve.scalar_tensor_tensor(
                out=Sf[:, h, :], in0=Sf[:, h, :], scalar=expL[:, h, eidx:eidx + 1], in1=pst[h],
                op0=ALU.mult, op1=ALU.add),
                reads=[hk_, ('expL',)] + pstk, writes=[hk_])

    PSB01 = [('ps', 1, 0), ('ps', 1, 1)]
    PSA01 = [('ps', 0, 0), ('ps', 0, 1)]

    PRE_CH = [(i * 128, min(NPRE, i * 128 + 128)) for i in range(8)]
    load_norm(x_pre, NPRE, RA, 'xT')
    xk_pre = [('xT', t) for t in range(8)]
    gates(next_slab(), RA, MTP, NPRE, xk_pre, PRE_CH)
    prefetch()
    qk_tilde(kt, 'kt', +1.0, 1.0, NPRE, MTP, RA, xk_pre)
    expL_cols(127, 128, 7, 0)
    expL_cols(NPRE - 1, 1, 1, 7)
    v_tok(NPRE, RA, xk_pre)
    def load_norm_tile(t):
        s_ = t % 4
        S.dma('sp', xstage[s_][:, :], x_all[t * 128:(t + 1) * 128, :], writes=[('xstage', s_)])
        norm_T(xstage[s_][:, :], [('xstage', s_)], 128, 0, RA, t * 128, ('xT', t))

    for ci, (a, b) in enumerate(PRE_CH):
        state_khat(a, b - a, ci)
        state_update(khat, ('khat',), b - a, ci, ci, Sst, ('Sst',), psB, PSB01)
        if stop >= 1:
            load_norm_tile(ci)
    S.op('act', lambda: act.copy(out=Sbb[0], in_=Sst[:]), reads=[('Sst', h_) for h_ in range(4)], writes=[('Sbb', 0)])
    if dbg == 'pre':
        S.dma('sp', dbg_d[:, 0:1024], Sst[:].rearrange("p h v -> p (h v)"), reads=[('Sst', h_) for h_ in range(4)])

    if stop >= 1:
        load_norm_tile(8)
        S.fence()
        xk = [('xT', t) for t in range(9)]
        CH = [(4 * s_, 4 * s_ + 4) for s_ in range(16)] + [(64, 128)] + [(128 * i, 128 * i + 128) for i in range(1, 9)]
        gates(next_slab(), RA, MT, NT, xk, CH)
        prefetch()
        for j2 in range(4):
            sl = next_slab()
            for j in range(2):
                g = j
                mm_fm(sl, j, RA, MT, g, xkeys=xk)
                if j == 1:
                    prefetch()
                S.op('act', lambda g=g: act.activation(out=scr[:, g, :], in_=PSG[g][:, 0:NT], func=AF.Silu),
                     reads=pkeys(g, 0, NT), writes=[('scr', g)])
                S.op('dve', lambda g=g, jj=j2 * 2 + j: dve.tensor_scalar(out=RB[:, 8 + jj, :], in0=scr[:, g, :],
                                                                         scalar1=glan[:, jj:jj + 1], scalar2=None, op0=ALU.mult),
                     reads=[('scr', g), ('c', 'glan')], writes=[('aT', 8 + j2 * 2 + j)])
        qk_tilde(kt, 'kt', +1.0, 1.0, NT, MT, RA, xk)
        qk_tilde(qt, 'qt', -1.0, 128 ** -0.5, NT, MT, RA, xk)
        expL_cols(255, 128, 8, 0)
        expL_cols(127, 1, 1, 8)
        expL_cols(3, 4, 16, 9)
        v_tok(NT, RA, xk)
        S.fence()

        xsf = xstage_t[:].rearrange("p a b -> p (a b)")
        s0b = [xsf[:, i * 1024:(i + 1) * 1024].rearrange("p (h v) -> p h v", h=4) for i in range(4)]
        sbs = [sbs_t[:, i, :].rearrange("p (h v) -> p h v", h=4) for i in range(4)]
        khm = [khm_t[:, i, :].rearrange("p (h d) -> p h d", h=4) for i in range(2)]
        osb = scr[:, 0, 0:1024]
        sqb = scr[:, 1, :].bitcast(BF16)[:, 0:1024]
        rstd = scr[:, 1, 512:1024]

        def gla_post(c0, n, extra=None):
            o3 = osb.rearrange("p (b t) -> p b t", b=8)
            o4 = osb.rearrange("p (h c t) -> p h c t", h=4, c=2)
            if extra is not None:
                S.op('act', lambda: act.copy(out=osb, in_=psB[:, 0:1024]), reads=PSB01, writes=[('scr', 0)])
                S.op('dve', lambda: dve.tensor_tensor(out=osb, in0=extra, in1=osb, op=ALU.add),
                     reads=[('scr', 0)] + PSA01, writes=[('scr', 0)])
                S.op('act', lambda: act.activation(out=sqb, in_=osb, func=AF.Square), reads=[('scr', 0)],
                     writes=[('sqb',)])
            else:
                S.op('act', lambda: act.activation(out=sqb, in_=psB[:, 0:1024], func=AF.Square), reads=PSB01,
                     writes=[('sqb',)])

            def msf():
                last = None
                for h in range(4):
                    for vc in range(2):
                        last = pe.matmul(psC[:, h * 128:(h + 1) * 128], lhsT=onesb[:, :],
                                         rhs=sqb[:, (h * 2 + vc) * 128:(h * 2 + vc + 1) * 128],
                                         start=(vc == 0), stop=(vc == 1))
                return last
            S.op('pe', msf, reads=[('sqb',), ('c', 'onesb')], writes=[('ps', 'C', 0)])
            S.op('act', lambda: act.activation(out=rstd, in_=psC[:, :], func=AF.Sqrt, scale=1.0 / 256, bias=EPS),
                 reads=[('ps', 'C', 0)], writes=[('rstd',)])
            S.op('dve', lambda: dve.reciprocal(out=rstd, in_=rstd), reads=[('rstd',)], writes=[('rstd',)])
            r4 = rstd.rearrange("p (h t) -> p h t", h=4).unsqueeze(2).broadcast_to([128, 4, 2, 128])
            if extra is not None:
                S.op('dve', lambda: dve.tensor_tensor(out=o4, in0=o4, in1=r4, op=ALU.mult),
                     reads=[('scr', 0), ('rstd',)], writes=[('scr', 0)])
            else:
                p4 = psB[:, 0:1024].rearrange("p (h c t) -> p h c t", h=4, c=2)
                S.op('dve', lambda: dve.tensor_tensor(out=o4, in0=p4, in1=r4, op=ALU.mult),
                     reads=PSB01 + [('rstd',)], writes=[('scr', 0)])
            S.op('dve', lambda: dve.tensor_tensor(out=RB[:, 8:16, c0:c0 + n], in0=o3[:, :, 0:n], in1=RB[:, 8:16, c0:c0 + n],
                                                  op=ALU.mult),
                 reads=[('scr', 0)] + [('aT', 8 + j) for j in range(8)], writes=[('aT', 8 + j) for j in range(8)])

        def gla_AT(c0, mask, mkey):
            def af():
                last = None
                for h in range(4):
                    last = pe.matmul(psC[:, h * 128:(h + 1) * 128], lhsT=kt[:, h, c0:c0 + 128], rhs=qt[:, h, c0:c0 + 128],
                                     start=True, stop=True)
                return last
            S.op('pe', af, reads=[('kt', h) for h in range(4)] + [('qt', h) for h in range(4)], writes=[('ps', 'C', 0)])
            mb = mask[:, :].unsqueeze(1).broadcast_to([128, 4, 128])
            S.op('dve', lambda: dve.tensor_tensor(out=Am, in0=psC[:, :].rearrange("p (h t) -> p h t", h=4), in1=mb, op=ALU.mult),
                 reads=[('ps', 'C', 0), ('c', mkey)], writes=[('Am',)])

        gla_AT(0, mask0, 'mask0')
        for h in range(4):
            e3 = expL[:, h, 9:25].unsqueeze(2).broadcast_to([128, 16, 4])
            S.op('dve', lambda h=h, e3=e3: dve.tensor_tensor(out=khT[:, h, 0:64].rearrange("p (s t) -> p s t", s=16),
                                                             in0=kt[:, h, 0:64].rearrange("p (s t) -> p s t", s=16),
                                                             in1=e3, op=ALU.mult),
                 reads=[('kt', h), ('expL',)], writes=[('khT',)])
            S.op('dve', lambda h=h: dve.tensor_scalar(out=khT[:, h, 64:128], in0=kt[:, h, 64:128],
                                                      scalar1=expL[:, h, 8:9], scalar2=None, op0=ALU.mult),
                 reads=[('kt', h), ('expL',)], writes=[('khT',)])
        khat_T(128)

        T0D = [psA[:, 1024:1280], psA[:, 1280:1536], psB[:, 1024:1280], psB[:, 1280:1536]]
        T0DK = [('ps', 0, 2), ('ps', 1, 2)]

        def intra0():
            last = None
            for b_ in range(8):
                last = pe.matmul(psB[:, b_ * 128:(b_ + 1) * 128], lhsT=vt[:, 0, b_ * 128:(b_ + 1) * 128], rhs=Am[:, b_ // 2, :],
                                 start=True, stop=True)
            for b_ in range(8):
                last = pe.matmul(psA[:, b_ * 128 + 64:(b_ + 1) * 128], lhsT=Sbb[0][:, b_ // 2, (b_ % 2) * 128:(b_ % 2 + 1) * 128],
                                 rhs=qt[:, b_ // 2, 64:128], start=True, stop=True)
            return last
        S.op('pe', intra0, reads=[('vt', 0), ('Am',), ('Sbb', 0)] + [('qt', h) for h in range(4)], writes=PSB01 + PSA01)
        def s0_load(q_):
            S.dma('sp', s0b[q_ % 4], sgla_d[q_].rearrange("h d v -> d h v"), writes=[('s0b', q_ % 4, h) for h in range(4)])
        for q_ in range(3):
            s0_load(q_)
        for s_ in range(NSEQ):
            sl = s_ % 4
            k2 = s_ % 2
            S.op('act', lambda sl=sl: act.copy(out=sbs[sl], in_=s0b[sl]), reads=[('s0b', sl, h) for h in range(4)], writes=[('sbs', sl)])

            def inter(s_=s_, sl=sl):
                last = None
                for b_ in range(8):
                    last = pe.matmul(psA[:, b_ * 128 + 4 * s_:b_ * 128 + 4 * s_ + 4],
                                     lhsT=sbs[sl][:, b_ // 2, (b_ % 2) * 128:(b_ % 2 + 1) * 128],
                                     rhs=qt[:, b_ // 2, 4 * s_:4 * s_ + 4], start=True, stop=True)
                return last
            S.op('pe', inter, reads=[('sbs', sl)] + [('qt', h) for h in range(4)], writes=PSA01)
            S.op('dve', lambda s_=s_, k2=k2: dve.tensor_scalar(
                out=khm[k2].rearrange("p h d -> p (h d)"), in0=khat.rearrange("p h d -> p (h d)"),
                scalar1=selm[:, s_:s_ + 1], scalar2=None, op0=ALU.mult),
                reads=[('khat',), ('c', 'selm')], writes=[('khm', k2)])
            state_update(khm[k2], ('khm', k2), 128, 0, 9 + s_, s0b[sl], ('s0b', sl), T0D, T0DK)
            S.dma('sp', o_gla_s[s_].rearrange("h d v -> d h v"), s0b[sl], reads=[('s0b', sl, h) for h in range(4)])
            if s_ + 3 < NSEQ:
                s0_load(s_ + 3)
        S.op('dve', lambda: dve.tensor_scalar(
            out=khm[0].rearrange("p h d -> p (h d)"), in0=khat.rearrange("p h d -> p (h d)"),
            scalar1=selm[:, 16:17], scalar2=None, op0=ALU.mult),
            reads=[('khat',), ('c', 'selm')], writes=[('khm', 0)])
        gla_post(0, 128, extra=psA[:, 0:1024])
        state_update(khm[0], ('khm', 0), 128, 0, 8, Sst, ('Sst',), T0D, T0DK)
        S.op('act', lambda: act.copy(out=Sbb[1], in_=Sst[:]), reads=[('Sst', h_) for h_ in range(4)], writes=[('Sbb', 1)])
        sb_cur[0] = 1

        cgs = [cvt[:, 0, :], cvt[:, 1, :]]
        us = [cvt[:, 2, :], cvt[:, 3, :]]
        ycs = cgs
        rowbuf = stgF[0:34, 0:1024]
        pcc = stgF[:, 1024:1280].rearrange("p (j r) -> p j r", j=8)
        utl = stgF[:, 1280:1280 + 8 * 34].rearrange("p (j r) -> p j r", j=8)
        S.dma('sp', rowbuf[0:32, :], cconv_d, writes=[('rowbuf',)])

        def trc():
            last = None
            for j in range(8):
                last = pe.transpose(out=psC[:, j * 32:(j + 1) * 32], in_=rowbuf[0:32, j * 128:(j + 1) * 128],
                                    identity=identf[0:32, 0:32])
            return last
        S.op('pe', trc, reads=[('rowbuf',), ('c', 'identf')], writes=[('ps', 'C', 0)])
        S.op('act', lambda: act.copy(out=pcc.rearrange("p j r -> p (j r)"), in_=psC[:, 0:256]),
             reads=[('ps', 'C', 0)], writes=[('pcc',)])

        def per_bank(eng, mk, reads, writes):
            for bi, (a, b) in enumerate(MT):
                S.op(eng, lambda a=a, b=b: mk(a, b), reads=[('ps', 0, bi)] + reads, writes=writes)

        def conv_gen():
            for j2 in range(4):
                sl = next_slab()
                for j in range(2):
                    mm_fm(sl, j, RA, MT, 0, xkeys=xk)
                    per_bank('act', lambda a, b, j=j: act.copy(out=cgs[j][:, a:b], in_=psA[:, a:b]), [], [('cgs', j)])
                    if j == 1:
                        prefetch()
                    yield
                sl = next_slab()
                for j in range(2):
                    jj = j2 * 2 + j
                    mm_fm(sl, j, RA, MT, 0, xkeys=xk)
                    per_bank('dve', lambda a, b, j=j: dve.tensor_tensor(out=us[j][:, a:b], in0=psA[:, a:b], in1=cgs[j][:, a:b], op=ALU.mult),
                             [('cgs', j)], [('us', j)])
                    if j == 1:
                        prefetch()
                    S.op('act', lambda j=j, jj=jj: act.copy(out=utl[:, jj, 0:32].rearrange("p (s r) -> p s r", s=16),
                                                            in_=us[j][:, 0:64].rearrange("p (s t) -> p s t", s=16)[:, :, 2:4]),
                         reads=[('us', j)], writes=[('utl',)])
                    S.op('act', lambda j=j, jj=jj: act.copy(out=utl[:, jj, 32:34], in_=us[j][:, NT - 2:NT]),
                         reads=[('us', j)], writes=[('utl',)])
                    yc = ycs[j]
                    S.op('pool', lambda j=j, jj=jj, yc=yc: pool.tensor_scalar(out=yc, in0=us[j], scalar1=convw[:, 2, jj:jj + 1],
                                                                              scalar2=0.0, op0=ALU.mult, op1=ALU.add),
                         reads=[('us', j), ('c', 'convw')], writes=[('cgs', j)])
                    y3 = yc[:, 0:64].rearrange("p (s t) -> p s t", s=16)
                    u3 = us[j][:, 0:64].rearrange("p (s t) -> p s t", s=16)
                    p3 = pcc[:, jj, :].rearrange("p (s r) -> p s r", s=16)
                    for (tap, sh) in ((1, 1), (0, 2)):
                        S.op('dve', lambda j=j, jj=jj, yc=yc, tap=tap, sh=sh: dve.scalar_tensor_tensor(
                            out=yc[:, 64 + sh:NT], in0=us[j][:, 64:NT - sh], scalar=convw[:, tap, jj:jj + 1], in1=yc[:, 64 + sh:NT],
                            op0=ALU.mult, op1=ALU.add),
                            reads=[('us', j), ('c', 'convw')], writes=[('cgs', j)])
                        S.op('dve', lambda j=j, jj=jj, y3=y3, u3=u3, tap=tap, sh=sh: dve.scalar_tensor_tensor(
                            out=y3[:, :, sh:4], in0=u3[:, :, 0:4 - sh], scalar=convw[:, tap, jj:jj + 1], in1=y3[:, :, sh:4],
                            op0=ALU.mult, op1=ALU.add),
                            reads=[('us', j), ('c', 'convw')], writes=[('cgs', j)])
                    S.op('dve', lambda jj=jj, y3=y3, p3=p3: dve.scalar_tensor_tensor(
                        out=y3[:, :, 0:1], in0=p3[:, :, 1:2], scalar=convw[:, 1, jj:jj + 1], in1=y3[:, :, 0:1],
                        op0=ALU.mult, op1=ALU.add),
                        reads=[('pcc',), ('c', 'convw')], writes=[('cgs', j)])
                    S.op('dve', lambda jj=jj, y3=y3, p3=p3: dve.scalar_tensor_tensor(
                        out=y3[:, :, 0:2], in0=p3[:, :, 0:2], scalar=convw[:, 0, jj:jj + 1], in1=y3[:, :, 0:2],
                        op0=ALU.mult, op1=ALU.add),
                        reads=[('pcc',), ('c', 'convw')], writes=[('cgs', j)])
                    yield
                sl = next_slab()
                for j in range(2):
                    jj = j2 * 2 + j
                    mm_fm(sl, j, RA, MT, 0, xkeys=xk)
                    per_bank('dve', lambda a, b, j=j, jj=jj: dve.tensor_tensor(out=RB[:, jj, a:b], in0=psA[:, a:b], in1=ycs[j][:, a:b], op=ALU.mult),
                             [('cgs', j)], [('aT', jj)])
                    if j == 1:
                        prefetch()
                    yield

        cg_it = conv_gen()

        def filler(n=1):
            for _ in range(n):
                try:
                    next(cg_it)
                except StopIteration:
                    return

        MCD = [psB[:, 1024:1280], psB[:, 1280:1536], psD[:, 0:256], psD[:, 256:512]]
        MCDK = [('ps', 1, 2), ('ps', 'D', 0)]
        for ci in range(1, 9):
            c0 = 128 * ci
            cur = sb_cur[0]
            gla_AT(c0, maskC, 'maskC')
            state_khat(c0, 128, ci - 1)
            filler()
            state_update(khat, ('khat',), 128, ci, ci - 1, Sst, ('Sst',), MCD, MCDK)
            S.op('act', lambda cur=cur: act.copy(out=Sbb[1 - cur], in_=Sst[:]), reads=[('Sst', h_) for h_ in range(4)], writes=[('Sbb', 1 - cur)])

            def of(ci=ci, c0=c0, cur=cur):
                last = None
                for b_ in range(8):
                    pe.matmul(psB[:, b_ * 128:(b_ + 1) * 128], lhsT=vt[:, ci, b_ * 128:(b_ + 1) * 128], rhs=Am[:, b_ // 2, :],
                              start=True, stop=False)
                    last = pe.matmul(psB[:, b_ * 128:(b_ + 1) * 128], lhsT=Sbb[cur][:, b_ // 2, (b_ % 2) * 128:(b_ % 2 + 1) * 128],
                                     rhs=qt[:, b_ // 2, c0:c0 + 128], start=False, stop=True)
                return last
            S.op('pe', of, reads=[('vt', ci), ('Am',), ('Sbb', cur)] + [('qt', h) for h in range(4)], writes=PSB01)
            filler()
            gla_post(c0, 128)
            filler()
            sb_cur[0] = 1 - cur
        S.dma('sp', o_gla_p.rearrange("h d v -> d h v"), Sst[:], reads=[('Sst', h_) for h_ in range(4)])
        filler(100)

        def tru():
            last = None
            for j in range(8):
                pt = psC if j < 4 else psD
                last = pe.transpose(out=pt[0:34, (j % 4) * 128:(j % 4 + 1) * 128], in_=utl[:, j, :], identity=identf[:, :])
            return last
        S.op('pe', tru, reads=[('utl',), ('c', 'identf')], writes=[('ps', 'C', 0), ('ps', 'D', 0)])
        S.op('act', lambda: act.copy(out=rowbuf[0:34, 0:512], in_=psC[0:34, 0:512]), reads=[('ps', 'C', 0)], writes=[('rowbuf',)])
        S.op('act', lambda: act.copy(out=rowbuf[0:34, 512:1024], in_=psD[0:34, 0:512]), reads=[('ps', 'D', 0)], writes=[('rowbuf',)])
        S.dma('sp', o_conv_s, rowbuf[0:32, :], reads=[('rowbuf',)])
        S.dma('sp', o_conv_p, rowbuf[32:34, :], reads=[('rowbuf',)])
        if dbg == 'p1':
            for c in range(16):
                S.dma('pool', dbg_d[:, c * NT:(c + 1) * NT], RB[:, c, :], reads=[('aT', c)])

    norm_alt[0] = False
    resident.clear()
    prefetch()
    S.fence()
    es_p1.close()
    RH = sb("RH", [128, 9, D], F32)
    ws23 = sb("ws23", [128, 2, 16 * 256], BF16)
    WS.append(ws23[:, 0, :].rearrange("p (k n) -> p k n", k=16))
    WS.append(ws23[:, 1, :].rearrange("p (k n) -> p k n", k=16))
    WW.append(ws23[:].rearrange("p a x -> p (a x)").rearrange("p (k n) -> p k n", k=16))
    WSF.append(ws23[:, 0, :])
    WSF.append(ws23[:, 1, :])
    xs2_t = sb("xs2", [128, 2, D], BF16)
    del xs_bufs[2:]
    xs_bufs[0] = xs2_t[:, 0, :]
    xs_bufs[1] = xs2_t[:, 1, :]
    xs_bufs.append(stgF[:, 1024:2048].bitcast(BF16))
    HK = [('h', t) for t in range(9)]
    AK = [('aT', c) for c in range(16)]
    xk = [('xT', t) for t in range(9)]
    tm_slots = [(0, 0), (1, 0), (0, 1), (1, 1), (0, 2), (1, 2)]
    tm_ctr = [0]

    norm_pending = [None]

    def norm_cb_flush():
        if norm_pending[0] is not None:
            norm_pending[0]()
            norm_pending[0] = None

    def make_norm_cb(gi):
        def cb(t):
            norm_cb_flush()
            norm_pending[0] = norm_T(RH[:, t, :], [('h', t)], 128, gi, RA, t * 128, ('xT', t), defer=True)
        return cb

    def _wview(sl, nk):
        if nk * 512 > 4096:
            return WW[sl], [('ws', 2 * sl), ('ws', 2 * sl + 1)]
        return WSF[sl][:, 0:nk * 512].rearrange("p (k n) -> p k n", k=nk), [('ws', sl)]

    def lin_tm_residual(nslab, nk, aT, akeys, defer_last=False, after_tile=None):
        n_plain = 4 if after_tile is None else 2
        for c4 in range(n_plain):
            sl = next_slab()
            wv, wkeys = _wview(sl, nk)
            dl = defer_last and c4 == 0 and nk > 1
            batches = [list(range(0, 6)), list(range(6, 9))] if dl else [[t] for t in range(9)]
            for bt in batches:
                slots = {}
                for t in bt:
                    g, b = tm_slots[tm_ctr[0] % 6]
                    tm_ctr[0] += 1
                    slots[t] = (g, b, PSG[g][:, b * 512:(b + 1) * 512])
                phases = [(0, nk - 1), (nk - 1, nk)] if dl else [(0, nk)]
                for (k0, k1) in phases:
                    for t in bt:
                        g, b, po = slots[t]

                        def fn(t=t, po=po, k0=k0, k1=k1, wv=wv):
                            last = None
                            for kc in range(k0, k1):
                                last = pe.matmul(po, lhsT=aT[:, kc, t * 128:(t + 1) * 128], rhs=wv[:, kc, 0:512],
                                                 start=(kc == 0), stop=(kc == nk - 1))
                            return last
                        S.op('pe', fn, reads=wkeys + akeys[k0:k1], writes=[('ps', g, b)])
                for t in bt:
                    g, b, po = slots[t]
                    S.op('dve', lambda t=t, c4=c4, po=po: dve.tensor_tensor(
                        out=RH[:, t, c4 * 512:(c4 + 1) * 512], in0=po, in1=RH[:, t, c4 * 512:(c4 + 1) * 512], op=ALU.add),
                        reads=[('ps', g, b), ('h', t)], writes=[('h', t)])
            prefetch()
        if after_tile is None:
            return
        slab_hold[0] = slab_state['used']
        sls = [next_slab(), next_slab()]
        views = [_wview(sl, nk) for sl in sls]
        for t in range(9):
            for c4 in (2, 3):
                wv, wkeys = views[c4 - 2]
                g, b = tm_slots[tm_ctr[0] % 6]
                tm_ctr[0] += 1
                po = PSG[g][:, b * 512:(b + 1) * 512]

                def fn(t=t, po=po, wv=wv):
                    last = None
                    for kc in range(nk):
                        last = pe.matmul(po, lhsT=aT[:, kc, t * 128:(t + 1) * 128], rhs=wv[:, kc, 0:512],
                                         start=(kc == 0), stop=(kc == nk - 1))
                    return last
                S.op('pe', fn, reads=wkeys + akeys, writes=[('ps', g, b)])
                S.op('dve', lambda t=t, c4=c4, po=po: dve.tensor_tensor(
                    out=RH[:, t, c4 * 512:(c4 + 1) * 512], in0=po, in1=RH[:, t, c4 * 512:(c4 + 1) * 512], op=ALU.add),
                    reads=[('ps', g, b), ('h', t)], writes=[('h', t)])
            after_tile(t)
        slab_hold[0] = None
        prefetch()

    if stop >= 2:
        for t in range(9):
            S.dma('sp', RH[:, t, :], x_all[t * 128:(t + 1) * 128, :], writes=[('h', t)])
        lin_tm_residual(4, KC, RB, AK, after_tile=make_norm_cb(1) if stop >= 3 else None)
        norm_cb_flush()
        if dbg == 'p2':
            for t in range(9):
                S.dma('sp', dbg_d[:, t * D:(t + 1) * D], RH[:, t, :], reads=[('h', t)])

    def lin_fm_to(dst, dkeys, xT, xkeys, func=None):
        for c8 in range(8):
            sl = next_slab()
            for j in range(2):
                c = c8 * 2 + j
                mm_fm(sl, j, xT, MT, j, xkeys=xkeys)
                S.op('act', lambda j=j, c=c: act.copy(out=dst[:, c, :], in_=PSG[j][:, 0:NT]),
                     reads=pkeys(j, 0, NT), writes=[dkeys[c]])
            prefetch()

    if stop >= 3:
        lin_fm_to(RB, AK, RA, xk)
        S.fence()
        if dbg == 'p3':
            for c in range(16):
                S.dma('pool', dbg_d[:, c * NT:(c + 1) * NT], RB[:, c, :], reads=[('aT', c)])

    if stop >= 4:
        RAf = RA[:].rearrange("p a b -> p (a b)")
        mstage = stgF[:, :]
        memnT = RAf[:, 4096:8192].rearrange("p (c m) -> p c m", c=16)
        mkT = RAf[:, 8192:12288].rearrange("p (c m) -> p c m", c=16)
        mvb = RAf[:, 12288:16384].rearrange("p (m f) -> p m f", m=2)
        xsm = RAf[:, 16384:18432]
        for mc in range(2):
            S.dma('sp', mstage, mem[mc * 128:(mc + 1) * 128, :], writes=[('mstage',)])
            norm_T(mstage, [('mstage',)], 128, 2, memnT, mc * 128, ('memnT', mc), xsb=xsm)
        S.fence()
        ostg = [stgF[:, 0:512], stgF[:, 512:1024]]
        mkt = [RAf[:, 0:512], RAf[:, 512:1024]]
        kv_ctr = [0]
        for (w_i, o_d) in ((0, o_mk), (1, o_mv)):
            for c4 in range(4):
                sl = next_slab()
                for mc in range(2):
                    i = kv_ctr[0]
                    kv_ctr[0] += 1
                    g, b = tm_slots[i % 6]
                    po = PSG[g][:, b * 512:(b + 1) * 512]
                    st_ = i % 2

                    def fn(mc=mc, sl=sl, po=po):
                        last = None
                        for kc in range(KC):
                            last = pe.matmul(po, lhsT=memnT[:, kc, mc * 128:(mc + 1) * 128], rhs=WW[sl][:, kc, 0:512],
                                             start=(kc == 0), stop=(kc == KC - 1))
                        return last
                    S.op('pe', fn, reads=[('ws', 2 * sl), ('ws', 2 * sl + 1), ('memnT', 0), ('memnT', 1)], writes=[('ps', g, b)])
                    S.op('act', lambda po=po, st_=st_: act.copy(out=ostg[st_], in_=po), reads=[('ps', g, b)],
                         writes=[('ostg', st_)])
                    S.dma('sp', o_d[mc * 128:(mc + 1) * 128, c4 * 512:(c4 + 1) * 512], ostg[st_], reads=[('ostg', st_)])
                    if w_i == 1:
                        S.op('dve', lambda po=po, mc=mc, c4=c4: dve.tensor_scalar(out=mvb[:, mc, c4 * 512:(c4 + 1) * 512], in0=po,
                                                                                  scalar1=1.0, scalar2=None, op0=ALU.mult),
                             reads=[('ps', g, b)], writes=[('mvb',)])
                    else:
                        S.op('dve', lambda po=po, st_=st_: dve.tensor_scalar(out=mkt[st_], in0=po, scalar1=1.0, scalar2=None, op0=ALU.mult),
                             reads=[('ps', g, b)], writes=[('mkt', st_)])

                        def trk(st_=st_):
                            last = None
                            for j in range(4):
                                last = pe.transpose(out=pvD[:, j * 128:(j + 1) * 128], in_=mkt[st_][:, j * 128:(j + 1) * 128],
                                                    identity=identb[:, :])
                            return last
                        S.op('pe', trk, reads=[('mkt', st_), ('c', 'identb')], writes=[('ps', 'D', 0)])
                        S.op('act', lambda mc=mc, c4=c4: act.copy(
                            out=mkT[:, c4 * 4:c4 * 4 + 4, mc * 128:(mc + 1) * 128],
                            in_=pvD[:, 0:512].rearrange("p (j m) -> p j m", j=4)),
                            reads=[('ps', 'D', 0)], writes=[('mkT',)])
                prefetch()

        qs = RAf[:, 1536:2560].rearrange("p (c t) -> p c t", c=16)
        S.op('dve', lambda: dve.tensor_scalar(out=qs, in0=RB[:, :, 0:64], scalar1=1.0, scalar2=None, op0=ALU.mult), reads=AK, writes=[('qs',)])
        pTs2 = [RAf[:, 2560:3584].rearrange("p (m t) -> p m t", m=2), RAf[:, 0:1024].rearrange("p (m t) -> p m t", m=2)]
        rden = RAf[:, 3584:4608].bitcast(F32)
        SC = 512 ** -0.5
        st_sets = [[(psA[:, 0:512], ('ps', 0, 0)), (psA[:, 512:1024], ('ps', 0, 1))],
                   [(psA[:, 1024:1536], ('ps', 0, 2)), (psC[:, :], ('ps', 'C', 0))]]
        o_slots = [(psB, 0, ('ps', 1, 0)), (psB, 512, ('ps', 1, 1)), (psB, 1024, ('ps', 1, 2))]
        steps = [(h, a, b) for h in range(4) for (a, b) in MT]
        o_ctr = [0]

        def emit_sf(i):
            h, a, b = steps[i]
            n = b - a
            pT = pTs2[i % 2]
            for mc in range(2):
                pa, pk = st_sets[i % 2][mc]

                def sf(h=h, a=a, b=b, n=n, mc=mc, pa=pa):
                    last = None
                    for dc in range(4):
                        last = pe.matmul(pa[:, 0:n], lhsT=mkT[:, 4 * h + dc, mc * 128:(mc + 1) * 128],
                                         rhs=RB[:, 4 * h + dc, a:b], start=(dc == 0), stop=(dc == 3))
                    return last
                S.op('pe', sf, reads=[('mkT',)] + [('aT', 4 * h + dc) for dc in range(4)], writes=[pk])
            for mc in range(2):
                pa, pk = st_sets[i % 2][mc]
                S.op('act', lambda n=n, mc=mc, pa=pa, pT=pT: act.activation(out=pT[:, mc, 0:n], in_=pa[:, 0:n], func=AF.Exp, scale=SC),
                     reads=[pk], writes=[('pT', i % 2, mc)])

        def emit_rest(i):
            h, a, b = steps[i]
            n = b - a
            pT = pTs2[i % 2]
            ptk = [('pT', i % 2, 0), ('pT', i % 2, 1)]

            def df(n=n, pT=pT):
                last = None
                for mc in range(2):
                    last = pe.matmul(psD[:, 0:n], lhsT=onesb[:, :], rhs=pT[:, mc, 0:n], start=(mc == 0), stop=(mc == 1))
                return last
            S.op('pe', df, reads=ptk + [('c', 'onesb')], writes=[('ps', 'D', 0)])
            S.op('dve', lambda n=n: dve.reciprocal(out=rden[:, 0:n], in_=psD[:, 0:n]), reads=[('ps', 'D', 0)], writes=[('rden',)])
            for dc in range(4):
                pt_, off, pk = o_slots[o_ctr[0] % 3]
                o_ctr[0] += 1

                def of(h=h, dc=dc, n=n, pt_=pt_, off=off, pT=pT):
                    last = None
                    for mc in range(2):
                        last = pe.matmul(pt_[:, off:off + n], lhsT=mvb[:, mc, (4 * h + dc) * 128:(4 * h + dc + 1) * 128],
                                         rhs=pT[:, mc, 0:n], start=(mc == 0), stop=(mc == 1))
                    return last
                S.op('pe', of, reads=ptk + [('mvb',)], writes=[pk])
                S.op('dve', lambda h=h, dc=dc, a=a, b=b, n=n, pt_=pt_, off=off: dve.tensor_tensor(
                    out=RB[:, 4 * h + dc, a:b], in0=pt_[:, off:off + n], in1=rden[:, 0:n], op=ALU.mult),
                    reads=[pk, ('rden',)], writes=[('aT', 4 * h + dc)])

        emit_sf(0)
        for i in range(len(steps)):
            if i + 1 < len(steps):
                emit_sf(i + 1)
            emit_rest(i)
        S.fence()

        NB = 4
        kvb = RAf[:, 4608:4608 + NB * 2048].rearrange("p (n x) -> p n x", n=NB)
        kTb = [RAf[:, 12800 + i * 1024:12800 + (i + 1) * 1024].rearrange("p (c m) -> p c m", c=4) for i in range(2)]
        pTs = RAf[:, 14848:14848 + 16].rearrange("p (i m t) -> p i m t", i=2, m=2)
        rds = RAf[:, 14880:14896].bitcast(F32).rearrange("p (i t) -> p i t", i=2)
        it = 0
        for s_ in range(NSEQ):
            for h in range(4):
                sl = it % NB
                i2 = it % 2
                it += 1
                kb = kvb[:, sl, 0:1024].rearrange("p (m d) -> p m d", m=2)
                vb = kvb[:, sl, 1024:2048].rearrange("p (m d) -> p m d", m=2)
                S.dma('pool', kb, cmk_d[s_, :, h * 512:(h + 1) * 512].rearrange("(m p) d -> p m d", p=128),
                      writes=[('kvb', sl, 0)])
                S.dma('pool', vb, cmv_d[s_, :, h * 512:(h + 1) * 512].rearrange("(m p) d -> p m d", p=128),
                      writes=[('kvb', sl, 1)])
                pz = psC if i2 == 0 else psD
                pzk = ('ps', 'C' if i2 == 0 else 'D', 0)
                pzb = pz[:].bitcast(BF16)

                def trf(kb=kb, pzb=pzb):
                    last = None
                    for mc in range(2):
                        for dc in range(4):
                            last = pe.transpose(out=pzb[:, dc * 256 + mc * 128:dc * 256 + (mc + 1) * 128],
                                                in_=kb[:, mc, dc * 128:(dc + 1) * 128], identity=identb[:, :])
                    return last
                S.op('pe', trf, reads=[('kvb', sl, 0), ('c', 'identb')], writes=[pzk])
                S.op('act', lambda i2=i2, pzb=pzb: act.copy(out=kTb[i2].rearrange("p c m -> p (c m)"), in_=pzb[:, 0:1024]),
                     reads=[pzk], writes=[('kTb', i2)])
                pw = PSG[i2]
                pwk = [('ps', i2, 0), ('ps', i2, 1), ('ps', i2, 2)]

                def scf(i2=i2, h=h, s_=s_, pw=pw):
                    last = None
                    for mc in range(2):
                        for dc in range(4):
                            last = pe.matmul(pw[:, mc * 4:mc * 4 + 4], lhsT=kTb[i2][:, dc, mc * 128:(mc + 1) * 128],
                                             rhs=qs[:, 4 * h + dc, 4 * s_:4 * s_ + 4], start=(dc == 0), stop=(dc == 3))
                    return last
                S.op('pe', scf, reads=[('kTb', i2), ('qs',)], writes=[pwk[0]])
                S.op('act', lambda i2=i2, pw=pw: act.activation(out=pTs[:, i2].rearrange("p m t -> p (m t)"), in_=pw[:, 0:8],
                                                                func=AF.Exp, scale=SC),
                     reads=[pwk[0]], writes=[('pTs', i2)])

                def pvf(i2=i2, pw=pw, vb=vb):
                    last = None
                    for mc in range(2):
                        last = pe.matmul(pw[:, 512:516], lhsT=onesb[:, :], rhs=pTs[:, i2, mc, :], start=(mc == 0), stop=(mc == 1))
                    for dc in range(4):
                        for mc in range(2):
                            last = pe.matmul(pw[:, 1024 + dc * 4:1024 + dc * 4 + 4], lhsT=vb[:, mc, dc * 128:(dc + 1) * 128],
                                             rhs=pTs[:, i2, mc, :], start=(mc == 0), stop=(mc == 1))
                    return last
                S.op('pe', pvf, reads=[('pTs', i2), ('kvb', sl, 1), ('c', 'onesb')], writes=[pwk[1], pwk[2]])
                S.op('dve', lambda i2=i2, pw=pw: dve.reciprocal(out=rds[:, i2, :], in_=pw[:, 512:516]),
                     reads=[pwk[1]], writes=[('rds', i2)])
                S.op('dve', lambda i2=i2, pw=pw, h=h, s_=s_: dve.tensor_tensor(
                    out=RB[:, 4 * h:4 * h + 4, 4 * s_:4 * s_ + 4], in0=pw[:, 1024:1040].rearrange("p (c t) -> p c t", c=4),
                    in1=rds[:, i2, :].unsqueeze(1).broadcast_to([128, 4, 4]), op=ALU.mult),
                    reads=[pwk[2], ('rds', i2)], writes=[('aT', 4 * h + dc) for dc in range(4)])
        S.fence()
        if dbg == 'p4':
            for c in range(16):
                S.dma('pool', dbg_d[:, c * NT:(c + 1) * NT], RB[:, c, :], reads=[('aT', c)])

    if stop >= 5:
        lin_tm_residual(4, KC, RB, AK, after_tile=make_norm_cb(3) if stop >= 6 else None)
        norm_cb_flush()
        if dbg == 'p5':
            for t in range(9):
                S.dma('sp', dbg_d[:, t * D:(t + 1) * D], RH[:, t, :], reads=[('h', t)])

    if stop >= 6:
        S.op('dve', lambda: dve.memset(stat[:, 6:7], 0.0), reads=[],
             writes=AK + [('gs',), ('cc',), ('pcg',), ('gtl',)] + [('act', ci) for ci in range(8)])
        RBf = RB[:].rearrange("p a b -> p (a b)")
        actb = RB[:, 0:8, :]
        gs = RBf[:, 8 * NT:10 * NT].bitcast(F32)
        cc = RBf[:, 10 * NT:12 * NT].bitcast(F32)
        misc = RBf[:, 12 * NT:16 * NT].bitcast(F32)
        pcg = misc[:, 0:256].rearrange("p (j r) -> p j r", j=8)
        gtl = misc[:, 256:256 + 272].rearrange("p (j r) -> p j r", j=8)
        rowb = stgF[0:34, 0:1024]
        for (f0, f1) in FGROUPS:
            ng = f1 - f0
            S.dma('sp', rowb[0:32, 0:ng * 128], cffn_d[:, f0 * 128:f1 * 128], writes=[('rowb',)])

            def trc2(ng=ng):
                last = None
                for j in range(ng):
                    last = pe.transpose(out=psC[:, j * 32:(j + 1) * 32], in_=rowb[0:32, j * 128:(j + 1) * 128],
                                        identity=identf[0:32, 0:32])
                return last
            S.op('pe', trc2, reads=[('rowb',), ('c', 'identf')], writes=[('ps', 'C', 0)])
            S.op('act', lambda ng=ng: act.copy(out=pcg.rearrange("p j r -> p (j r)")[:, 0:ng * 32], in_=psC[:, 0:ng * 32]),
                 reads=[('ps', 'C', 0)], writes=[('pcg',)])
            for ci in range(ng):
                fc = f0 + ci
                sl = next_slab()
                mm_fm(sl, 0, RA, MT, 0, xkeys=xk)
                prefetch()
                S.op('act', lambda: act.copy(out=gs, in_=psA[:, 0:NT]), reads=pkeys(0, 0, NT), writes=[('gs',)])
                sl = next_slab()
                mm_fm(sl, 0, RA, MT, 1, xkeys=xk)
                prefetch()
                S.op('act', lambda ci=ci: act.copy(out=gtl[:, ci, 0:32].rearrange("p (s r) -> p s r", s=16),
                                                  in_=gs[:, 0:64].rearrange("p (s t) -> p s t", s=16)[:, :, 2:4]),
                     reads=[('gs',)], writes=[('gtl',)])
                S.op('act', lambda ci=ci: act.copy(out=gtl[:, ci, 32:34], in_=gs[:, NT - 2:NT]), reads=[('gs',)], writes=[('gtl',)])
                S.op('dve', lambda: dve.tensor_scalar(out=gs[:, 64:128], in0=gs[:, 64:128], scalar1=ovm[:, 0:1], scalar2=None,
                                                      op0=ALU.mult), reads=[('gs',), ('c', 'ovm')], writes=[('gs',)])
                S.op('dve', lambda fc=fc: dve.tensor_scalar(out=cc, in0=gs, scalar1=fcw[:, 2, fc:fc + 1], scalar2=fcb[:, fc:fc + 1],
                                                            op0=ALU.mult, op1=ALU.add),
                     reads=[('gs',), ('c', 'fcw'), ('c', 'fcb')], writes=[('cc',)])
                y3 = cc[:, 0:64].rearrange("p (s t) -> p s t", s=16)
                u3 = gs[:, 0:64].rearrange("p (s t) -> p s t", s=16)
                p3 = pcg[:, ci, :].rearrange("p (s r) -> p s r", s=16)
                for (tap, sh) in ((1, 1), (0, 2)):
                    S.op('dve', lambda fc=fc, tap=tap, sh=sh: dve.scalar_tensor_tensor(
                        out=cc[:, 64 + sh:NT], in0=gs[:, 64:NT - sh], scalar=fcw[:, tap, fc:fc + 1], in1=cc[:, 64 + sh:NT],
                        op0=ALU.mult, op1=ALU.add), reads=[('gs',), ('c', 'fcw')], writes=[('cc',)])
                    S.op('dve', lambda fc=fc, tap=tap, sh=sh, y3=y3, u3=u3: dve.scalar_tensor_tensor(
                        out=y3[:, :, sh:4], in0=u3[:, :, 0:4 - sh], scalar=fcw[:, tap, fc:fc + 1], in1=y3[:, :, sh:4],
                        op0=ALU.mult, op1=ALU.add), reads=[('gs',), ('c', 'fcw')], writes=[('cc',)])
                S.op('dve', lambda fc=fc, y3=y3, p3=p3: dve.scalar_tensor_tensor(
                    out=y3[:, :, 0:1], in0=p3[:, :, 1:2], scalar=fcw[:, 1, fc:fc + 1], in1=y3[:, :, 0:1],
                    op0=ALU.mult, op1=ALU.add), reads=[('pcg',), ('c', 'fcw')], writes=[('cc',)])
                S.op('dve', lambda fc=fc, y3=y3, p3=p3: dve.scalar_tensor_tensor(
                    out=y3[:, :, 0:2], in0=p3[:, :, 0:2], scalar=fcw[:, 0, fc:fc + 1], in1=y3[:, :, 0:2],
                    op0=ALU.mult, op1=ALU.add), reads=[('pcg',), ('c', 'fcw')], writes=[('cc',)])
                S.op('act', lambda: act.activation(out=cc, in_=cc, func=AF.Silu), reads=[('cc',)], writes=[('cc',)])
                S.op('dve', lambda ci=ci: dve.tensor_tensor(out=actb[:, ci, :], in0=psB[:, 0:NT], in1=cc, op=ALU.mult),
                     reads=pkeys(1, 0, NT) + [('cc',)], writes=[('act', ci)])
            lin_tm_residual(4, ng, actb, [('act', ci) for ci in range(ng)], defer_last=True)

            def trg(ng=ng):
                last = None
                for j in range(ng):
                    pt = psC if j < 4 else psD
                    last = pe.transpose(out=pt[0:34, (j % 4) * 128:(j % 4 + 1) * 128], in_=gtl[:, j, :], identity=identf[:, :])
                return last
            S.op('pe', trg, reads=[('gtl',), ('c', 'identf')], writes=[('ps', 'C', 0), ('ps', 'D', 0)])
            n1 = min(ng, 4) * 128
            S.op('act', lambda n1=n1: act.copy(out=rowb[0:34, 0:n1], in_=psC[0:34, 0:n1]), reads=[('ps', 'C', 0)], writes=[('rowb',)])
            if ng > 4:
                S.op('act', lambda ng=ng: act.copy(out=rowb[0:34, 512:ng * 128], in_=psD[0:34, 0:(ng - 4) * 128]),
                     reads=[('ps', 'D', 0)], writes=[('rowb',)])
            S.dma('sp', o_ffn_s[:, f0 * 128:f1 * 128], rowb[0:32, 0:ng * 128], reads=[('rowb',)])
            S.dma('sp', o_ffn_p[:, f0 * 128:f1 * 128], rowb[32:34, 0:ng * 128], reads=[('rowb',)])
        S.fence()

    if stop >= 7:
        gbc = stgF[:, :]
        with nc.allow_non_contiguous_dma(reason="gain broadcast"):
            S.dma('sp', gbc, gains_d[4:5, :].partition_broadcast(128).rearrange("p o d -> p (o d)"), writes=[('gbc',)])
        for t in range(9):
            s2 = t % 2
            ss = stat[:, 2 * s2:2 * s2 + 1]
            rs = stat[:, 2 * s2 + 1:2 * s2 + 2]
            junk = xs_bufs[s2]
            S.op('act', lambda t=t, junk=junk, ss=ss: act.activation(out=junk, in_=RH[:, t, :], func=AF.Square, accum_out=ss),
                 reads=[('h', t)], writes=[('junk', s2), ('st', s2)])
            S.op('act', lambda ss=ss: act.activation(out=ss, in_=ss, func=AF.Sqrt, scale=1.0 / D, bias=EPS),
                 reads=[('st', s2)], writes=[('st', s2)])
            S.op('dve', lambda ss=ss, rs=rs: dve.reciprocal(out=rs, in_=ss), reads=[('st', s2)], writes=[('rs', s2)])
            S.op('dve', lambda t=t, rs=rs: dve.scalar_tensor_tensor(out=RH[:, t, :], in0=RH[:, t, :], scalar=rs, in1=gbc,
                                                                    op0=ALU.mult, op1=ALU.mult),
                 reads=[('h', t), ('rs', s2), ('gbc',)], writes=[('h', t)])
            S.dma('sp', y_all[t * 128:(t + 1) * 128, :], RH[:, t, :], reads=[('h', t)])

    S.finish()
    es.close()
    return nc


def make_consts():
    c = np.zeros((128, 128 * 3 + 17), np.float32)
    c[:, 0:128] = np.eye(128, dtype=np.float32)
    j = np.arange(128)[:, None]
    i = np.arange(128)[None, :]
    c[:, 128:256] = (j <= i).astype(np.float32)
    m0 = np.zeros((128, 128), np.float32)
    samp = (j < 64) & (i < 64) & ((j // 4) == (i // 4)) & (j <= i)
    ovl = (j >= 64) & (i >= 64) & (j <= i)
    m0[samp | ovl] = 1.0
    c[:, 256:384] = m0
    for s in range(16):
        c[4 * s:4 * s + 4, 384 + s] = 1.0
    c[64:128, 384 + 16] = 1.0
    return c


def core_inputs(inp, c):
    seq, half = c // 2, c % 2
    xp = inp["x_prompt"][seq]
    pos0 = half * 1024
    x_all = np.zeros((NT, D), np.float32)
    x_all[0:64] = inp["x_sample"][16 * c:16 * c + 16].reshape(64, D)
    if half == 1:
        x_all[64:128] = xp[pos0 - 64:pos0]
    x_all[128:] = xp[pos0:pos0 + 1024]
    x_pre = np.zeros((1024, D), np.float32)
    if half == 1:
        x_pre[0:NPRE] = xp[0:NPRE]
    m = {
        "x_all": x_all, "x_pre": x_pre, "mem": np.ascontiguousarray(inp["mem_prompt"][seq]),
        "ovm": np.full((128, 1), float(half), np.float32),
        "cache_conv": np.ascontiguousarray(inp["cache_conv"][0, 16 * c:16 * c + 16].reshape(32, 1024)),
        "state_gla": np.ascontiguousarray(inp["state_gla"][0, 16 * c:16 * c + 16]),
        "cache_ffn": np.ascontiguousarray(inp["cache_ffn"][0, 16 * c:16 * c + 16].reshape(32, DFF)),
        "cache_mem_k": np.ascontiguousarray(inp["cache_mem_k"][0, 16 * c:16 * c + 16].reshape(16, 256, D)),
        "cache_mem_v": np.ascontiguousarray(inp["cache_mem_v"][0, 16 * c:16 * c + 16].reshape(16, 256, D)),
        "w_in": inp["w_in"][0], "w_out": inp["w_out"][0], "w_xq": inp["w_xq"][0], "w_xk": inp["w_xk"][0],
        "w_xv": inp["w_xv"][0], "w_xo": inp["w_xo"][0], "w_ffn_gate": inp["w_ffn_gate"][0],
        "w_ffn_up": inp["w_ffn_up"][0], "w_ffn_down": inp["w_ffn_down"][0],
        "gains": np.stack([inp["norm_mix"][0], inp["norm_x"][0], inp["norm_mem"][0], inp["norm_ffn"][0],
                           inp["norm_final"]]).astype(np.float32),
        "conv_w": inp["conv_w"][0], "w_gate2": inp["w_gate2"][0], "b_gate": inp["b_gate"],
        "gla_norm": inp["gla_norm"], "ffn_conv_w": inp["ffn_conv_w"][0], "ffn_conv_b": inp["ffn_conv_b"],
        "consts": make_consts(),
    }
    return {k: np.ascontiguousarray(np.asarray(v, dtype=np.float32)) for k, v in m.items()}


_NC_CACHE = {}


def kernel(**inputs):
    inp = {k: np.asarray(v) for k, v in inputs.items()}
    if 'nc' not in _NC_CACHE:
        _NC_CACHE['nc'] = build()
    nc = _NC_CACHE['nc']
    in_maps = [core_inputs(inp, c) for c in range(8)]
    res = run_bass_kernel_spmd(nc, in_maps, core_ids=list(range(8)))
    R = res.results
    y_prompt = np.zeros((4, 2048, D), np.float32)
    y_sample = np.zeros((128, 4, D), np.float32)
    conv_p = np.zeros((1, 4, 2, 1024), np.float32)
    gla_p = np.zeros((1, 4, 4, 128, 256), np.float32)
    ffn_p = np.zeros((1, 4, 2, DFF), np.float32)
    mk_p = np.zeros((1, 4, 256, 4, 512), np.float32)
    mv_p = np.zeros((1, 4, 256, 4, 512), np.float32)
    conv_s = np.zeros((1, 128, 2, 1024), np.float32)
    gla_s = np.zeros((1, 128, 4, 128, 256), np.float32)
    ffn_s = np.zeros((1, 128, 2, DFF), np.float32)
    for c in range(8):
        seq, half = c // 2, c % 2
        r = R[c]
        y_prompt[seq, half * 1024:(half + 1) * 1024] = r["y_all"][128:NT]
        y_sample[16 * c:16 * c + 16] = r["y_all"][0:64].reshape(16, 4, D)
        conv_s[0, 16 * c:16 * c + 16] = r["o_conv_s"].reshape(16, 2, 1024)
        gla_s[0, 16 * c:16 * c + 16] = r["o_gla_s"]
        ffn_s[0, 16 * c:16 * c + 16] = r["o_ffn_s"].reshape(16, 2, DFF)
        if half == 1:
            conv_p[0, seq] = r["o_conv_p"]
            gla_p[0, seq] = r["o_gla_p"]
            ffn_p[0, seq] = r["o_ffn_p"]
        else:
            mk_p[0, seq] = r["o_mk"].reshape(256, 4, 512)
            mv_p[0, seq] = r["o_mv"].reshape(256, 4, 512)
    return (y_prompt, y_sample, conv_p, gla_p, ffn_p, mk_p, mv_p, conv_s, gla_s, ffn_s)
```

```python
import os
import numpy as np
from contextlib import ExitStack
import concourse.bass as bass
import concourse.mybir as mybir
from concourse.bass_utils import run_bass_kernel_spmd

F32 = mybir.dt.float32
BF16 = mybir.dt.bfloat16
ALU = mybir.AluOpType
AF = mybir.ActivationFunctionType

NT = 1152
NPRE = 960
D = 2048
KC = 16
NSEQ = 16
DFF = 5632
FC = 44
COL = dict(bg=0, cg=1024, vc=2048, q=3072, k=3584, v=4096, r=5120, g1=6144)
MT = [(0, 512), (512, 1024), (1024, 1152)]
MTP = [(0, 512), (512, 960)]
EPS = 1e-6


class Sched:
    def __init__(self, nc, es):
        self.nc = nc
        self.engs = {'pe': nc.tensor, 'act': nc.scalar, 'dve': nc.vector, 'pool': nc.gpsimd, 'sp': nc.sync}
        self.sem = {k: es.enter_context(nc.semaphore('sem_' + k)) for k in ['pe', 'act', 'dve', 'pool']}
        self.cnt = {k: 0 for k in self.sem}
        self.ndma = 48
        self.dsem = [es.enter_context(nc.semaphore('dsem%d' % i)) for i in range(self.ndma)]
        self.dval = [0] * self.ndma
        self.dnext = {'w': 0, 'p': 0, 's': 0}
        self.seen = {e: {} for e in self.engs}
        self.lastw = {}
        self.readers = {}

    def _semof(self, k):
        return self.sem[k] if isinstance(k, str) else self.dsem[k]

    def _wait(self, eng, semkey, val):
        if self.seen[eng].get(semkey, 0) >= val:
            return
        self.seen[eng][semkey] = val
        self.engs[eng].wait_ge(self._semof(semkey), val)

    def _deps(self, eng, reads, writes):
        best = {}

        def add(t):
            if t is not None and best.get(t[0], 0) < t[1]:
                best[t[0]] = t[1]
        for k in reads:
            add(self.lastw.get(k))
        for k in writes:
            add(self.lastw.get(k))
            for s, v in self.readers.get(k, {}).items():
                add((s, v))
        for s, v in best.items():
            if s == 'pe' and eng == 'pe':
                continue
            self._wait(eng, s, v)

    def _commit(self, tok, reads, writes):
        for k in reads:
            r = self.readers.setdefault(k, {})
            if r.get(tok[0], 0) < tok[1]:
                r[tok[0]] = tok[1]
        for k in writes:
            self.lastw[k] = tok
            self.readers[k] = {}

    def op(self, eng, fn, reads=(), writes=()):
        if eng != 'pe':
            pr = [k for k in reads if k[0] == 'ps']
            if pr:
                writes = list(writes) + pr
        self._deps(eng, reads, writes)
        inst = fn()
        self.cnt[eng] += 1
        inst.then_inc(self.sem[eng], 1)
        tok = (eng, self.cnt[eng])
        self._commit(tok, reads, writes)
        return tok

    def dma(self, eng, out, in_, reads=(), writes=(), pool='g'):
        if pool != 'w':
            pool = 'p' if eng == 'pool' else 's'
        lo, hi = {'w': (0, 8), 'p': (8, 24), 's': (24, self.ndma)}[pool]
        i = lo + self.dnext[pool]
        self.dnext[pool] = (self.dnext[pool] + 1) % (hi - lo)
        if self.dval[i] > 0:
            self._wait(eng, i, self.dval[i])
        self._deps(eng, reads, writes)
        inst = self.engs[eng].dma_start(out=out, in_=in_)
        self.dval[i] += 16
        inst.then_inc(self.dsem[i], 16)
        tok = (i, self.dval[i])
        self._commit(tok, reads, writes)
        return tok

    def fence(self):
        for e in self.engs:
            for o in self.sem:
                if o != e and self.cnt[o] > 0:
                    self._wait(e, o, self.cnt[o])
            for i in range(8, self.ndma):
                if self.dval[i] > 0:
                    self._wait(e, i, self.dval[i])

    def finish(self):
        for i in range(self.ndma):
            if self.dval[i] > 0:
                self._wait('sp', i, self.dval[i])
        for o in self.sem:
            if self.cnt[o] > 0:
                self._wait('sp', o, self.cnt[o])


def build(stop=99, dbg=None):
    nc = bass.Bass("TRN2", target_bir_lowering=False)
    es = ExitStack()
    S = Sched(nc, es)
    pe, act, dve, pool, sp = nc.tensor, nc.scalar, nc.vector, nc.gpsimd, nc.sync

    def din(name, shape):
        return nc.dram_tensor(name, shape, F32, kind="ExternalInput").ap()

    def dout(name, shape):
        return nc.dram_tensor(name, shape, F32, kind="ExternalOutput").ap()

    x_all = din("x_all", [NT, D])
    x_pre = din("x_pre", [1024, D])
    mem = din("mem", [256, D])
    ovm_d = din("ovm", [128, 1])
    cconv_d = din("cache_conv", [32, 1024])
    sgla_d = din("state_gla", [NSEQ, 4, 128, 256])
    cffn_d = din("cache_ffn", [32, DFF])
    cmk_d = din("cache_mem_k", [NSEQ, 256, D])
    cmv_d = din("cache_mem_v", [NSEQ, 256, D])
    w_in = din("w_in", [D, 6160])
    w_out = din("w_out", [D, D])
    w_xq = din("w_xq", [D, D])
    w_xk = din("w_xk", [D, D])
    w_xv = din("w_xv", [D, D])
    w_xo = din("w_xo", [D, D])
    w_fg = din("w_ffn_gate", [D, DFF])
    w_fu = din("w_ffn_up", [D, DFF])
    w_fd = din("w_ffn_down", [DFF, D])
    gains_d = din("gains", [5, D])
    conv_w_d = din("conv_w", [3, 1024])
    wg2_d = din("w_gate2", [16, 512])
    bgate_d = din("b_gate", [1, 512])
    glan_d = din("gla_norm", [1, 1024])
    fcw_d = din("ffn_conv_w", [3, DFF])
    fcb_d = din("ffn_conv_b", [1, DFF])
    consts_d = din("consts", [128, 128 * 3 + 17])

    y_all = dout("y_all", [NT, D])
    o_conv_p = dout("o_conv_p", [2, 1024])
    o_gla_p = dout("o_gla_p", [4, 128, 256])
    o_ffn_p = dout("o_ffn_p", [2, DFF])
    o_mk = dout("o_mk", [256, D])
    o_mv = dout("o_mv", [256, D])
    o_conv_s = dout("o_conv_s", [32, 1024])
    o_gla_s = dout("o_gla_s", [NSEQ, 4, 128, 256])
    o_ffn_s = dout("o_ffn_s", [32, DFF])
    dbg_d = dout("dbg", [128, 16 * NT]) if dbg else None

    def sb(name, shape, dt):
        return es.enter_context(nc.sbuf_tensor("s_" + name, shape, dt))

    identb = sb("identb", [128, 128], BF16)
    identf = sb("identf", [128, 128], F32)
    maskC = sb("maskC", [128, 128], BF16)
    mask0 = sb("mask0", [128, 128], BF16)
    selm = sb("selm", [128, 17], F32)
    onesb = sb("onesb", [128, 128], BF16)
    onesf = sb("onesf", [128, 128], F32)
    gains = sb("gains", [128, 5, 16], F32)
    convw = sb("convw", [128, 3, 8], F32)
    fcw = sb("fcw", [128, 3, FC], F32)
    fcb = sb("fcb", [128, FC], F32)
    negb = sb("negb", [128, 4], F32)
    glan = sb("glan", [128, 8], F32)
    wg2 = sb("wg2", [16, 512], BF16)
    ovm = sb("ovm", [128, 1], F32)
    g1T = sb("g1T", [16, NT], BF16)
    expL = sb("expL", [128, 4, 32], F32)
    stat = sb("stat", [128, 8], F32)
    Sst = sb("Sst", [128, 4, 256], F32)
    Sbb2 = sb("Sbb", [128, 2, 1024], BF16)
    Sbb = [Sbb2[:, i, :].rearrange("p (h v) -> p h v", h=4) for i in range(2)]
    sb_cur = [0]

    stgF = sb("stgF", [128, D], F32)
    ws01 = sb("ws01", [128, 2, 16 * 256], BF16)
    WS = [ws01[:, i, :].rearrange("p (k n) -> p k n", k=16) for i in range(2)]
    WW = [ws01[:].rearrange("p a x -> p (a x)").rearrange("p (k n) -> p k n", k=16)]
    WSF = [ws01[:, i, :] for i in range(2)]
    RA = sb("RA", [128, 16, NT], BF16)
    RB = sb("RB", [128, 16, NT], BF16)

    psA = es.enter_context(nc.psum_tensor("psA", [128, 1536], F32))
    psB = es.enter_context(nc.psum_tensor("psB", [128, 1536], F32))
    psC = es.enter_context(nc.psum_tensor("psC", [128, 512], F32))
    psD = es.enter_context(nc.psum_tensor("psD", [128, 512], F32))
    PSG = [psA, psB]

    def pkeys(g, c0, c1):
        return [('ps', g, b) for b in range(c0 // 512, (c1 - 1) // 512 + 1)]

    with nc.allow_non_contiguous_dma(reason="small constant loads"):
        S.dma('pool', identb[:], consts_d[:, 0:128], writes=[('c', 'identb')])
        S.dma('sp', identf[:], consts_d[:, 0:128], writes=[('c', 'identf')])
        S.dma('pool', maskC[:], consts_d[:, 128:256], writes=[('c', 'maskC')])
        S.dma('pool', mask0[:], consts_d[:, 256:384], writes=[('c', 'mask0')])
        S.dma('sp', selm[:], consts_d[:, 384:401], writes=[('c', 'selm')])
        S.dma('sp', gains[:], gains_d.rearrange("g (c p) -> p g c", p=128), writes=[('c', 'gains')])
        S.dma('sp', convw[:], conv_w_d.rearrange("k (c p) -> p k c", p=128), writes=[('c', 'convw')])
        S.dma('sp', fcw[:], fcw_d.rearrange("k (c p) -> p k c", p=128), writes=[('c', 'fcw')])
        S.dma('sp', fcb[:], fcb_d.rearrange("o (c p) -> p (o c)", p=128), writes=[('c', 'fcb')])
        S.dma('sp', negb[:], bgate_d.rearrange("o (c p) -> p (o c)", p=128), writes=[('c', 'negb')])
        S.dma('sp', glan[:], glan_d.rearrange("o (c p) -> p (o c)", p=128), writes=[('c', 'glan')])
        S.dma('pool', wg2[:], wg2_d, writes=[('c', 'wg2')])
        S.dma('sp', ovm[:], ovm_d, writes=[('c', 'ovm')])
    S.op('dve', lambda: dve.memset(onesb[:], 1.0), writes=[('c', 'onesb')])
    S.op('dve', lambda: dve.memset(onesf[:], 1.0), writes=[('c', 'onesf')])
    S.op('dve', lambda: dve.tensor_scalar(out=negb[:], in0=negb[:], scalar1=-1.0, scalar2=None, op0=ALU.mult),
         reads=[('c', 'negb')], writes=[('c', 'negb')])
    S.op('dve', lambda: dve.memset(Sst[:], 0.0), writes=[('Sst', h_) for h_ in range(4)])
    S.op('dve', lambda: dve.memset(Sbb2[:], 0.0), writes=[('Sbb', 0), ('Sbb', 1)])

    slabs = []
    slab_state = {'issued': 0, 'used': 0, 'p1_end': None, 'rr': 0}
    resident = {}

    slab_hold = [None]

    def _free(slot):
        r = resident.get(slot)
        if r is not None and slab_hold[0] is not None and r >= slab_hold[0]:
            return False
        return r is None or r <= slab_state['used'] - 2

    def _try_issue(i):
        w_ap, r0, nk, c0, cw = slabs[i]
        src = w_ap[r0:r0 + nk * 128, c0:c0 + cw].rearrange("(kc p) n -> p kc n", p=128)
        if nk * cw > 4096:
            for ws_ in range(len(WW)):
                if _free(2 * ws_) and _free(2 * ws_ + 1):
                    resident[2 * ws_] = i
                    resident[2 * ws_ + 1] = i
                    slab_state[('slot', i)] = ws_
                    S.dma('pool', WW[ws_][:, 0:nk, 0:cw], src, writes=[('ws', 2 * ws_), ('ws', 2 * ws_ + 1)], pool='w')
                    return True
            return False
        n = len(WS)
        for d in range(n):
            sl_ = (slab_state['rr'] + d) % n
            if _free(sl_):
                slab_state['rr'] = (sl_ + 1) % n
                resident[sl_] = i
                slab_state[('slot', i)] = sl_
                dst = WS[sl_][:, 0:nk, 0:cw] if cw <= 256 else WSF[sl_][:, 0:nk * cw].rearrange("p (k n) -> p k n", k=nk)
                S.dma('pool', dst, src, writes=[('ws', sl_)], pool='w')
                return True
        return False

    def slab_issue_upto(n):
        while slab_state['issued'] < min(n, len(slabs)):
            if not _try_issue(slab_state['issued']):
                return False
            slab_state['issued'] += 1
        return True

    def next_slab():
        i = slab_state['used']
        slab_state['used'] += 1
        ok = slab_issue_upto(i + 1)
        assert ok and slab_state['issued'] >= i + 1, "no free weight slot"
        prefetch()
        return slab_state[('slot', i)]

    def prefetch():
        slab_issue_upto(slab_state['used'] + 3)

    def add_fm(w_ap, c0, ncols):
        c = c0
        while c < c0 + ncols:
            cw = min(256, c0 + ncols - c)
            slabs.append((w_ap, 0, KC, c, cw))
            c += cw

    add_fm(w_in, COL['g1'], 16)
    add_fm(w_in, COL['k'], 512)
    add_fm(w_in, COL['v'], 1024)
    add_fm(w_in, COL['g1'], 16)
    add_fm(w_in, COL['r'], 1024)
    add_fm(w_in, COL['k'], 512)
    add_fm(w_in, COL['q'], 512)
    add_fm(w_in, COL['v'], 1024)
    for j in range(4):
        for nm in ('cg', 'vc', 'bg'):
            add_fm(w_in, COL[nm] + 256 * j, 256)
    slab_state['p1_end'] = len(slabs)
    for c4 in range(4):
        slabs.append((w_out, 0, KC, c4 * 512, 512))
    add_fm(w_xq, 0, D)
    for w_ in (w_xk, w_xv):
        for c4 in range(4):
            slabs.append((w_, 0, KC, c4 * 512, 512))
    for c4 in range(4):
        slabs.append((w_xo, 0, KC, c4 * 512, 512))
    FGROUPS = [(0, 8), (8, 16), (16, 24), (24, 32), (32, 40), (40, 44)]
    for (f0, f1) in FGROUPS:
        for c in range(f0, f1):
            slabs.append((w_fg, 0, KC, c * 128, 128))
            slabs.append((w_fu, 0, KC, c * 128, 128))
        for c4 in range(4):
            slabs.append((w_fd, f0 * 128, f1 - f0, c4 * 512, 512))

    def mm_fm(slot, j, xT, mts, psg, ncol=128, xkeys=None):
        for (a, b) in mts:
            def fn(a=a, b=b):
                last = None
                for kc in range(KC):
                    last = pe.matmul(PSG[psg][0:ncol, a:b], lhsT=WS[slot][:, kc, j * 128:j * 128 + ncol],
                                     rhs=xT[:, kc, a:b], start=(kc == 0), stop=(kc == KC - 1))
                return last
            xk_ = [k for k in xkeys if a <= k[1] * 128 < b]
            S.op('pe', fn, reads=[('ws', slot)] + xk_, writes=pkeys(psg, a, b))

    xs_bufs = [RB[:, 8 + 2 * i:10 + 2 * i, :].rearrange("p a b -> p (a b)")[:, 0:2048] for i in range(4)]
    norm_ctr = [0]
    norm_alt = [True]

    def norm_T(src, srckeys, nrows, gi, dstT, col0, dstkey, xsb=None, defer=False):
        i = norm_ctr[0]
        norm_ctr[0] += 1
        s = i % len(xs_bufs)
        xs = xs_bufs[s] if xsb is None else xsb
        xkey = ('xs', s) if xsb is None else ('xs', 'm')
        ss = stat[:, 2 * s:2 * s + 1]
        rs = stat[:, 2 * s + 1:2 * s + 2]
        S.op('act', lambda: act.activation(out=xs[0:nrows, :], in_=src, func=AF.Square, accum_out=ss[0:nrows, :]),
             reads=srckeys, writes=[xkey, ('st', s)])
        S.op('act', lambda: act.activation(out=ss[0:nrows, :], in_=ss[0:nrows, :], func=AF.Sqrt, scale=1.0 / D, bias=EPS),
             reads=[('st', s)], writes=[('st', s)])
        S.op('dve', lambda: dve.reciprocal(out=rs[0:nrows, :], in_=ss[0:nrows, :]),
             reads=[('st', s)], writes=[('rs', s)])
        S.op('pool', lambda: pool.tensor_scalar(out=xs[0:nrows, :], in0=src, scalar1=rs[0:nrows, :], scalar2=0.0,
                                                op0=ALU.mult, op1=ALU.add),
             reads=srckeys + [('rs', s)], writes=[xkey])
        def part2():
            for half in range(2):
                if norm_alt[0]:
                    bsel = [[(psA[:, 0:512], ('ps', 0, 0)), (psA[:, 512:1024], ('ps', 0, 1))],
                            [(psA[:, 1024:1536], ('ps', 0, 2)), (psC[:, :], ('ps', 'C', 0))]][i % 2][half]
                    pst_ap, pk0 = bsel
                    pk = [pk0]
                    pv = pst_ap.bitcast(BF16)
                else:
                    pst = psC if half == 0 else psD
                    pk = [('ps', 'C' if half == 0 else 'D', 0)]
                    pv = pst[:].bitcast(BF16)

                def tr(half=half, pv=pv):
                    last = None
                    for c in range(8):
                        cc = half * 8 + c
                        last = pe.transpose(out=pv[:, c * 128:c * 128 + nrows],
                                            in_=xs[0:nrows, cc * 128:(cc + 1) * 128], identity=identb[0:nrows, 0:nrows])
                    return last
                S.op('pe', tr, reads=[xkey, ('c', 'identb')], writes=pk)
                gb = gains[:, gi, half * 8:half * 8 + 8].unsqueeze(2).broadcast_to([128, 8, nrows])
                src_ps = pv.rearrange("p (c t) -> p c t", c=8)[:, :, 0:nrows]
                S.op('dve', lambda half=half, gb=gb, src_ps=src_ps: dve.tensor_tensor(
                    out=dstT[:, half * 8:half * 8 + 8, col0:col0 + nrows], in0=src_ps, in1=gb, op=ALU.mult),
                    reads=pk + [('c', 'gains')], writes=[dstkey])
        if defer:
            return part2
        part2()

    es_p1 = ExitStack()
    xstage_t = es_p1.enter_context(nc.sbuf_tensor("xstage", [128, 2, D], F32))
    xstage = [xstage_t[:, 0, :], xstage_t[:, 1, :]]
    qt = es_p1.enter_context(nc.sbuf_tensor("qt", [128, 4, NT], BF16))
    kt = es_p1.enter_context(nc.sbuf_tensor("kt", [128, 4, NT], BF16))
    cums = es_p1.enter_context(nc.sbuf_tensor("cums", [128, 4, NT], F32))
    vt = cums[:].rearrange("p h t -> p (h t)").bitcast(BF16).rearrange("p (t v) -> p t v", t=9)
    scr = es_p1.enter_context(nc.sbuf_tensor("scr", [128, 3, 1152], F32))
    sbs_t = es_p1.enter_context(nc.sbuf_tensor("sbs_t", [128, 4, 1024], BF16))
    khm_t = es_p1.enter_context(nc.sbuf_tensor("khm_t", [128, 2, 512], BF16))
    cvt = es_p1.enter_context(nc.sbuf_tensor("cvt", [128, 4, NT], F32))
    CUMK = [('cums', h) for h in range(4)]
    cvf = cvt[:].rearrange("p a b -> p (a b)")
    xstage = xstage + [cvf[:, 0:D], cvf[:, D:2 * D]]

    def load_norm(x_ap, ntok, dstT, col0key):
        nt = (ntok + 127) // 128
        for t in range(nt):
            nr = min(128, ntok - t * 128)
            s = t % 4
            S.dma('sp', xstage[s][0:nr, :], x_ap[t * 128:t * 128 + nr, :], writes=[('xstage', s)])
            norm_T(xstage[s][0:nr, :], [('xstage', s)], nr, 0, dstT, t * 128, (col0key, t))

    def gates(slot, xT, mts, ntok, xk, chunks):
        mm_fm(slot, 0, xT, mts, 0, ncol=16, xkeys=xk)
        S.op('act', lambda: act.copy(out=g1T[:, 0:ntok], in_=psA[0:16, 0:ntok]),
             reads=pkeys(0, 0, ntok), writes=[('g1T',)])
        for h in range(4):
            g = h % 2

            def zf(h=h, g=g):
                last = None
                for (a, b) in mts:
                    last = pe.matmul(PSG[g][:, a:b], lhsT=wg2[:, h * 128:(h + 1) * 128], rhs=g1T[:, a:b],
                                     start=True, stop=True)
                return last
            S.op('pe', zf, reads=[('g1T',), ('c', 'wg2')], writes=pkeys(g, 0, ntok))
            S.op('act', lambda h=h, g=g: act.activation(out=scr[:, 0, 0:ntok], in_=PSG[g][:, 0:ntok], func=AF.Exp,
                                                        scale=-1.0, bias=negb[:, h:h + 1]),
                 reads=pkeys(g, 0, ntok) + [('c', 'negb')], writes=[('scr', 0)])
            S.op('act', lambda h=h: act.activation(out=scr[:, 1, 0:ntok], in_=scr[:, 0, 0:ntok], func=AF.Ln,
                                                   scale=1.0, bias=1.0),
                 reads=[('scr', 0)], writes=[('scr', 1)])
            for (a, b) in chunks:
                S.op('dve', lambda h=h, a=a, b=b: dve.tensor_tensor_scan(
                    out=cums[:, h, a:b], data0=onesf[:, 0:b - a], data1=scr[:, 1, a:b], initial=0.0,
                    op0=ALU.mult, op1=ALU.add),
                    reads=[('scr', 1), ('c', 'onesf')], writes=[('cums', h)])

    def expL_cols(c_first, stride, n, e0):
        src = cums[:, :, c_first:c_first + stride * (n - 1) + 1:stride]
        S.op('act', lambda: act.activation(out=expL[:, :, e0:e0 + n], in_=src, func=AF.Exp, scale=-1.0 / 16),
             reads=CUMK, writes=[('expL',)])

    def qk_tilde(dst, dkey, sign, scale, ntok, mts, xT, xk):
        for h in range(4):
            if h % 2 == 0:
                sl = next_slab()
            g = h % 2
            mm_fm(sl, h % 2, xT, mts, g, xkeys=xk)
            if h % 2 == 1:
                prefetch()
            S.op('act', lambda h=h: act.activation(out=scr[:, 0, 0:ntok], in_=cums[:, h, 0:ntok], func=AF.Exp,
                                                   scale=sign / 16.0),
                 reads=[('cums', h)], writes=[('scr', 0)])
            S.op('dve', lambda h=h, g=g: dve.scalar_tensor_tensor(
                out=dst[:, h, 0:ntok], in0=PSG[g][:, 0:ntok], scalar=scale, in1=scr[:, 0, 0:ntok],
                op0=ALU.mult, op1=ALU.mult),
                reads=pkeys(g, 0, ntok) + [('scr', 0)], writes=[(dkey, h)])

    def v_tok(ntok, xT, xk):
        nt = (ntok + 127) // 128
        for q4 in range(4):
            sl = next_slab()
            for t in range(nt):
                nr = min(128, ntok - t * 128)
                g = t % 2
                half = (t // 2) % 2

                def fn(t=t, nr=nr, g=g, half=half, sl=sl):
                    last = None
                    for kc in range(KC):
                        last = pe.matmul(PSG[g][0:nr, half * 512:half * 512 + 256],
                                         lhsT=xT[:, kc, t * 128:t * 128 + nr], rhs=WS[sl][:, kc, 0:256],
                                         start=(kc == 0), stop=(kc == KC - 1))
                    return last
                S.op('pe', fn, reads=[('ws', sl), xk[t]], writes=[('ps', g, half)])
                S.op('act', lambda t=t, nr=nr, g=g, half=half, q4=q4: act.copy(
                    out=vt[0:nr, t, q4 * 256:(q4 + 1) * 256], in_=PSG[g][0:nr, half * 512:half * 512 + 256]),
                    reads=[('ps', g, half)], writes=[('vt', t)] + CUMK)
            prefetch()

    kh = scr[:, 2, :].bitcast(BF16)
    khT = kh[:, 0:512].rearrange("p (h t) -> p h t", h=4)
    khat = kh[:, 512:1024].rearrange("p (h d) -> p h d", h=4)
    Am = kh[:, 1024:1536].rearrange("p (h t) -> p h t", h=4)
    pvD = psD[:].bitcast(BF16)

    def khat_T(n):
        def tr():
            last = None
            for h in range(4):
                last = pe.transpose(out=pvD[0:n, h * 128:(h + 1) * 128], in_=khT[:, h, 0:n], identity=identb[:, :])
            return last
        S.op('pe', tr, reads=[('khT',), ('c', 'identb')], writes=[('ps', 'D', 0)])
        S.op('act', lambda: act.copy(out=khat[0:n].rearrange("p h d -> p (h d)"), in_=pvD[0:n, 0:512]),
             reads=[('ps', 'D', 0)], writes=[('khat',)])

    def state_khat(c0, n, eidx):
        eb = expL[:, :, eidx:eidx + 1].broadcast_to([128, 4, n])
        S.op('dve', lambda: dve.tensor_tensor(out=khT[:, :, 0:n], in0=kt[:, :, c0:c0 + n], in1=eb, op=ALU.mult),
             reads=[('kt', h) for h in range(4)] + [('expL',)], writes=[('khT',)])
        khat_T(n)

    def state_update(lhs, lkey, n, vti, eidx, Sf, skey, pst, pstk):
        if not isinstance(pst, list):
            pst = [pst[:, h * 256:(h + 1) * 256] for h in range(4)]

        def ds():
            last = None
            for h in range(4):
                last = pe.matmul(pst[h], lhsT=lhs[0:n, h, :],
                                 rhs=vt[0:n, vti, h * 256:(h + 1) * 256], start=True, stop=True)
            return last
        S.op('pe', ds, reads=[lkey, ('vt', vti)], writes=pstk)
        for h in range(4):
            hk_ = skey + (h,)
            S.op('dve', lambda h=h: dve.scalar_tensor_tensor(
                out=Sf[:, h, :], in0=Sf[:, h, :], scalar=expL[:, h, eidx:eidx + 1], in1=pst[h],
                op0=ALU.mult, op1=ALU.add),
                reads=[hk_, ('expL',)] + pstk, writes=[hk_])

    PSB01 = [('ps', 1, 0), ('ps', 1, 1)]
    PSA01 = [('ps', 0, 0), ('ps', 0, 1)]

    PRE_CH = [(i * 128, min(NPRE, i * 128 + 128)) for i in range(8)]
    load_norm(x_pre, NPRE, RA, 'xT')
    xk_pre = [('xT', t) for t in range(8)]
    gates(next_slab(), RA, MTP, NPRE, xk_pre, PRE_CH)
    prefetch()
    qk_tilde(kt, 'kt', +1.0, 1.0, NPRE, MTP, RA, xk_pre)
    expL_cols(127, 128, 7, 0)
    expL_cols(NPRE - 1, 1, 1, 7)
    v_tok(NPRE, RA, xk_pre)
    def load_norm_tile(t):
        s_ = t % 4
        S.dma('sp', xstage[s_][:, :], x_all[t * 128:(t + 1) * 128, :], writes=[('xstage', s_)])
        norm_T(xstage[s_][:, :], [('xstage', s_)], 128, 0, RA, t * 128, ('xT', t))

    for ci, (a, b) in enumerate(PRE_CH):
        state_khat(a, b - a, ci)
        state_update(khat, ('khat',), b - a, ci, ci, Sst, ('Sst',), psB, PSB01)
        if stop >= 1:
            load_norm_tile(ci)
    S.op('act', lambda: act.copy(out=Sbb[0], in_=Sst[:]), reads=[('Sst', h_) for h_ in range(4)], writes=[('Sbb', 0)])
    if dbg == 'pre':
        S.dma('sp', dbg_d[:, 0:1024], Sst[:].rearrange("p h v -> p (h v)"), reads=[('Sst', h_) for h_ in range(4)])

    if stop >= 1:
        load_norm_tile(8)
        S.fence()
        xk = [('xT', t) for t in range(9)]
        CH = [(4 * s_, 4 * s_ + 4) for s_ in range(16)] + [(64, 128)] + [(128 * i, 128 * i + 128) for i in range(1, 9)]
        g1slot = next_slab()
        mm_fm(g1slot, 0, RA, MT, 0, ncol=16, xkeys=xk)
        S.op('act', lambda: act.copy(out=g1T[:, 0:NT], in_=psA[0:16, 0:NT]), reads=pkeys(0, 0, NT), writes=[('g1T',)])

        def gates_head(h):
            g = h % 2

            def zf(h=h, g=g):
                last = None
                for (a, b) in MT:
                    last = pe.matmul(PSG[g][:, a:b], lhsT=wg2[:, h * 128:(h + 1) * 128], rhs=g1T[:, a:b], start=True, stop=True)
                return last
            S.op('pe', zf, reads=[('g1T',), ('c', 'wg2')], writes=pkeys(g, 0, NT))
            S.op('act', lambda h=h, g=g: act.activation(out=scr[:, 2, :], in_=PSG[g][:, 0:NT], func=AF.Exp, scale=-1.0,
                                                        bias=negb[:, h:h + 1]),
                 reads=pkeys(g, 0, NT) + [('c', 'negb')], writes=[('scr', 2), ('khT',), ('khat',), ('Am',)])
            S.op('act', lambda: act.activation(out=scr[:, 2, :], in_=scr[:, 2, :], func=AF.Ln, scale=1.0, bias=1.0),
                 reads=[('scr', 2)], writes=[('scr', 2)])
            for (a, b) in CH:
                S.op('dve', lambda h=h, a=a, b=b: dve.tensor_tensor_scan(
                    out=cums[:, h, a:b], data0=onesf[:, 0:b - a], data1=scr[:, 2, a:b], initial=0.0,
                    op0=ALU.mult, op1=ALU.add),
                    reads=[('scr', 2), ('c', 'onesf')], writes=[('cums', h)])

        for j2 in range(4):
            if j2 > 0:
                gates_head(j2 - 1)
            sl = next_slab()
            for j in range(2):
                g = j
                mm_fm(sl, j, RA, MT, g, xkeys=xk)
                if j == 1:
                    prefetch()
                S.op('act', lambda g=g: act.activation(out=scr[:, g, :], in_=PSG[g][:, 0:NT], func=AF.Silu),
                     reads=pkeys(g, 0, NT), writes=[('scr', g)])
                S.op('dve', lambda g=g, jj=j2 * 2 + j: dve.tensor_scalar(out=RB[:, 8 + jj, :], in0=scr[:, g, :],
                                                                         scalar1=glan[:, jj:jj + 1], scalar2=None, op0=ALU.mult),
                     reads=[('scr', g), ('c', 'glan')], writes=[('aT', 8 + j2 * 2 + j)])
        gates_head(3)
        qk_tilde(kt, 'kt', +1.0, 1.0, NT, MT, RA, xk)
        qk_tilde(qt, 'qt', -1.0, 128 ** -0.5, NT, MT, RA, xk)
        expL_cols(255, 128, 8, 0)
        expL_cols(127, 1, 1, 8)
        expL_cols(3, 4, 16, 9)
        v_tok(NT, RA, xk)
        S.fence()

        xsf = xstage_t[:].rearrange("p a b -> p (a b)")
        s0b = [xsf[:, i * 1024:(i + 1) * 1024].rearrange("p (h v) -> p h v", h=4) for i in range(4)]
        sbs = [sbs_t[:, i, :].rearrange("p (h v) -> p h v", h=4) for i in range(4)]
        khm = [khm_t[:, i, :].rearrange("p (h d) -> p h d", h=4) for i in range(2)]
        osb = scr[:, 0, 0:1024]
        sqb = scr[:, 1, :].bitcast(BF16)[:, 0:1024]
        rstd = scr[:, 1, 512:1024]

        def gla_post(c0, n, extra=None):
            o3 = osb.rearrange("p (b t) -> p b t", b=8)
            o4 = osb.rearrange("p (h c t) -> p h c t", h=4, c=2)
            if extra is not None:
                S.op('act', lambda: act.copy(out=osb, in_=psB[:, 0:1024]), reads=PSB01, writes=[('scr', 0)])
                S.op('dve', lambda: dve.tensor_tensor(out=osb, in0=extra, in1=osb, op=ALU.add),
                     reads=[('scr', 0)] + PSA01, writes=[('scr', 0)])
                S.op('act', lambda: act.activation(out=sqb, in_=osb, func=AF.Square), reads=[('scr', 0)],
                     writes=[('sqb',)])
            else:
                S.op('act', lambda: act.activation(out=sqb, in_=psB[:, 0:1024], func=AF.Square), reads=PSB01,
                     writes=[('sqb',)])

            def msf():
                last = None
                for h in range(4):
                    for vc in range(2):
                        last = pe.matmul(psC[:, h * 128:(h + 1) * 128], lhsT=onesb[:, :],
                                         rhs=sqb[:, (h * 2 + vc) * 128:(h * 2 + vc + 1) * 128],
                                         start=(vc == 0), stop=(vc == 1))
                return last
            S.op('pe', msf, reads=[('sqb',), ('c', 'onesb')], writes=[('ps', 'C', 0)])
            S.op('act', lambda: act.activation(out=rstd, in_=psC[:, :], func=AF.Sqrt, scale=1.0 / 256, bias=EPS),
                 reads=[('ps', 'C', 0)], writes=[('rstd',)])
            S.op('dve', lambda: dve.reciprocal(out=rstd, in_=rstd), reads=[('rstd',)], writes=[('rstd',)])
            r4 = rstd.rearrange("p (h t) -> p h t", h=4).unsqueeze(2).broadcast_to([128, 4, 2, 128])
            if extra is not None:
                S.op('dve', lambda: dve.tensor_tensor(out=o4, in0=o4, in1=r4, op=ALU.mult),
                     reads=[('scr', 0), ('rstd',)], writes=[('scr', 0)])
            else:
                p4 = psB[:, 0:1024].rearrange("p (h c t) -> p h c t", h=4, c=2)
                S.op('dve', lambda: dve.tensor_tensor(out=o4, in0=p4, in1=r4, op=ALU.mult),
                     reads=PSB01 + [('rstd',)], writes=[('scr', 0)])
            S.op('dve', lambda: dve.tensor_tensor(out=RB[:, 8:16, c0:c0 + n], in0=o3[:, :, 0:n], in1=RB[:, 8:16, c0:c0 + n],
                                                  op=ALU.mult),
                 reads=[('scr', 0)] + [('aT', 8 + j) for j in range(8)], writes=[('aT', 8 + j) for j in range(8)])

        def gla_AT(c0, mask, mkey):
            def af():
                last = None
                for h in range(4):
                    last = pe.matmul(psC[:, h * 128:(h + 1) * 128], lhsT=kt[:, h, c0:c0 + 128], rhs=qt[:, h, c0:c0 + 128],
                                     start=True, stop=True)
                return last
            S.op('pe', af, reads=[('kt', h) for h in range(4)] + [('qt', h) for h in range(4)], writes=[('ps', 'C', 0)])
            mb = mask[:, :].unsqueeze(1).broadcast_to([128, 4, 128])
            S.op('dve', lambda: dve.tensor_tensor(out=Am, in0=psC[:, :].rearrange("p (h t) -> p h t", h=4), in1=mb, op=ALU.mult),
                 reads=[('ps', 'C', 0), ('c', mkey)], writes=[('Am',)])

        gla_AT(0, mask0, 'mask0')
        for h in range(4):
            e3 = expL[:, h, 9:25].unsqueeze(2).broadcast_to([128, 16, 4])
            S.op('dve', lambda h=h, e3=e3: dve.tensor_tensor(out=khT[:, h, 0:64].rearrange("p (s t) -> p s t", s=16),
                                                             in0=kt[:, h, 0:64].rearrange("p (s t) -> p s t", s=16),
                                                             in1=e3, op=ALU.mult),
                 reads=[('kt', h), ('expL',)], writes=[('khT',)])
            S.op('dve', lambda h=h: dve.tensor_scalar(out=khT[:, h, 64:128], in0=kt[:, h, 64:128],
                                                      scalar1=expL[:, h, 8:9], scalar2=None, op0=ALU.mult),
                 reads=[('kt', h), ('expL',)], writes=[('khT',)])
        khat_T(128)

        T0D = [psA[:, 1024:1280], psA[:, 1280:1536], psB[:, 1024:1280], psB[:, 1280:1536]]
        T0DK = [('ps', 0, 2), ('ps', 1, 2)]

        def intra0():
            last = None
            for b_ in range(8):
                last = pe.matmul(psB[:, b_ * 128:(b_ + 1) * 128], lhsT=vt[:, 0, b_ * 128:(b_ + 1) * 128], rhs=Am[:, b_ // 2, :],
                                 start=True, stop=True)
            for b_ in range(8):
                last = pe.matmul(psA[:, b_ * 128 + 64:(b_ + 1) * 128], lhsT=Sbb[0][:, b_ // 2, (b_ % 2) * 128:(b_ % 2 + 1) * 128],
                                 rhs=qt[:, b_ // 2, 64:128], start=True, stop=True)
            return last
        S.op('pe', intra0, reads=[('vt', 0), ('Am',), ('Sbb', 0)] + [('qt', h) for h in range(4)], writes=PSB01 + PSA01)
        def s0_load(q_):
            S.dma('sp', s0b[q_ % 4], sgla_d[q_].rearrange("h d v -> d h v"), writes=[('s0b', q_ % 4, h) for h in range(4)])
        for q_ in range(3):
            s0_load(q_)
        for s_ in range(NSEQ):
            sl = s_ % 4
            k2 = s_ % 2
            S.op('act', lambda sl=sl: act.copy(out=sbs[sl], in_=s0b[sl]), reads=[('s0b', sl, h) for h in range(4)], writes=[('sbs', sl)])

            def inter(s_=s_, sl=sl):
                last = None
                for b_ in range(8):
                    last = pe.matmul(psA[:, b_ * 128 + 4 * s_:b_ * 128 + 4 * s_ + 4],
                                     lhsT=sbs[sl][:, b_ // 2, (b_ % 2) * 128:(b_ % 2 + 1) * 128],
                                     rhs=qt[:, b_ // 2, 4 * s_:4 * s_ + 4], start=True, stop=True)
                return last
            S.op('pe', inter, reads=[('sbs', sl)] + [('qt', h) for h in range(4)], writes=PSA01)
            S.op('dve', lambda s_=s_, k2=k2: dve.tensor_scalar(
                out=khm[k2].rearrange("p h d -> p (h d)"), in0=khat.rearrange("p h d -> p (h d)"),
                scalar1=selm[:, s_:s_ + 1], scalar2=None, op0=ALU.mult),
                reads=[('khat',), ('c', 'selm')], writes=[('khm', k2)])
            state_update(khm[k2], ('khm', k2), 128, 0, 9 + s_, s0b[sl], ('s0b', sl), T0D, T0DK)
            S.dma('sp', o_gla_s[s_].rearrange("h d v -> d h v"), s0b[sl], reads=[('s0b', sl, h) for h in range(4)])
            if s_ + 3 < NSEQ:
                s0_load(s_ + 3)
        S.op('dve', lambda: dve.tensor_scalar(
            out=khm[0].rearrange("p h d -> p (h d)"), in0=khat.rearrange("p h d -> p (h d)"),
            scalar1=selm[:, 16:17], scalar2=None, op0=ALU.mult),
            reads=[('khat',), ('c', 'selm')], writes=[('khm', 0)])
        gla_post(0, 128, extra=psA[:, 0:1024])
        state_update(khm[0], ('khm', 0), 128, 0, 8, Sst, ('Sst',), T0D, T0DK)
        S.op('act', lambda: act.copy(out=Sbb[1], in_=Sst[:]), reads=[('Sst', h_) for h_ in range(4)], writes=[('Sbb', 1)])
        sb_cur[0] = 1

        cgs = [cvt[:, 0, :], cvt[:, 1, :]]
        us = [cvt[:, 2, :], cvt[:, 3, :]]
        ycs = cgs
        rowbuf = stgF[0:34, 0:1024]
        pcc = stgF[:, 1024:1280].rearrange("p (j r) -> p j r", j=8)
        utl = stgF[:, 1280:1280 + 8 * 34].rearrange("p (j r) -> p j r", j=8)
        S.dma('sp', rowbuf[0:32, :], cconv_d, writes=[('rowbuf',)])

        def trc():
            last = None
            for j in range(8):
                last = pe.transpose(out=psC[:, j * 32:(j + 1) * 32], in_=rowbuf[0:32, j * 128:(j + 1) * 128],
                                    identity=identf[0:32, 0:32])
            return last
        S.op('pe', trc, reads=[('rowbuf',), ('c', 'identf')], writes=[('ps', 'C', 0)])
        S.op('act', lambda: act.copy(out=pcc.rearrange("p j r -> p (j r)"), in_=psC[:, 0:256]),
             reads=[('ps', 'C', 0)], writes=[('pcc',)])

        def per_bank(eng, mk, reads, writes):
            for bi, (a, b) in enumerate(MT):
                S.op(eng, lambda a=a, b=b: mk(a, b), reads=[('ps', 0, bi)] + reads, writes=writes)

        def conv_gen():
            for j2 in range(4):
                sl = next_slab()
                for j in range(2):
                    mm_fm(sl, j, RA, MT, 0, xkeys=xk)
                    per_bank('act', lambda a, b, j=j: act.copy(out=cgs[j][:, a:b], in_=psA[:, a:b]), [], [('cgs', j)])
                    if j == 1:
                        prefetch()
                    yield
                sl = next_slab()
                for j in range(2):
                    jj = j2 * 2 + j
                    mm_fm(sl, j, RA, MT, 0, xkeys=xk)
                    per_bank('dve', lambda a, b, j=j: dve.tensor_tensor(out=us[j][:, a:b], in0=psA[:, a:b], in1=cgs[j][:, a:b], op=ALU.mult),
                             [('cgs', j)], [('us', j)])
                    if j == 1:
                        prefetch()
                    S.op('act', lambda j=j, jj=jj: act.copy(out=utl[:, jj, 0:32].rearrange("p (s r) -> p s r", s=16),
                                                            in_=us[j][:, 0:64].rearrange("p (s t) -> p s t", s=16)[:, :, 2:4]),
                         reads=[('us', j)], writes=[('utl',)])
                    S.op('act', lambda j=j, jj=jj: act.copy(out=utl[:, jj, 32:34], in_=us[j][:, NT - 2:NT]),
                         reads=[('us', j)], writes=[('utl',)])
                    yc = ycs[j]
                    S.op('pool', lambda j=j, jj=jj, yc=yc: pool.tensor_scalar(out=yc, in0=us[j], scalar1=convw[:, 2, jj:jj + 1],
                                                                              scalar2=0.0, op0=ALU.mult, op1=ALU.add),
                         reads=[('us', j), ('c', 'convw')], writes=[('cgs', j)])
                    y3 = yc[:, 0:64].rearrange("p (s t) -> p s t", s=16)
                    u3 = us[j][:, 0:64].rearrange("p (s t) -> p s t", s=16)
                    p3 = pcc[:, jj, :].rearrange("p (s r) -> p s r", s=16)
                    for (tap, sh) in ((1, 1), (0, 2)):
                        S.op('dve', lambda j=j, jj=jj, yc=yc, tap=tap, sh=sh: dve.scalar_tensor_tensor(
                            out=yc[:, 64 + sh:NT], in0=us[j][:, 64:NT - sh], scalar=convw[:, tap, jj:jj + 1], in1=yc[:, 64 + sh:NT],
                            op0=ALU.mult, op1=ALU.add),
                            reads=[('us', j), ('c', 'convw')], writes=[('cgs', j)])
                        S.op('dve', lambda j=j, jj=jj, y3=y3, u3=u3, tap=tap, sh=sh: dve.scalar_tensor_tensor(
                            out=y3[:, :, sh:4], in0=u3[:, :, 0:4 - sh], scalar=convw[:, tap, jj:jj + 1], in1=y3[:, :, sh:4],
                            op0=ALU.mult, op1=ALU.add),
                            reads=[('us', j), ('c', 'convw')], writes=[('cgs', j)])
                    S.op('dve', lambda jj=jj, y3=y3, p3=p3: dve.scalar_tensor_tensor(
                        out=y3[:, :, 0:1], in0=p3[:, :, 1:2], scalar=convw[:, 1, jj:jj + 1], in1=y3[:, :, 0:1],
                        op0=ALU.mult, op1=ALU.add),
                        reads=[('pcc',), ('c', 'convw')], writes=[('cgs', j)])
                    S.op('dve', lambda jj=jj, y3=y3, p3=p3: dve.scalar_tensor_tensor(
                        out=y3[:, :, 0:2], in0=p3[:, :, 0:2], scalar=convw[:, 0, jj:jj + 1], in1=y3[:, :, 0:2],
                        op0=ALU.mult, op1=ALU.add),
                        reads=[('pcc',), ('c', 'convw')], writes=[('cgs', j)])
                    yield
                sl = next_slab()
                for j in range(2):
                    jj = j2 * 2 + j
                    mm_fm(sl, j, RA, MT, 0, xkeys=xk)
                    per_bank('dve', lambda a, b, j=j, jj=jj: dve.tensor_tensor(out=RB[:, jj, a:b], in0=psA[:, a:b], in1=ycs[j][:, a:b], op=ALU.mult),
                             [('cgs', j)], [('aT', jj)])
                    if j == 1:
                        prefetch()
                    yield

        cg_it = conv_gen()

        def filler(n=1):
            for _ in range(n):
                try:
                    next(cg_it)
                except StopIteration:
                    return

        MCD = [psB[:, 1024:1280], psB[:, 1280:1536], psD[:, 0:256], psD[:, 256:512]]
        MCDK = [('ps', 1, 2), ('ps', 'D', 0)]
        for ci in range(1, 9):
            c0 = 128 * ci
            cur = sb_cur[0]
            gla_AT(c0, maskC, 'maskC')
            state_khat(c0, 128, ci - 1)
            filler()
            state_update(khat, ('khat',), 128, ci, ci - 1, Sst, ('Sst',), MCD, MCDK)
            S.op('act', lambda cur=cur: act.copy(out=Sbb[1 - cur], in_=Sst[:]), reads=[('Sst', h_) for h_ in range(4)], writes=[('Sbb', 1 - cur)])

            def of(ci=ci, c0=c0, cur=cur):
                last = None
                for b_ in range(8):
                    pe.matmul(psB[:, b_ * 128:(b_ + 1) * 128], lhsT=vt[:, ci, b_ * 128:(b_ + 1) * 128], rhs=Am[:, b_ // 2, :],
                              start=True, stop=False)
                    last = pe.matmul(psB[:, b_ * 128:(b_ + 1) * 128], lhsT=Sbb[cur][:, b_ // 2, (b_ % 2) * 128:(b_ % 2 + 1) * 128],
                                     rhs=qt[:, b_ // 2, c0:c0 + 128], start=False, stop=True)
                return last
            S.op('pe', of, reads=[('vt', ci), ('Am',), ('Sbb', cur)] + [('qt', h) for h in range(4)], writes=PSB01)
            filler()
            gla_post(c0, 128)
            filler()
            sb_cur[0] = 1 - cur
        S.dma('sp', o_gla_p.rearrange("h d v -> d h v"), Sst[:], reads=[('Sst', h_) for h_ in range(4)])
        filler(100)

        def tru():
            last = None
            for j in range(8):
                pt = psC if j < 4 else psD
                last = pe.transpose(out=pt[0:34, (j % 4) * 128:(j % 4 + 1) * 128], in_=utl[:, j, :], identity=identf[:, :])
            return last
        S.op('pe', tru, reads=[('utl',), ('c', 'identf')], writes=[('ps', 'C', 0), ('ps', 'D', 0)])
        S.op('act', lambda: act.copy(out=rowbuf[0:34, 0:512], in_=psC[0:34, 0:512]), reads=[('ps', 'C', 0)], writes=[('rowbuf',)])
        S.op('act', lambda: act.copy(out=rowbuf[0:34, 512:1024], in_=psD[0:34, 0:512]), reads=[('ps', 'D', 0)], writes=[('rowbuf',)])
        S.dma('sp', o_conv_s, rowbuf[0:32, :], reads=[('rowbuf',)])
        S.dma('sp', o_conv_p, rowbuf[32:34, :], reads=[('rowbuf',)])
        if dbg == 'p1':
            for c in range(16):
                S.dma('pool', dbg_d[:, c * NT:(c + 1) * NT], RB[:, c, :], reads=[('aT', c)])

    norm_alt[0] = False
    resident.clear()
    prefetch()
    S.fence()
    es_p1.close()
    RH = sb("RH", [128, 9, D], F32)
    ws23 = sb("ws23", [128, 2, 16 * 256], BF16)
    WS.append(ws23[:, 0, :].rearrange("p (k n) -> p k n", k=16))
    WS.append(ws23[:, 1, :].rearrange("p (k n) -> p k n", k=16))
    WW.append(ws23[:].rearrange("p a x -> p (a x)").rearrange("p (k n) -> p k n", k=16))
    WSF.append(ws23[:, 0, :])
    WSF.append(ws23[:, 1, :])
    xs2_t = sb("xs2", [128, 2, D], BF16)
    del xs_bufs[2:]
    xs_bufs[0] = xs2_t[:, 0, :]
    xs_bufs[1] = xs2_t[:, 1, :]
    HK = [('h', t) for t in range(9)]
    AK = [('aT', c) for c in range(16)]
    xk = [('xT', t) for t in range(9)]
    tm_slots = [(0, 0), (1, 0), (0, 1), (1, 1), (0, 2), (1, 2)]
    tm_ctr = [0]

    norm_pending = [None]

    def norm_cb_flush():
        if norm_pending[0] is not None:
            norm_pending[0]()
            norm_pending[0] = None

    def make_norm_cb(gi):
        def cb(t):
            norm_cb_flush()
            norm_pending[0] = norm_T(RH[:, t, :], [('h', t)], 128, gi, RA, t * 128, ('xT', t), defer=True)
        return cb

    def _wview(sl, nk):
        if nk * 512 > 4096:
            return WW[sl], [('ws', 2 * sl), ('ws', 2 * sl + 1)]
        return WSF[sl][:, 0:nk * 512].rearrange("p (k n) -> p k n", k=nk), [('ws', sl)]

    def lin_tm_residual(nslab, nk, aT, akeys, defer_last=False, after_tile=None):
        n_plain = 4 if after_tile is None else 2
        for c4 in range(n_plain):
            sl = next_slab()
            wv, wkeys = _wview(sl, nk)
            dl = defer_last and c4 == 0 and nk > 1
            batches = [list(range(0, 6)), list(range(6, 9))] if dl else [[t] for t in range(9)]
            for bt in batches:
                slots = {}
                for t in bt:
                    g, b = tm_slots[tm_ctr[0] % 6]
                    tm_ctr[0] += 1
                    slots[t] = (g, b, PSG[g][:, b * 512:(b + 1) * 512])
                phases = [(0, nk - 1), (nk - 1, nk)] if dl else [(0, nk)]
                for (k0, k1) in phases:
                    for t in bt:
                        g, b, po = slots[t]

                        def fn(t=t, po=po, k0=k0, k1=k1, wv=wv):
                            last = None
                            for kc in range(k0, k1):
                                last = pe.matmul(po, lhsT=aT[:, kc, t * 128:(t + 1) * 128], rhs=wv[:, kc, 0:512],
                                                 start=(kc == 0), stop=(kc == nk - 1))
                            return last
                        S.op('pe', fn, reads=wkeys + akeys[k0:k1], writes=[('ps', g, b)])
                for t in bt:
                    g, b, po = slots[t]
                    S.op('dve', lambda t=t, c4=c4, po=po: dve.tensor_tensor(
                        out=RH[:, t, c4 * 512:(c4 + 1) * 512], in0=po, in1=RH[:, t, c4 * 512:(c4 + 1) * 512], op=ALU.add),
                        reads=[('ps', g, b), ('h', t)], writes=[('h', t)])
            prefetch()
        if after_tile is None:
            return
        slab_hold[0] = slab_state['used']
        sls = [next_slab(), next_slab()]
        views = [_wview(sl, nk) for sl in sls]
        for t in range(9):
            for c4 in (2, 3):
                wv, wkeys = views[c4 - 2]
                g, b = tm_slots[tm_ctr[0] % 6]
                tm_ctr[0] += 1
                po = PSG[g][:, b * 512:(b + 1) * 512]

                def fn(t=t, po=po, wv=wv):
                    last = None
                    for kc in range(nk):
                        last = pe.matmul(po, lhsT=aT[:, kc, t * 128:(t + 1) * 128], rhs=wv[:, kc, 0:512],
                                         start=(kc == 0), stop=(kc == nk - 1))
                    return last
                S.op('pe', fn, reads=wkeys + akeys, writes=[('ps', g, b)])
                S.op('dve', lambda t=t, c4=c4, po=po: dve.tensor_tensor(
                    out=RH[:, t, c4 * 512:(c4 + 1) * 512], in0=po, in1=RH[:, t, c4 * 512:(c4 + 1) * 512], op=ALU.add),
                    reads=[('ps', g, b), ('h', t)], writes=[('h', t)])
            after_tile(t)
        slab_hold[0] = None
        prefetch()

    if stop >= 2:
        for t in range(9):
            S.dma('sp', RH[:, t, :], x_all[t * 128:(t + 1) * 128, :], writes=[('h', t)])
        lin_tm_residual(4, KC, RB, AK, after_tile=make_norm_cb(1) if stop >= 3 else None)
        norm_cb_flush()
        if dbg == 'p2':
            for t in range(9):
                S.dma('sp', dbg_d[:, t * D:(t + 1) * D], RH[:, t, :], reads=[('h', t)])

    def lin_fm_to(dst, dkeys, xT, xkeys, func=None):
        for c8 in range(8):
            sl = next_slab()
            for j in range(2):
                c = c8 * 2 + j
                mm_fm(sl, j, xT, MT, j, xkeys=xkeys)
                S.op('act', lambda j=j, c=c: act.copy(out=dst[:, c, :], in_=PSG[j][:, 0:NT]),
                     reads=pkeys(j, 0, NT), writes=[dkeys[c]])
            prefetch()

    if stop >= 3:
        lin_fm_to(RB, AK, RA, xk)
        S.fence()
        if dbg == 'p3':
            for c in range(16):
                S.dma('pool', dbg_d[:, c * NT:(c + 1) * NT], RB[:, c, :], reads=[('aT', c)])

    if stop >= 4:
        RAf = RA[:].rearrange("p a b -> p (a b)")
        mstage = stgF[:, :]
        memnT = RAf[:, 4096:8192].rearrange("p (c m) -> p c m", c=16)
        mkT = RAf[:, 8192:12288].rearrange("p (c m) -> p c m", c=16)
        mvb = RAf[:, 12288:16384].rearrange("p (m f) -> p m f", m=2)
        xsm = RAf[:, 16384:18432]
        for mc in range(2):
            S.dma('sp', mstage, mem[mc * 128:(mc + 1) * 128, :], writes=[('mstage',)])
            norm_T(mstage, [('mstage',)], 128, 2, memnT, mc * 128, ('memnT', mc), xsb=xsm)
        S.fence()
        ostg = [stgF[:, 0:512], stgF[:, 512:1024]]
        mkt = [RAf[:, 0:512], RAf[:, 512:1024]]
        kv_ctr = [0]
        for (w_i, o_d) in ((0, o_mk), (1, o_mv)):
            for c4 in range(4):
                sl = next_slab()
                for mc in range(2):
                    i = kv_ctr[0]
                    kv_ctr[0] += 1
                    g, b = tm_slots[i % 6]
                    po = PSG[g][:, b * 512:(b + 1) * 512]
                    st_ = i % 2

                    def fn(mc=mc, sl=sl, po=po):
                        last = None
                        for kc in range(KC):
                            last = pe.matmul(po, lhsT=memnT[:, kc, mc * 128:(mc + 1) * 128], rhs=WW[sl][:, kc, 0:512],
                                             start=(kc == 0), stop=(kc == KC - 1))
                        return last
                    S.op('pe', fn, reads=[('ws', 2 * sl), ('ws', 2 * sl + 1), ('memnT', 0), ('memnT', 1)], writes=[('ps', g, b)])
                    S.op('act', lambda po=po, st_=st_: act.copy(out=ostg[st_], in_=po), reads=[('ps', g, b)],
                         writes=[('ostg', st_)])
                    S.dma('sp', o_d[mc * 128:(mc + 1) * 128, c4 * 512:(c4 + 1) * 512], ostg[st_], reads=[('ostg', st_)])
                    if w_i == 1:
                        S.op('dve', lambda po=po, mc=mc, c4=c4: dve.tensor_scalar(out=mvb[:, mc, c4 * 512:(c4 + 1) * 512], in0=po,
                                                                                  scalar1=1.0, scalar2=None, op0=ALU.mult),
                             reads=[('ps', g, b)], writes=[('mvb',)])
                    else:
                        S.op('dve', lambda po=po, st_=st_: dve.tensor_scalar(out=mkt[st_], in0=po, scalar1=1.0, scalar2=None, op0=ALU.mult),
                             reads=[('ps', g, b)], writes=[('mkt', st_)])

                        def trk(st_=st_):
                            last = None
                            for j in range(4):
                                last = pe.transpose(out=pvD[:, j * 128:(j + 1) * 128], in_=mkt[st_][:, j * 128:(j + 1) * 128],
                                                    identity=identb[:, :])
                            return last
                        S.op('pe', trk, reads=[('mkt', st_), ('c', 'identb')], writes=[('ps', 'D', 0)])
                        S.op('act', lambda mc=mc, c4=c4: act.copy(
                            out=mkT[:, c4 * 4:c4 * 4 + 4, mc * 128:(mc + 1) * 128],
                            in_=pvD[:, 0:512].rearrange("p (j m) -> p j m", j=4)),
                            reads=[('ps', 'D', 0)], writes=[('mkT',)])
                prefetch()

        qs = RAf[:, 1536:2560].rearrange("p (c t) -> p c t", c=16)
        S.op('dve', lambda: dve.tensor_scalar(out=qs, in0=RB[:, :, 0:64], scalar1=1.0, scalar2=None, op0=ALU.mult), reads=AK, writes=[('qs',)])
        pTs2 = [RAf[:, 2560:3584].rearrange("p (m t) -> p m t", m=2), RAf[:, 0:1024].rearrange("p (m t) -> p m t", m=2)]
        rden = RAf[:, 3584:4608].bitcast(F32)
        SC = 512 ** -0.5
        st_sets = [[(psA[:, 0:512], ('ps', 0, 0)), (psA[:, 512:1024], ('ps', 0, 1))],
                   [(psA[:, 1024:1536], ('ps', 0, 2)), (psC[:, :], ('ps', 'C', 0))]]
        o_slots = [(psB, 0, ('ps', 1, 0)), (psB, 512, ('ps', 1, 1)), (psB, 1024, ('ps', 1, 2))]
        steps = [(h, a, b) for h in range(4) for (a, b) in MT]
        o_ctr = [0]

        def emit_sf(i):
            h, a, b = steps[i]
            n = b - a
            pT = pTs2[i % 2]
            for mc in range(2):
                pa, pk = st_sets[i % 2][mc]

                def sf(h=h, a=a, b=b, n=n, mc=mc, pa=pa):
                    last = None
                    for dc in range(4):
                        last = pe.matmul(pa[:, 0:n], lhsT=mkT[:, 4 * h + dc, mc * 128:(mc + 1) * 128],
                                         rhs=RB[:, 4 * h + dc, a:b], start=(dc == 0), stop=(dc == 3))
                    return last
                S.op('pe', sf, reads=[('mkT',)] + [('aT', 4 * h + dc) for dc in range(4)], writes=[pk])
            for mc in range(2):
                pa, pk = st_sets[i % 2][mc]
                S.op('act', lambda n=n, mc=mc, pa=pa, pT=pT: act.activation(out=pT[:, mc, 0:n], in_=pa[:, 0:n], func=AF.Exp, scale=SC),
                     reads=[pk], writes=[('pT', i % 2, mc)])

        def emit_rest(i):
            h, a, b = steps[i]
            n = b - a
            pT = pTs2[i % 2]
            ptk = [('pT', i % 2, 0), ('pT', i % 2, 1)]

            def df(n=n, pT=pT):
                last = None
                for mc in range(2):
                    last = pe.matmul(psD[:, 0:n], lhsT=onesb[:, :], rhs=pT[:, mc, 0:n], start=(mc == 0), stop=(mc == 1))
                return last
            S.op('pe', df, reads=ptk + [('c', 'onesb')], writes=[('ps', 'D', 0)])
            S.op('dve', lambda n=n: dve.reciprocal(out=rden[:, 0:n], in_=psD[:, 0:n]), reads=[('ps', 'D', 0)], writes=[('rden',)])
            for dc in range(4):
                pt_, off, pk = o_slots[o_ctr[0] % 3]
                o_ctr[0] += 1

                def of(h=h, dc=dc, n=n, pt_=pt_, off=off, pT=pT):
                    last = None
                    for mc in range(2):
                        last = pe.matmul(pt_[:, off:off + n], lhsT=mvb[:, mc, (4 * h + dc) * 128:(4 * h + dc + 1) * 128],
                                         rhs=pT[:, mc, 0:n], start=(mc == 0), stop=(mc == 1))
                    return last
                S.op('pe', of, reads=ptk + [('mvb',)], writes=[pk])
                S.op('dve', lambda h=h, dc=dc, a=a, b=b, n=n, pt_=pt_, off=off: dve.tensor_tensor(
                    out=RB[:, 4 * h + dc, a:b], in0=pt_[:, off:off + n], in1=rden[:, 0:n], op=ALU.mult),
                    reads=[pk, ('rden',)], writes=[('aT', 4 * h + dc)])

        emit_sf(0)
        for i in range(len(steps)):
            if i + 1 < len(steps):
                emit_sf(i + 1)
            emit_rest(i)
        S.fence()

        NB = 4
        kvb = RAf[:, 4608:4608 + NB * 2048].rearrange("p (n x) -> p n x", n=NB)
        kTb = [RAf[:, 12800 + i * 1024:12800 + (i + 1) * 1024].rearrange("p (c m) -> p c m", c=4) for i in range(2)]
        pTs = RAf[:, 14848:14848 + 16].rearrange("p (i m t) -> p i m t", i=2, m=2)
        rds = RAf[:, 14880:14896].bitcast(F32).rearrange("p (i t) -> p i t", i=2)
        it = 0
        for s_ in range(NSEQ):
            for h in range(4):
                sl = it % NB
                i2 = it % 2
                it += 1
                kb = kvb[:, sl, 0:1024].rearrange("p (m d) -> p m d", m=2)
                vb = kvb[:, sl, 1024:2048].rearrange("p (m d) -> p m d", m=2)
                S.dma('pool', kb, cmk_d[s_, :, h * 512:(h + 1) * 512].rearrange("(m p) d -> p m d", p=128),
                      writes=[('kvb', sl, 0)])
                S.dma('pool', vb, cmv_d[s_, :, h * 512:(h + 1) * 512].rearrange("(m p) d -> p m d", p=128),
                      writes=[('kvb', sl, 1)])
                pz = psC if i2 == 0 else psD
                pzk = ('ps', 'C' if i2 == 0 else 'D', 0)
                pzb = pz[:].bitcast(BF16)

                def trf(kb=kb, pzb=pzb):
                    last = None
                    for mc in range(2):
                        for dc in range(4):
                            last = pe.transpose(out=pzb[:, dc * 256 + mc * 128:dc * 256 + (mc + 1) * 128],
                                                in_=kb[:, mc, dc * 128:(dc + 1) * 128], identity=identb[:, :])
                    return last
                S.op('pe', trf, reads=[('kvb', sl, 0), ('c', 'identb')], writes=[pzk])
                S.op('act', lambda i2=i2, pzb=pzb: act.copy(out=kTb[i2].rearrange("p c m -> p (c m)"), in_=pzb[:, 0:1024]),
                     reads=[pzk], writes=[('kTb', i2)])
                pw = PSG[i2]
                pwk = [('ps', i2, 0), ('ps', i2, 1), ('ps', i2, 2)]

                def scf(i2=i2, h=h, s_=s_, pw=pw):
                    last = None
                    for mc in range(2):
                        for dc in range(4):
                            last = pe.matmul(pw[:, mc * 4:mc * 4 + 4], lhsT=kTb[i2][:, dc, mc * 128:(mc + 1) * 128],
                                             rhs=qs[:, 4 * h + dc, 4 * s_:4 * s_ + 4], start=(dc == 0), stop=(dc == 3))
                    return last
                S.op('pe', scf, reads=[('kTb', i2), ('qs',)], writes=[pwk[0]])
                S.op('act', lambda i2=i2, pw=pw: act.activation(out=pTs[:, i2].rearrange("p m t -> p (m t)"), in_=pw[:, 0:8],
                                                                func=AF.Exp, scale=SC),
                     reads=[pwk[0]], writes=[('pTs', i2)])

                def pvf(i2=i2, pw=pw, vb=vb):
                    last = None
                    for mc in range(2):
                        last = pe.matmul(pw[:, 512:516], lhsT=onesb[:, :], rhs=pTs[:, i2, mc, :], start=(mc == 0), stop=(mc == 1))
                    for dc in range(4):
                        for mc in range(2):
                            last = pe.matmul(pw[:, 1024 + dc * 4:1024 + dc * 4 + 4], lhsT=vb[:, mc, dc * 128:(dc + 1) * 128],
                                             rhs=pTs[:, i2, mc, :], start=(mc == 0), stop=(mc == 1))
                    return last
                S.op('pe', pvf, reads=[('pTs', i2), ('kvb', sl, 1), ('c', 'onesb')], writes=[pwk[1], pwk[2]])
                S.op('dve', lambda i2=i2, pw=pw: dve.reciprocal(out=rds[:, i2, :], in_=pw[:, 512:516]),
                     reads=[pwk[1]], writes=[('rds', i2)])
                S.op('dve', lambda i2=i2, pw=pw, h=h, s_=s_: dve.tensor_tensor(
                    out=RB[:, 4 * h:4 * h + 4, 4 * s_:4 * s_ + 4], in0=pw[:, 1024:1040].rearrange("p (c t) -> p c t", c=4),
                    in1=rds[:, i2, :].unsqueeze(1).broadcast_to([128, 4, 4]), op=ALU.mult),
                    reads=[pwk[2], ('rds', i2)], writes=[('aT', 4 * h + dc) for dc in range(4)])
        S.fence()
        if dbg == 'p4':
            for c in range(16):
                S.dma('pool', dbg_d[:, c * NT:(c + 1) * NT], RB[:, c, :], reads=[('aT', c)])

    if stop >= 5:
        lin_tm_residual(4, KC, RB, AK, after_tile=make_norm_cb(3) if stop >= 6 else None)
        norm_cb_flush()
        if dbg == 'p5':
            for t in range(9):
                S.dma('sp', dbg_d[:, t * D:(t + 1) * D], RH[:, t, :], reads=[('h', t)])

    if stop >= 6:
        S.op('dve', lambda: dve.memset(stat[:, 6:7], 0.0), reads=[],
             writes=AK + [('gs',), ('cc',), ('pcg',), ('gtl',)] + [('act', ci) for ci in range(8)])
        RBf = RB[:].rearrange("p a b -> p (a b)")
        actb = RB[:, 0:8, :]
        gs = RBf[:, 8 * NT:10 * NT].bitcast(F32)
        cc = RBf[:, 10 * NT:12 * NT].bitcast(F32)
        misc = RBf[:, 12 * NT:16 * NT].bitcast(F32)
        pcg = misc[:, 0:256].rearrange("p (j r) -> p j r", j=8)
        gtl = misc[:, 256:256 + 272].rearrange("p (j r) -> p j r", j=8)
        rowb = stgF[0:34, 0:1024]
        for (f0, f1) in FGROUPS:
            ng = f1 - f0
            S.dma('sp', rowb[0:32, 0:ng * 128], cffn_d[:, f0 * 128:f1 * 128], writes=[('rowb',)])

            def trc2(ng=ng):
                last = None
                for j in range(ng):
                    last = pe.transpose(out=psC[:, j * 32:(j + 1) * 32], in_=rowb[0:32, j * 128:(j + 1) * 128],
                                        identity=identf[0:32, 0:32])
                return last
            S.op('pe', trc2, reads=[('rowb',), ('c', 'identf')], writes=[('ps', 'C', 0)])
            S.op('act', lambda ng=ng: act.copy(out=pcg.rearrange("p j r -> p (j r)")[:, 0:ng * 32], in_=psC[:, 0:ng * 32]),
                 reads=[('ps', 'C', 0)], writes=[('pcg',)])
            for ci in range(ng):
                fc = f0 + ci
                sl = next_slab()
                mm_fm(sl, 0, RA, MT, 0, xkeys=xk)
                prefetch()
                S.op('act', lambda: act.copy(out=gs, in_=psA[:, 0:NT]), reads=pkeys(0, 0, NT), writes=[('gs',)])
                sl = next_slab()
                mm_fm(sl, 0, RA, MT, 1, xkeys=xk)
                prefetch()
                S.op('act', lambda ci=ci: act.copy(out=gtl[:, ci, 0:32].rearrange("p (s r) -> p s r", s=16),
                                                  in_=gs[:, 0:64].rearrange("p (s t) -> p s t", s=16)[:, :, 2:4]),
                     reads=[('gs',)], writes=[('gtl',)])
                S.op('act', lambda ci=ci: act.copy(out=gtl[:, ci, 32:34], in_=gs[:, NT - 2:NT]), reads=[('gs',)], writes=[('gtl',)])
                S.op('dve', lambda: dve.tensor_scalar(out=gs[:, 64:128], in0=gs[:, 64:128], scalar1=ovm[:, 0:1], scalar2=None,
                                                      op0=ALU.mult), reads=[('gs',), ('c', 'ovm')], writes=[('gs',)])
                S.op('dve', lambda fc=fc: dve.tensor_scalar(out=cc, in0=gs, scalar1=fcw[:, 2, fc:fc + 1], scalar2=fcb[:, fc:fc + 1],
                                                            op0=ALU.mult, op1=ALU.add),
                     reads=[('gs',), ('c', 'fcw'), ('c', 'fcb')], writes=[('cc',)])
                y3 = cc[:, 0:64].rearrange("p (s t) -> p s t", s=16)
                u3 = gs[:, 0:64].rearrange("p (s t) -> p s t", s=16)
                p3 = pcg[:, ci, :].rearrange("p (s r) -> p s r", s=16)
                for (tap, sh) in ((1, 1), (0, 2)):
                    S.op('dve', lambda fc=fc, tap=tap, sh=sh: dve.scalar_tensor_tensor(
                        out=cc[:, 64 + sh:NT], in0=gs[:, 64:NT - sh], scalar=fcw[:, tap, fc:fc + 1], in1=cc[:, 64 + sh:NT],
                        op0=ALU.mult, op1=ALU.add), reads=[('gs',), ('c', 'fcw')], writes=[('cc',)])
                    S.op('dve', lambda fc=fc, tap=tap, sh=sh, y3=y3, u3=u3: dve.scalar_tensor_tensor(
                        out=y3[:, :, sh:4], in0=u3[:, :, 0:4 - sh], scalar=fcw[:, tap, fc:fc + 1], in1=y3[:, :, sh:4],
                        op0=ALU.mult, op1=ALU.add), reads=[('gs',), ('c', 'fcw')], writes=[('cc',)])
                S.op('dve', lambda fc=fc, y3=y3, p3=p3: dve.scalar_tensor_tensor(
                    out=y3[:, :, 0:1], in0=p3[:, :, 1:2], scalar=fcw[:, 1, fc:fc + 1], in1=y3[:, :, 0:1],
                    op0=ALU.mult, op1=ALU.add), reads=[('pcg',), ('c', 'fcw')], writes=[('cc',)])
                S.op('dve', lambda fc=fc, y3=y3, p3=p3: dve.scalar_tensor_tensor(
                    out=y3[:, :, 0:2], in0=p3[:, :, 0:2], scalar=fcw[:, 0, fc:fc + 1], in1=y3[:, :, 0:2],
                    op0=ALU.mult, op1=ALU.add), reads=[('pcg',), ('c', 'fcw')], writes=[('cc',)])
                S.op('act', lambda: act.activation(out=cc, in_=cc, func=AF.Silu), reads=[('cc',)], writes=[('cc',)])
                S.op('dve', lambda ci=ci: dve.tensor_tensor(out=actb[:, ci, :], in0=psB[:, 0:NT], in1=cc, op=ALU.mult),
                     reads=pkeys(1, 0, NT) + [('cc',)], writes=[('act', ci)])
            lin_tm_residual(4, ng, actb, [('act', ci) for ci in range(ng)], defer_last=True)

            def trg(ng=ng):
                last = None
                for j in range(ng):
                    pt = psC if j < 4 else psD
                    last = pe.transpose(out=pt[0:34, (j % 4) * 128:(j % 4 + 1) * 128], in_=gtl[:, j, :], identity=identf[:, :])
                return last
            S.op('pe', trg, reads=[('gtl',), ('c', 'identf')], writes=[('ps', 'C', 0), ('ps', 'D', 0)])
            n1 = min(ng, 4) * 128
            S.op('act', lambda n1=n1: act.copy(out=rowb[0:34, 0:n1], in_=psC[0:34, 0:n1]), reads=[('ps', 'C', 0)], writes=[('rowb',)])
            if ng > 4:
                S.op('act', lambda ng=ng: act.copy(out=rowb[0:34, 512:ng * 128], in_=psD[0:34, 0:(ng - 4) * 128]),
                     reads=[('ps', 'D', 0)], writes=[('rowb',)])
            S.dma('sp', o_ffn_s[:, f0 * 128:f1 * 128], rowb[0:32, 0:ng * 128], reads=[('rowb',)])
            S.dma('sp', o_ffn_p[:, f0 * 128:f1 * 128], rowb[32:34, 0:ng * 128], reads=[('rowb',)])
        S.fence()

    if stop >= 7:
        gbc = stgF[:, :]
        with nc.allow_non_contiguous_dma(reason="gain broadcast"):
            S.dma('sp', gbc, gains_d[4:5, :].partition_broadcast(128).rearrange("p o d -> p (o d)"), writes=[('gbc',)])
        for t in range(9):
            s2 = t % 2
            ss = stat[:, 2 * s2:2 * s2 + 1]
            rs = stat[:, 2 * s2 + 1:2 * s2 + 2]
            junk = xs_bufs[s2]
            S.op('act', lambda t=t, junk=junk, ss=ss: act.activation(out=junk, in_=RH[:, t, :], func=AF.Square, accum_out=ss),
                 reads=[('h', t)], writes=[('junk', s2), ('st', s2)])
            S.op('act', lambda ss=ss: act.activation(out=ss, in_=ss, func=AF.Sqrt, scale=1.0 / D, bias=EPS),
                 reads=[('st', s2)], writes=[('st', s2)])
            S.op('dve', lambda ss=ss, rs=rs: dve.reciprocal(out=rs, in_=ss), reads=[('st', s2)], writes=[('rs', s2)])
            S.op('dve', lambda t=t, rs=rs: dve.scalar_tensor_tensor(out=RH[:, t, :], in0=RH[:, t, :], scalar=rs, in1=gbc,
                                                                    op0=ALU.mult, op1=ALU.mult),
                 reads=[('h', t), ('rs', s2), ('gbc',)], writes=[('h', t)])
            S.dma('sp', y_all[t * 128:(t + 1) * 128, :], RH[:, t, :], reads=[('h', t)])

    S.finish()
    es.close()
    return nc


def make_consts():
    c = np.zeros((128, 128 * 3 + 17), np.float32)
    c[:, 0:128] = np.eye(128, dtype=np.float32)
    j = np.arange(128)[:, None]
    i = np.arange(128)[None, :]
    c[:, 128:256] = (j <= i).astype(np.float32)
    m0 = np.zeros((128, 128), np.float32)
    samp = (j < 64) & (i < 64) & ((j // 4) == (i // 4)) & (j <= i)
    ovl = (j >= 64) & (i >= 64) & (j <= i)
    m0[samp | ovl] = 1.0
    c[:, 256:384] = m0
    for s in range(16):
        c[4 * s:4 * s + 4, 384 + s] = 1.0
    c[64:128, 384 + 16] = 1.0
    return c


def core_inputs(inp, c):
    seq, half = c // 2, c % 2
    xp = inp["x_prompt"][seq]
    pos0 = half * 1024
    x_all = np.zeros((NT, D), np.float32)
    x_all[0:64] = inp["x_sample"][16 * c:16 * c + 16].reshape(64, D)
    if half == 1:
        x_all[64:128] = xp[pos0 - 64:pos0]
    x_all[128:] = xp[pos0:pos0 + 1024]
    x_pre = np.zeros((1024, D), np.float32)
    if half == 1:
        x_pre[0:NPRE] = xp[0:NPRE]
    m = {
        "x_all": x_all, "x_pre": x_pre, "mem": np.ascontiguousarray(inp["mem_prompt"][seq]),
        "ovm": np.full((128, 1), float(half), np.float32),
        "cache_conv": np.ascontiguousarray(inp["cache_conv"][0, 16 * c:16 * c + 16].reshape(32, 1024)),
        "state_gla": np.ascontiguousarray(inp["state_gla"][0, 16 * c:16 * c + 16]),
        "cache_ffn": np.ascontiguousarray(inp["cache_ffn"][0, 16 * c:16 * c + 16].reshape(32, DFF)),
        "cache_mem_k": np.ascontiguousarray(inp["cache_mem_k"][0, 16 * c:16 * c + 16].reshape(16, 256, D)),
        "cache_mem_v": np.ascontiguousarray(inp["cache_mem_v"][0, 16 * c:16 * c + 16].reshape(16, 256, D)),
        "w_in": inp["w_in"][0], "w_out": inp["w_out"][0], "w_xq": inp["w_xq"][0], "w_xk": inp["w_xk"][0],
        "w_xv": inp["w_xv"][0], "w_xo": inp["w_xo"][0], "w_ffn_gate": inp["w_ffn_gate"][0],
        "w_ffn_up": inp["w_ffn_up"][0], "w_ffn_down": inp["w_ffn_down"][0],
        "gains": np.stack([inp["norm_mix"][0], inp["norm_x"][0], inp["norm_mem"][0], inp["norm_ffn"][0],
                           inp["norm_final"]]).astype(np.float32),
        "conv_w": inp["conv_w"][0], "w_gate2": inp["w_gate2"][0], "b_gate": inp["b_gate"],
        "gla_norm": inp["gla_norm"], "ffn_conv_w": inp["ffn_conv_w"][0], "ffn_conv_b": inp["ffn_conv_b"],
        "consts": make_consts(),
    }
    return {k: np.ascontiguousarray(np.asarray(v, dtype=np.float32)) for k, v in m.items()}


_NC_CACHE = {}


def kernel(**inputs):
    inp = {k: np.asarray(v) for k, v in inputs.items()}
    if 'nc' not in _NC_CACHE:
        _NC_CACHE['nc'] = build()
    nc = _NC_CACHE['nc']
    in_maps = [core_inputs(inp, c) for c in range(8)]
    res = run_bass_kernel_spmd(nc, in_maps, core_ids=list(range(8)))
    R = res.results
    y_prompt = np.zeros((4, 2048, D), np.float32)
    y_sample = np.zeros((128, 4, D), np.float32)
    conv_p = np.zeros((1, 4, 2, 1024), np.float32)
    gla_p = np.zeros((1, 4, 4, 128, 256), np.float32)
    ffn_p = np.zeros((1, 4, 2, DFF), np.float32)
    mk_p = np.zeros((1, 4, 256, 4, 512), np.float32)
    mv_p = np.zeros((1, 4, 256, 4, 512), np.float32)
    conv_s = np.zeros((1, 128, 2, 1024), np.float32)
    gla_s = np.zeros((1, 128, 4, 128, 256), np.float32)
    ffn_s = np.zeros((1, 128, 2, DFF), np.float32)
    for c in range(8):
        seq, half = c // 2, c % 2
        r = R[c]
        y_prompt[seq, half * 1024:(half + 1) * 1024] = r["y_all"][128:NT]
        y_sample[16 * c:16 * c + 16] = r["y_all"][0:64].reshape(16, 4, D)
        conv_s[0, 16 * c:16 * c + 16] = r["o_conv_s"].reshape(16, 2, 1024)
        gla_s[0, 16 * c:16 * c + 16] = r["o_gla_s"]
        ffn_s[0, 16 * c:16 * c + 16] = r["o_ffn_s"].reshape(16, 2, DFF)
        if half == 1:
            conv_p[0, seq] = r["o_conv_p"]
            gla_p[0, seq] = r["o_gla_p"]
            ffn_p[0, seq] = r["o_ffn_p"]
        else:
            mk_p[0, seq] = r["o_mk"].reshape(256, 4, 512)
            mv_p[0, seq] = r["o_mv"].reshape(256, 4, 512)
    return (y_prompt, y_sample, conv_p, gla_p, ffn_p, mk_p, mv_p, conv_s, gla_s, ffn_s)
```

```python
import os
import numpy as np
from contextlib import ExitStack
import concourse.bass as bass
import concourse.mybir as mybir
from concourse.bass_utils import run_bass_kernel_spmd

F32 = mybir.dt.float32
BF16 = mybir.dt.bfloat16
ALU = mybir.AluOpType
AF = mybir.ActivationFunctionType

NT = 1152
NPRE = 960
D = 2048
KC = 16
NSEQ = 16
DFF = 5632
FC = 44
COL = dict(bg=0, cg=1024, vc=2048, q=3072, k=3584, v=4096, r=5120, g1=6144)
MT = [(0, 512), (512, 1024), (1024, 1152)]
MTP = [(0, 512), (512, 960)]
EPS = 1e-6


class Sched:
    def __init__(self, nc, es):
        self.nc = nc
        self.engs = {'pe': nc.tensor, 'act': nc.scalar, 'dve': nc.vector, 'pool': nc.gpsimd, 'sp': nc.sync}
        self.sem = {k: es.enter_context(nc.semaphore('sem_' + k)) for k in ['pe', 'act', 'dve', 'pool']}
        self.cnt = {k: 0 for k in self.sem}
        self.ndma = 48
        self.dsem = [es.enter_context(nc.semaphore('dsem%d' % i)) for i in range(self.ndma)]
        self.dval = [0] * self.ndma
        self.dnext = {'w': 0, 'p': 0, 's': 0}
        self.seen = {e: {} for e in self.engs}
        self.lastw = {}
        self.readers = {}

    def _semof(self, k):
        return self.sem[k] if isinstance(k, str) else self.dsem[k]

    def _wait(self, eng, semkey, val):
        if self.seen[eng].get(semkey, 0) >= val:
            return
        self.seen[eng][semkey] = val
        self.engs[eng].wait_ge(self._semof(semkey), val)

    def _deps(self, eng, reads, writes):
        best = {}

        def add(t):
            if t is not None and best.get(t[0], 0) < t[1]:
                best[t[0]] = t[1]
        for k in reads:
            add(self.lastw.get(k))
        for k in writes:
            add(self.lastw.get(k))
            for s, v in self.readers.get(k, {}).items():
                add((s, v))
        for s, v in best.items():
            if s == 'pe' and eng == 'pe':
                continue
            self._wait(eng, s, v)

    def _commit(self, tok, reads, writes):
        for k in reads:
            r = self.readers.setdefault(k, {})
            if r.get(tok[0], 0) < tok[1]:
                r[tok[0]] = tok[1]
        for k in writes:
            self.lastw[k] = tok
            self.readers[k] = {}

    def op(self, eng, fn, reads=(), writes=()):
        if eng != 'pe':
            pr = [k for k in reads if k[0] == 'ps']
            if pr:
                writes = list(writes) + pr
        self._deps(eng, reads, writes)
        inst = fn()
        self.cnt[eng] += 1
        inst.then_inc(self.sem[eng], 1)
        tok = (eng, self.cnt[eng])
        self._commit(tok, reads, writes)
        return tok

    def dma(self, eng, out, in_, reads=(), writes=(), pool='g'):
        if pool != 'w':
            pool = 'p' if eng == 'pool' else 's'
        lo, hi = {'w': (0, 8), 'p': (8, 24), 's': (24, self.ndma)}[pool]
        i = lo + self.dnext[pool]
        self.dnext[pool] = (self.dnext[pool] + 1) % (hi - lo)
        if self.dval[i] > 0:
            self._wait(eng, i, self.dval[i])
        self._deps(eng, reads, writes)
        inst = self.engs[eng].dma_start(out=out, in_=in_)
        self.dval[i] += 16
        inst.then_inc(self.dsem[i], 16)
        tok = (i, self.dval[i])
        self._commit(tok, reads, writes)
        return tok

    def fence(self):
        for e in self.engs:
            for o in self.sem:
                if o != e and self.cnt[o] > 0:
                    self._wait(e, o, self.cnt[o])
            for i in range(8, self.ndma):
                if self.dval[i] > 0:
                    self._wait(e, i, self.dval[i])

    def finish(self):
        for i in range(self.ndma):
            if self.dval[i] > 0:
                self._wait('sp', i, self.dval[i])
        for o in self.sem:
            if self.cnt[o] > 0:
                self._wait('sp', o, self.cnt[o])


def build(stop=99, dbg=None):
    nc = bass.Bass("TRN2", target_bir_lowering=False)
    es = ExitStack()
    S = Sched(nc, es)
    pe, act, dve, pool, sp = nc.tensor, nc.scalar, nc.vector, nc.gpsimd, nc.sync

    def din(name, shape):
        return nc.dram_tensor(name, shape, F32, kind="ExternalInput").ap()

    def dout(name, shape):
        return nc.dram_tensor(name, shape, F32, kind="ExternalOutput").ap()

    x_all = din("x_all", [NT, D])
    x_pre = din("x_pre", [1024, D])
    mem = din("mem", [256, D])
    ovm_d = din("ovm", [128, 1])
    cconv_d = din("cache_conv", [32, 1024])
    sgla_d = din("state_gla", [NSEQ, 4, 128, 256])
    cffn_d = din("cache_ffn", [32, DFF])
    cmk_d = din("cache_mem_k", [NSEQ, 256, D])
    cmv_d = din("cache_mem_v", [NSEQ, 256, D])
    w_in = din("w_in", [D, 6160])
    w_out = din("w_out", [D, D])
    w_xq = din("w_xq", [D, D])
    w_xk = din("w_xk", [D, D])
    w_xv = din("w_xv", [D, D])
    w_xo = din("w_xo", [D, D])
    w_fg = din("w_ffn_gate", [D, DFF])
    w_fu = din("w_ffn_up", [D, DFF])
    w_fd = din("w_ffn_down", [DFF, D])
    gains_d = din("gains", [5, D])
    conv_w_d = din("conv_w", [3, 1024])
    wg2_d = din("w_gate2", [16, 512])
    bgate_d = din("b_gate", [1, 512])
    glan_d = din("gla_norm", [1, 1024])
    fcw_d = din("ffn_conv_w", [3, DFF])
    fcb_d = din("ffn_conv_b", [1, DFF])
    consts_d = din("consts", [128, 128 * 3 + 17])

    y_all = dout("y_all", [NT, D])
    o_conv_p = dout("o_conv_p", [2, 1024])
    o_gla_p = dout("o_gla_p", [4, 128, 256])
    o_ffn_p = dout("o_ffn_p", [2, DFF])
    o_mk = dout("o_mk", [256, D])
    o_mv = dout("o_mv", [256, D])
    o_conv_s = dout("o_conv_s", [32, 1024])
    o_gla_s = dout("o_gla_s", [NSEQ, 4, 128, 256])
    o_ffn_s = dout("o_ffn_s", [32, DFF])
    dbg_d = dout("dbg", [128, 16 * NT]) if dbg else None

    def sb(name, shape, dt):
        return es.enter_context(nc.sbuf_tensor("s_" + name, shape, dt))

    identb = sb("identb", [128, 128], BF16)
    identf = sb("identf", [128, 128], F32)
    maskC = sb("maskC", [128, 128], BF16)
    mask0 = sb("mask0", [128, 128], BF16)
    selm = sb("selm", [128, 17], F32)
    onesb = sb("onesb", [128, 128], BF16)
    onesf = sb("onesf", [128, 128], F32)
    gains = sb("gains", [128, 5, 16], F32)
    convw = sb("convw", [128, 3, 8], F32)
    fcw = sb("fcw", [128, 3, FC], F32)
    fcb = sb("fcb", [128, FC], F32)
    negb = sb("negb", [128, 4], F32)
    glan = sb("glan", [128, 8], F32)
    wg2 = sb("wg2", [16, 512], BF16)
    ovm = sb("ovm", [128, 1], F32)
    g1T = sb("g1T", [16, NT], BF16)
    expL = sb("expL", [128, 4, 32], F32)
    stat = sb("stat", [128, 8], F32)
    Sst = sb("Sst", [128, 4, 256], F32)
    Sbb2 = sb("Sbb", [128, 2, 1024], BF16)
    Sbb = [Sbb2[:, i, :].rearrange("p (h v) -> p h v", h=4) for i in range(2)]
    sb_cur = [0]

    stgF = sb("stgF", [128, D], F32)
    ws01 = sb("ws01", [128, 2, 16 * 256], BF16)
    WS = [ws01[:, i, :].rearrange("p (k n) -> p k n", k=16) for i in range(2)]
    WW = [ws01[:].rearrange("p a x -> p (a x)").rearrange("p (k n) -> p k n", k=16)]
    WSF = [ws01[:, i, :] for i in range(2)]
    RA = sb("RA", [128, 16, NT], BF16)
    RB = sb("RB", [128, 16, NT], BF16)

    psA = es.enter_context(nc.psum_tensor("psA", [128, 1536], F32))
    psB = es.enter_context(nc.psum_tensor("psB", [128, 1536], F32))
    psC = es.enter_context(nc.psum_tensor("psC", [128, 512], F32))
    psD = es.enter_context(nc.psum_tensor("psD", [128, 512], F32))
    PSG = [psA, psB]

    def pkeys(g, c0, c1):
        return [('ps', g, b) for b in range(c0 // 512, (c1 - 1) // 512 + 1)]

    with nc.allow_non_contiguous_dma(reason="small constant loads"):
        S.dma('pool', identb[:], consts_d[:, 0:128], writes=[('c', 'identb')])
        S.dma('sp', identf[:], consts_d[:, 0:128], writes=[('c', 'identf')])
        S.dma('pool', maskC[:], consts_d[:, 128:256], writes=[('c', 'maskC')])
        S.dma('pool', mask0[:], consts_d[:, 256:384], writes=[('c', 'mask0')])
        S.dma('sp', selm[:], consts_d[:, 384:401], writes=[('c', 'selm')])
        S.dma('sp', gains[:], gains_d.rearrange("g (c p) -> p g c", p=128), writes=[('c', 'gains')])
        S.dma('sp', convw[:], conv_w_d.rearrange("k (c p) -> p k c", p=128), writes=[('c', 'convw')])
        S.dma('sp', fcw[:], fcw_d.rearrange("k (c p) -> p k c", p=128), writes=[('c', 'fcw')])
        S.dma('sp', fcb[:], fcb_d.rearrange("o (c p) -> p (o c)", p=128), writes=[('c', 'fcb')])
        S.dma('sp', negb[:], bgate_d.rearrange("o (c p) -> p (o c)", p=128), writes=[('c', 'negb')])
        S.dma('sp', glan[:], glan_d.rearrange("o (c p) -> p (o c)", p=128), writes=[('c', 'glan')])
        S.dma('pool', wg2[:], wg2_d, writes=[('c', 'wg2')])
        S.dma('sp', ovm[:], ovm_d, writes=[('c', 'ovm')])
    S.op('dve', lambda: dve.memset(onesb[:], 1.0), writes=[('c', 'onesb')])
    S.op('dve', lambda: dve.memset(onesf[:], 1.0), writes=[('c', 'onesf')])
    S.op('dve', lambda: dve.tensor_scalar(out=negb[:], in0=negb[:], scalar1=-1.0, scalar2=None, op0=ALU.mult),
         reads=[('c', 'negb')], writes=[('c', 'negb')])
    S.op('dve', lambda: dve.memset(Sst[:], 0.0), writes=[('Sst', h_) for h_ in range(4)])
    S.op('dve', lambda: dve.memset(Sbb2[:], 0.0), writes=[('Sbb', 0), ('Sbb', 1)])

    slabs = []
    slab_state = {'issued': 0, 'used': 0, 'p1_end': None, 'rr': 0}
    resident = {}

    slab_hold = [None]

    def _free(slot):
        r = resident.get(slot)
        if r is not None and slab_hold[0] is not None and r >= slab_hold[0]:
            return False
        return r is None or r <= slab_state['used'] - 2

    def _try_issue(i):
        w_ap, r0, nk, c0, cw = slabs[i]
        src = w_ap[r0:r0 + nk * 128, c0:c0 + cw].rearrange("(kc p) n -> p kc n", p=128)
        if nk * cw > 4096:
            for ws_ in range(len(WW)):
                if _free(2 * ws_) and _free(2 * ws_ + 1):
                    resident[2 * ws_] = i
                    resident[2 * ws_ + 1] = i
                    slab_state[('slot', i)] = ws_
                    S.dma('pool', WW[ws_][:, 0:nk, 0:cw], src, writes=[('ws', 2 * ws_), ('ws', 2 * ws_ + 1)], pool='w')
                    return True
            return False
        n = len(WS)
        for d in range(n):
            sl_ = (slab_state['rr'] + d) % n
            if _free(sl_):
                slab_state['rr'] = (sl_ + 1) % n
                resident[sl_] = i
                slab_state[('slot', i)] = sl_
                dst = WS[sl_][:, 0:nk, 0:cw] if cw <= 256 else WSF[sl_][:, 0:nk * cw].rearrange("p (k n) -> p k n", k=nk)
                S.dma('pool', dst, src, writes=[('ws', sl_)], pool='w')
                return True
        return False

    def slab_issue_upto(n):
        while slab_state['issued'] < min(n, len(slabs)):
            if not _try_issue(slab_state['issued']):
                return False
            slab_state['issued'] += 1
        return True

    def next_slab():
        i = slab_state['used']
        slab_state['used'] += 1
        ok = slab_issue_upto(i + 1)
        assert ok and slab_state['issued'] >= i + 1, "no free weight slot"
        prefetch()
        return slab_state[('slot', i)]

    def prefetch():
        slab_issue_upto(slab_state['used'] + 3)

    def add_fm(w_ap, c0, ncols):
        c = c0
        while c < c0 + ncols:
            cw = min(256, c0 + ncols - c)
            slabs.append((w_ap, 0, KC, c, cw))
            c += cw

    add_fm(w_in, COL['g1'], 16)
    add_fm(w_in, COL['k'], 512)
    add_fm(w_in, COL['v'], 1024)
    add_fm(w_in, COL['g1'], 16)
    add_fm(w_in, COL['r'], 1024)
    add_fm(w_in, COL['k'], 512)
    add_fm(w_in, COL['q'], 512)
    add_fm(w_in, COL['v'], 1024)
    for j in range(4):
        for nm in ('cg', 'vc', 'bg'):
            add_fm(w_in, COL[nm] + 256 * j, 256)
    slab_state['p1_end'] = len(slabs)
    for c4 in range(4):
        slabs.append((w_out, 0, KC, c4 * 512, 512))
    add_fm(w_xq, 0, D)
    for w_ in (w_xk, w_xv):
        for c4 in range(4):
            slabs.append((w_, 0, KC, c4 * 512, 512))
    for c4 in range(4):
        slabs.append((w_xo, 0, KC, c4 * 512, 512))
    FGROUPS = [(0, 8), (8, 16), (16, 24), (24, 32), (32, 40), (40, 44)]
    for (f0, f1) in FGROUPS:
        for c in range(f0, f1):
            slabs.append((w_fg, 0, KC, c * 128, 128))
            slabs.append((w_fu, 0, KC, c * 128, 128))
        for c4 in range(4):
            slabs.append((w_fd, f0 * 128, f1 - f0, c4 * 512, 512))

    def mm_fm(slot, j, xT, mts, psg, ncol=128, xkeys=None):
        for (a, b) in mts:
            def fn(a=a, b=b):
                last = None
                for kc in range(KC):
                    last = pe.matmul(PSG[psg][0:ncol, a:b], lhsT=WS[slot][:, kc, j * 128:j * 128 + ncol],
                                     rhs=xT[:, kc, a:b], start=(kc == 0), stop=(kc == KC - 1))
                return last
            xk_ = [k for k in xkeys if a <= k[1] * 128 < b]
            S.op('pe', fn, reads=[('ws', slot)] + xk_, writes=pkeys(psg, a, b))

    xs_bufs = [RB[:, 8 + 2 * i:10 + 2 * i, :].rearrange("p a b -> p (a b)")[:, 0:2048] for i in range(4)]
    norm_ctr = [0]
    norm_alt = [True]

    def norm_T(src, srckeys, nrows, gi, dstT, col0, dstkey, xsb=None, defer=False):
        i = norm_ctr[0]
        norm_ctr[0] += 1
        s = i % len(xs_bufs)
        xs = xs_bufs[s] if xsb is None else xsb
        xkey = ('xs', s) if xsb is None else ('xs', 'm')
        ss = stat[:, 2 * s:2 * s + 1]
        rs = stat[:, 2 * s + 1:2 * s + 2]
        S.op('act', lambda: act.activation(out=xs[0:nrows, :], in_=src, func=AF.Square, accum_out=ss[0:nrows, :]),
             reads=srckeys, writes=[xkey, ('st', s)])
        S.op('act', lambda: act.activation(out=ss[0:nrows, :], in_=ss[0:nrows, :], func=AF.Sqrt, scale=1.0 / D, bias=EPS),
             reads=[('st', s)], writes=[('st', s)])
        S.op('dve', lambda: dve.reciprocal(out=rs[0:nrows, :], in_=ss[0:nrows, :]),
             reads=[('st', s)], writes=[('rs', s)])
        S.op('pool', lambda: pool.tensor_scalar(out=xs[0:nrows, :], in0=src, scalar1=rs[0:nrows, :], scalar2=0.0,
                                                op0=ALU.mult, op1=ALU.add),
             reads=srckeys + [('rs', s)], writes=[xkey])
        def part2():
            for half in range(2):
                if norm_alt[0]:
                    bsel = [[(psA[:, 0:512], ('ps', 0, 0)), (psA[:, 512:1024], ('ps', 0, 1))],
                            [(psA[:, 1024:1536], ('ps', 0, 2)), (psC[:, :], ('ps', 'C', 0))]][i % 2][half]
                    pst_ap, pk0 = bsel
                    pk = [pk0]
                    pv = pst_ap.bitcast(BF16)
                else:
                    pst = psC if half == 0 else psD
                    pk = [('ps', 'C' if half == 0 else 'D', 0)]
                    pv = pst[:].bitcast(BF16)

                def tr(half=half, pv=pv):
                    last = None
                    for c in range(8):
                        cc = half * 8 + c
                        last = pe.transpose(out=pv[:, c * 128:c * 128 + nrows],
                                            in_=xs[0:nrows, cc * 128:(cc + 1) * 128], identity=identb[0:nrows, 0:nrows])
                    return last
                S.op('pe', tr, reads=[xkey, ('c', 'identb')], writes=pk)
                gb = gains[:, gi, half * 8:half * 8 + 8].unsqueeze(2).broadcast_to([128, 8, nrows])
                src_ps = pv.rearrange("p (c t) -> p c t", c=8)[:, :, 0:nrows]
                S.op('dve', lambda half=half, gb=gb, src_ps=src_ps: dve.tensor_tensor(
                    out=dstT[:, half * 8:half * 8 + 8, col0:col0 + nrows], in0=src_ps, in1=gb, op=ALU.mult),
                    reads=pk + [('c', 'gains')], writes=[dstkey])
        if defer:
            return part2
        part2()

    es_p1 = ExitStack()
    xstage_t = es_p1.enter_context(nc.sbuf_tensor("xstage", [128, 2, D], F32))
    xstage = [xstage_t[:, 0, :], xstage_t[:, 1, :]]
    qt = es_p1.enter_context(nc.sbuf_tensor("qt", [128, 4, NT], BF16))
    kt = es_p1.enter_context(nc.sbuf_tensor("kt", [128, 4, NT], BF16))
    cums = es_p1.enter_context(nc.sbuf_tensor("cums", [128, 4, NT], F32))
    vt = cums[:].rearrange("p h t -> p (h t)").bitcast(BF16).rearrange("p (t v) -> p t v", t=9)
    scr = es_p1.enter_context(nc.sbuf_tensor("scr", [128, 3, 1152], F32))
    sbs_t = es_p1.enter_context(nc.sbuf_tensor("sbs_t", [128, 4, 1024], BF16))
    khm_t = es_p1.enter_context(nc.sbuf_tensor("khm_t", [128, 2, 512], BF16))
    cvt = es_p1.enter_context(nc.sbuf_tensor("cvt", [128, 4, NT], F32))
    CUMK = [('cums', h) for h in range(4)]
    cvf = cvt[:].rearrange("p a b -> p (a b)")
    xstage = xstage + [cvf[:, 0:D], cvf[:, D:2 * D]]

    def load_norm(x_ap, ntok, dstT, col0key):
        nt = (ntok + 127) // 128
        for t in range(nt):
            nr = min(128, ntok - t * 128)
            s = t % 4
            S.dma('sp', xstage[s][0:nr, :], x_ap[t * 128:t * 128 + nr, :], writes=[('xstage', s)])
            norm_T(xstage[s][0:nr, :], [('xstage', s)], nr, 0, dstT, t * 128, (col0key, t))

    def gates(slot, xT, mts, ntok, xk, chunks):
        mm_fm(slot, 0, xT, mts, 0, ncol=16, xkeys=xk)
        S.op('act', lambda: act.copy(out=g1T[:, 0:ntok], in_=psA[0:16, 0:ntok]),
             reads=pkeys(0, 0, ntok), writes=[('g1T',)])
        for h in range(4):
            g = h % 2

            def zf(h=h, g=g):
                last = None
                for (a, b) in mts:
                    last = pe.matmul(PSG[g][:, a:b], lhsT=wg2[:, h * 128:(h + 1) * 128], rhs=g1T[:, a:b],
                                     start=True, stop=True)
                return last
            S.op('pe', zf, reads=[('g1T',), ('c', 'wg2')], writes=pkeys(g, 0, ntok))
            S.op('act', lambda h=h, g=g: act.activation(out=scr[:, 0, 0:ntok], in_=PSG[g][:, 0:ntok], func=AF.Exp,
                                                        scale=-1.0, bias=negb[:, h:h + 1]),
                 reads=pkeys(g, 0, ntok) + [('c', 'negb')], writes=[('scr', 0)])
            S.op('act', lambda h=h: act.activation(out=scr[:, 1, 0:ntok], in_=scr[:, 0, 0:ntok], func=AF.Ln,
                                                   scale=1.0, bias=1.0),
                 reads=[('scr', 0)], writes=[('scr', 1)])
            for (a, b) in chunks:
                S.op('dve', lambda h=h, a=a, b=b: dve.tensor_tensor_scan(
                    out=cums[:, h, a:b], data0=onesf[:, 0:b - a], data1=scr[:, 1, a:b], initial=0.0,
                    op0=ALU.mult, op1=ALU.add),
                    reads=[('scr', 1), ('c', 'onesf')], writes=[('cums', h)])

    def expL_cols(c_first, stride, n, e0):
        src = cums[:, :, c_first:c_first + stride * (n - 1) + 1:stride]
        S.op('act', lambda: act.activation(out=expL[:, :, e0:e0 + n], in_=src, func=AF.Exp, scale=-1.0 / 16),
             reads=CUMK, writes=[('expL',)])

    def qk_tilde(dst, dkey, sign, scale, ntok, mts, xT, xk):
        for h in range(4):
            if h % 2 == 0:
                sl = next_slab()
            g = h % 2
            mm_fm(sl, h % 2, xT, mts, g, xkeys=xk)
            if h % 2 == 1:
                prefetch()
            S.op('act', lambda h=h: act.activation(out=scr[:, 0, 0:ntok], in_=cums[:, h, 0:ntok], func=AF.Exp,
                                                   scale=sign / 16.0),
                 reads=[('cums', h)], writes=[('scr', 0)])
            S.op('dve', lambda h=h, g=g: dve.scalar_tensor_tensor(
                out=dst[:, h, 0:ntok], in0=PSG[g][:, 0:ntok], scalar=scale, in1=scr[:, 0, 0:ntok],
                op0=ALU.mult, op1=ALU.mult),
                reads=pkeys(g, 0, ntok) + [('scr', 0)], writes=[(dkey, h)])

    def v_tok(ntok, xT, xk):
        nt = (ntok + 127) // 128
        for q4 in range(4):
            sl = next_slab()
            for t in range(nt):
                nr = min(128, ntok - t * 128)
                g = t % 2
                half = (t // 2) % 2

                def fn(t=t, nr=nr, g=g, half=half, sl=sl):
                    last = None
                    for kc in range(KC):
                        last = pe.matmul(PSG[g][0:nr, half * 512:half * 512 + 256],
                                         lhsT=xT[:, kc, t * 128:t * 128 + nr], rhs=WS[sl][:, kc, 0:256],
                                         start=(kc == 0), stop=(kc == KC - 1))
                    return last
                S.op('pe', fn, reads=[('ws', sl), xk[t]], writes=[('ps', g, half)])
                S.op('act', lambda t=t, nr=nr, g=g, half=half, q4=q4: act.copy(
                    out=vt[0:nr, t, q4 * 256:(q4 + 1) * 256], in_=PSG[g][0:nr, half * 512:half * 512 + 256]),
                    reads=[('ps', g, half)], writes=[('vt', t)] + CUMK)
            prefetch()

    kh = scr[:, 2, :].bitcast(BF16)
    khT = kh[:, 0:512].rearrange("p (h t) -> p h t", h=4)
    khat = kh[:, 512:1024].rearrange("p (h d) -> p h d", h=4)
    Am = kh[:, 1024:1536].rearrange("p (h t) -> p h t", h=4)
    pvD = psD[:].bitcast(BF16)

    def khat_T(n):
        def tr():
            last = None
            for h in range(4):
                last = pe.transpose(out=pvD[0:n, h * 128:(h + 1) * 128], in_=khT[:, h, 0:n], identity=identb[:, :])
            return last
        S.op('pe', tr, reads=[('khT',), ('c', 'identb')], writes=[('ps', 'D', 0)])
        S.op('act', lambda: act.copy(out=khat[0:n].rearrange("p h d -> p (h d)"), in_=pvD[0:n, 0:512]),
             reads=[('ps', 'D', 0)], writes=[('khat',)])

    def state_khat(c0, n, eidx):
        eb = expL[:, :, eidx:eidx + 1].broadcast_to([128, 4, n])
        S.op('dve', lambda: dve.tensor_tensor(out=khT[:, :, 0:n], in0=kt[:, :, c0:c0 + n], in1=eb, op=ALU.mult),
             reads=[('kt', h) for h in range(4)] + [('expL',)], writes=[('khT',)])
        khat_T(n)

    def state_update(lhs, lkey, n, vti, eidx, Sf, skey, pst, pstk):
        if not isinstance(pst, list):
            pst = [pst[:, h * 256:(h + 1) * 256] for h in range(4)]

        def ds():
            last = None
            for h in range(4):
                last = pe.matmul(pst[h], lhsT=lhs[0:n, h, :],
                                 rhs=vt[0:n, vti, h * 256:(h + 1) * 256], start=True, stop=True)
            return last
        S.op('pe', ds, reads=[lkey, ('vt', vti)], writes=pstk)
        for h in range(4):
            hk_ = skey + (h,)
            S.op('dve', lambda h=h: dve.scalar_tensor_tensor(
                out=Sf[:, h, :], in0=Sf[:, h, :], scalar=expL[:, h, eidx:eidx + 1], in1=pst[h],
                op0=ALU.mult, op1=ALU.add),
                reads=[hk_, ('expL',)] + pstk, writes=[hk_])

    PSB01 = [('ps', 1, 0), ('ps', 1, 1)]
    PSA01 = [('ps', 0, 0), ('ps', 0, 1)]

    PRE_CH = [(i * 128, min(NPRE, i * 128 + 128)) for i in range(8)]
    load_norm(x_pre, NPRE, RA, 'xT')
    xk_pre = [('xT', t) for t in range(8)]
    gates(next_slab(), RA, MTP, NPRE, xk_pre, PRE_CH)
    prefetch()
    qk_tilde(kt, 'kt', +1.0, 1.0, NPRE, MTP, RA, xk_pre)
    expL_cols(127, 128, 7, 0)
    expL_cols(NPRE - 1, 1, 1, 7)
    v_tok(NPRE, RA, xk_pre)
    def load_norm_tile(t):
        s_ = t % 4
        S.dma('sp', xstage[s_][:, :], x_all[t * 128:(t + 1) * 128, :], writes=[('xstage', s_)])
        norm_T(xstage[s_][:, :], [('xstage', s_)], 128, 0, RA, t * 128, ('xT', t))

    for ci, (a, b) in enumerate(PRE_CH):
        state_khat(a, b - a, ci)
        state_update(khat, ('khat',), b - a, ci, ci, Sst, ('Sst',), psB, PSB01)
        if stop >= 1:
            load_norm_tile(ci)
    S.op('act', lambda: act.copy(out=Sbb[0], in_=Sst[:]), reads=[('Sst', h_) for h_ in range(4)], writes=[('Sbb', 0)])
    if dbg == 'pre':
        S.dma('sp', dbg_d[:, 0:1024], Sst[:].rearrange("p h v -> p (h v)"), reads=[('Sst', h_) for h_ in range(4)])

    if stop >= 1:
        load_norm_tile(8)
        S.fence()
        xk = [('xT', t) for t in range(9)]
        CH = [(4 * s_, 4 * s_ + 4) for s_ in range(16)] + [(64, 128)] + [(128 * i, 128 * i + 128) for i in range(1, 9)]
        g1slot = next_slab()
        mm_fm(g1slot, 0, RA, MT, 0, ncol=16, xkeys=xk)
        S.op('act', lambda: act.copy(out=g1T[:, 0:NT], in_=psA[0:16, 0:NT]), reads=pkeys(0, 0, NT), writes=[('g1T',)])

        def gates_head(h):
            g = h % 2

            def zf(h=h, g=g):
                last = None
                for (a, b) in MT:
                    last = pe.matmul(PSG[g][:, a:b], lhsT=wg2[:, h * 128:(h + 1) * 128], rhs=g1T[:, a:b], start=True, stop=True)
                return last
            S.op('pe', zf, reads=[('g1T',), ('c', 'wg2')], writes=pkeys(g, 0, NT))
            S.op('act', lambda h=h, g=g: act.activation(out=scr[:, 2, :], in_=PSG[g][:, 0:NT], func=AF.Exp, scale=-1.0,
                                                        bias=negb[:, h:h + 1]),
                 reads=pkeys(g, 0, NT) + [('c', 'negb')], writes=[('scr', 2), ('khT',), ('khat',), ('Am',)])
            S.op('act', lambda: act.activation(out=scr[:, 2, :], in_=scr[:, 2, :], func=AF.Ln, scale=1.0, bias=1.0),
                 reads=[('scr', 2)], writes=[('scr', 2)])
            for (a, b) in CH:
                S.op('dve', lambda h=h, a=a, b=b: dve.tensor_tensor_scan(
                    out=cums[:, h, a:b], data0=onesf[:, 0:b - a], data1=scr[:, 2, a:b], initial=0.0,
                    op0=ALU.mult, op1=ALU.add),
                    reads=[('scr', 2), ('c', 'onesf')], writes=[('cums', h)])

        for j2 in range(4):
            if j2 > 0:
                gates_head(j2 - 1)
            sl = next_slab()
            for j in range(2):
                g = j
                mm_fm(sl, j, RA, MT, g, xkeys=xk)
                if j == 1:
                    prefetch()
                S.op('act', lambda g=g: act.activation(out=scr[:, g, :], in_=PSG[g][:, 0:NT], func=AF.Silu),
                     reads=pkeys(g, 0, NT), writes=[('scr', g)])
                S.op('dve', lambda g=g, jj=j2 * 2 + j: dve.tensor_scalar(out=RB[:, 8 + jj, :], in0=scr[:, g, :],
                                                                         scalar1=glan[:, jj:jj + 1], scalar2=None, op0=ALU.mult),
                     reads=[('scr', g), ('c', 'glan')], writes=[('aT', 8 + j2 * 2 + j)])
        gates_head(3)
        qk_tilde(kt, 'kt', +1.0, 1.0, NT, MT, RA, xk)
        qk_tilde(qt, 'qt', -1.0, 128 ** -0.5, NT, MT, RA, xk)
        expL_cols(255, 128, 8, 0)
        expL_cols(127, 1, 1, 8)
        expL_cols(3, 4, 16, 9)
        v_tok(NT, RA, xk)
        S.fence()

        xsf = xstage_t[:].rearrange("p a b -> p (a b)")
        s0b = [xsf[:, i * 1024:(i + 1) * 1024].rearrange("p (h v) -> p h v", h=4) for i in range(4)]
        sbs = [sbs_t[:, i, :].rearrange("p (h v) -> p h v", h=4) for i in range(4)]
        khm = [khm_t[:, i, :].rearrange("p (h d) -> p h d", h=4) for i in range(2)]
        osb = scr[:, 0, 0:1024]
        sqb = scr[:, 1, :].bitcast(BF16)[:, 0:1024]
        rstd = scr[:, 1, 512:1024]

        def gla_post(c0, n, extra=None):
            o3 = osb.rearrange("p (b t) -> p b t", b=8)
            o4 = osb.rearrange("p (h c t) -> p h c t", h=4, c=2)
            if extra is not None:
                S.op('act', lambda: act.copy(out=osb, in_=psB[:, 0:1024]), reads=PSB01, writes=[('scr', 0)])
                S.op('dve', lambda: dve.tensor_tensor(out=osb, in0=extra, in1=osb, op=ALU.add),
                     reads=[('scr', 0)] + PSA01, writes=[('scr', 0)])
                S.op('act', lambda: act.activation(out=sqb, in_=osb, func=AF.Square), reads=[('scr', 0)],
                     writes=[('sqb',)])
            else:
                S.op('act', lambda: act.activation(out=sqb, in_=psB[:, 0:1024], func=AF.Square), reads=PSB01,
                     writes=[('sqb',)])

            def msf():
                last = None
                for h in range(4):
                    for vc in range(2):
                        last = pe.matmul(psC[:, h * 128:(h + 1) * 128], lhsT=onesb[:, :],
                                         rhs=sqb[:, (h * 2 + vc) * 128:(h * 2 + vc + 1) * 128],
                                         start=(vc == 0), stop=(vc == 1))
                return last
            S.op('pe', msf, reads=[('sqb',), ('c', 'onesb')], writes=[('ps', 'C', 0)])
            S.op('act', lambda: act.activation(out=rstd, in_=psC[:, :], func=AF.Sqrt, scale=1.0 / 256, bias=EPS),
                 reads=[('ps', 'C', 0)], writes=[('rstd',)])
            S.op('dve', lambda: dve.reciprocal(out=rstd, in_=rstd), reads=[('rstd',)], writes=[('rstd',)])
            r4 = rstd.rearrange("p (h t) -> p h t", h=4).unsqueeze(2).broadcast_to([128, 4, 2, 128])
            if extra is not None:
                S.op('dve', lambda: dve.tensor_tensor(out=o4, in0=o4, in1=r4, op=ALU.mult),
                     reads=[('scr', 0), ('rstd',)], writes=[('scr', 0)])
            else:
                p4 = psB[:, 0:1024].rearrange("p (h c t) -> p h c t", h=4, c=2)
                S.op('dve', lambda: dve.tensor_tensor(out=o4, in0=p4, in1=r4, op=ALU.mult),
                     reads=PSB01 + [('rstd',)], writes=[('scr', 0)])
            S.op('dve', lambda: dve.tensor_tensor(out=RB[:, 8:16, c0:c0 + n], in0=o3[:, :, 0:n], in1=RB[:, 8:16, c0:c0 + n],
                                                  op=ALU.mult),
                 reads=[('scr', 0)] + [('aT', 8 + j) for j in range(8)], writes=[('aT', 8 + j) for j in range(8)])

        def gla_AT(c0, mask, mkey):
            def af():
                last = None
                for h in range(4):
                    last = pe.matmul(psC[:, h * 128:(h + 1) * 128], lhsT=kt[:, h, c0:c0 + 128], rhs=qt[:, h, c0:c0 + 128],
                                     start=True, stop=True)
                return last
            S.op('pe', af, reads=[('kt', h) for h in range(4)] + [('qt', h) for h in range(4)], writes=[('ps', 'C', 0)])
            mb = mask[:, :].unsqueeze(1).broadcast_to([128, 4, 128])
            S.op('dve', lambda: dve.tensor_tensor(out=Am, in0=psC[:, :].rearrange("p (h t) -> p h t", h=4), in1=mb, op=ALU.mult),
                 reads=[('ps', 'C', 0), ('c', mkey)], writes=[('Am',)])

        gla_AT(0, mask0, 'mask0')
        for h in range(4):
            e3 = expL[:, h, 9:25].unsqueeze(2).broadcast_to([128, 16, 4])
            S.op('dve', lambda h=h, e3=e3: dve.tensor_tensor(out=khT[:, h, 0:64].rearrange("p (s t) -> p s t", s=16),
                                                             in0=kt[:, h, 0:64].rearrange("p (s t) -> p s t", s=16),
                                                             in1=e3, op=ALU.mult),
                 reads=[('kt', h), ('expL',)], writes=[('khT',)])
            S.op('dve', lambda h=h: dve.tensor_scalar(out=khT[:, h, 64:128], in0=kt[:, h, 64:128],
                                                      scalar1=expL[:, h, 8:9], scalar2=None, op0=ALU.mult),
                 reads=[('kt', h), ('expL',)], writes=[('khT',)])
        khat_T(128)

        T0D = [psA[:, 1024:1280], psA[:, 1280:1536], psB[:, 1024:1280], psB[:, 1280:1536]]
        T0DK = [('ps', 0, 2), ('ps', 1, 2)]

        def intra0():
            last = None
            for b_ in range(8):
                last = pe.matmul(psB[:, b_ * 128:(b_ + 1) * 128], lhsT=vt[:, 0, b_ * 128:(b_ + 1) * 128], rhs=Am[:, b_ // 2, :],
                                 start=True, stop=True)
            for b_ in range(8):
                last = pe.matmul(psA[:, b_ * 128 + 64:(b_ + 1) * 128], lhsT=Sbb[0][:, b_ // 2, (b_ % 2) * 128:(b_ % 2 + 1) * 128],
                                 rhs=qt[:, b_ // 2, 64:128], start=True, stop=True)
            return last
        S.op('pe', intra0, reads=[('vt', 0), ('Am',), ('Sbb', 0)] + [('qt', h) for h in range(4)], writes=PSB01 + PSA01)
        def s0_load(q_):
            S.dma('sp', s0b[q_ % 4], sgla_d[q_].rearrange("h d v -> d h v"), writes=[('s0b', q_ % 4, h) for h in range(4)])
        for q_ in range(3):
            s0_load(q_)
        for s_ in range(NSEQ):
            sl = s_ % 4
            k2 = s_ % 2
            S.op('act', lambda sl=sl: act.copy(out=sbs[sl], in_=s0b[sl]), reads=[('s0b', sl, h) for h in range(4)], writes=[('sbs', sl)])

            def inter(s_=s_, sl=sl):
                last = None
                for b_ in range(8):
                    last = pe.matmul(psA[:, b_ * 128 + 4 * s_:b_ * 128 + 4 * s_ + 4],
                                     lhsT=sbs[sl][:, b_ // 2, (b_ % 2) * 128:(b_ % 2 + 1) * 128],
                                     rhs=qt[:, b_ // 2, 4 * s_:4 * s_ + 4], start=True, stop=True)
                return last
            S.op('pe', inter, reads=[('sbs', sl)] + [('qt', h) for h in range(4)], writes=PSA01)
            S.op('dve', lambda s_=s_, k2=k2: dve.tensor_scalar(
                out=khm[k2].rearrange("p h d -> p (h d)"), in0=khat.rearrange("p h d -> p (h d)"),
                scalar1=selm[:, s_:s_ + 1], scalar2=None, op0=ALU.mult),
                reads=[('khat',), ('c', 'selm')], writes=[('khm', k2)])
            state_update(khm[k2], ('khm', k2), 128, 0, 9 + s_, s0b[sl], ('s0b', sl), T0D, T0DK)
            S.dma('sp', o_gla_s[s_].rearrange("h d v -> d h v"), s0b[sl], reads=[('s0b', sl, h) for h in range(4)])
            if s_ + 3 < NSEQ:
                s0_load(s_ + 3)
        S.op('dve', lambda: dve.tensor_scalar(
            out=khm[0].rearrange("p h d -> p (h d)"), in0=khat.rearrange("p h d -> p (h d)"),
            scalar1=selm[:, 16:17], scalar2=None, op0=ALU.mult),
            reads=[('khat',), ('c', 'selm')], writes=[('khm', 0)])
        gla_post(0, 128, extra=psA[:, 0:1024])
        state_update(khm[0], ('khm', 0), 128, 0, 8, Sst, ('Sst',), T0D, T0DK)
        S.op('act', lambda: act.copy(out=Sbb[1], in_=Sst[:]), reads=[('Sst', h_) for h_ in range(4)], writes=[('Sbb', 1)])
        sb_cur[0] = 1

        cgs = [cvt[:, 0, :], cvt[:, 1, :]]
        us = [cvt[:, 2, :], cvt[:, 3, :]]
        ycs = cgs
        rowbuf = stgF[0:34, 0:1024]
        pcc = stgF[:, 1024:1280].rearrange("p (j r) -> p j r", j=8)
        utl = stgF[:, 1280:1280 + 8 * 34].rearrange("p (j r) -> p j r", j=8)
        S.dma('sp', rowbuf[0:32, :], cconv_d, writes=[('rowbuf',)])

        def trc():
            last = None
            for j in range(8):
                last = pe.transpose(out=psC[:, j * 32:(j + 1) * 32], in_=rowbuf[0:32, j * 128:(j + 1) * 128],
                                    identity=identf[0:32, 0:32])
            return last
        S.op('pe', trc, reads=[('rowbuf',), ('c', 'identf')], writes=[('ps', 'C', 0)])
        S.op('act', lambda: act.copy(out=pcc.rearrange("p j r -> p (j r)"), in_=psC[:, 0:256]),
             reads=[('ps', 'C', 0)], writes=[('pcc',)])

        def per_bank(eng, mk, reads, writes):
            for bi, (a, b) in enumerate(MT):
                S.op(eng, lambda a=a, b=b: mk(a, b), reads=[('ps', 0, bi)] + reads, writes=writes)

        def conv_gen():
            for j2 in range(4):
                sl = next_slab()
                for j in range(2):
                    mm_fm(sl, j, RA, MT, 0, xkeys=xk)
                    per_bank('act', lambda a, b, j=j: act.copy(out=cgs[j][:, a:b], in_=psA[:, a:b]), [], [('cgs', j)])
                    if j == 1:
                        prefetch()
                    yield
                sl = next_slab()
                for j in range(2):
                    jj = j2 * 2 + j
                    mm_fm(sl, j, RA, MT, 0, xkeys=xk)
                    per_bank('dve', lambda a, b, j=j: dve.tensor_tensor(out=us[j][:, a:b], in0=psA[:, a:b], in1=cgs[j][:, a:b], op=ALU.mult),
                             [('cgs', j)], [('us', j)])
                    if j == 1:
                        prefetch()
                    S.op('act', lambda j=j, jj=jj: act.copy(out=utl[:, jj, 0:32].rearrange("p (s r) -> p s r", s=16),
                                                            in_=us[j][:, 0:64].rearrange("p (s t) -> p s t", s=16)[:, :, 2:4]),
                         reads=[('us', j)], writes=[('utl',)])
                    S.op('act', lambda j=j, jj=jj: act.copy(out=utl[:, jj, 32:34], in_=us[j][:, NT - 2:NT]),
                         reads=[('us', j)], writes=[('utl',)])
                    yc = ycs[j]
                    S.op('pool', lambda j=j, jj=jj, yc=yc: pool.tensor_scalar(out=yc, in0=us[j], scalar1=convw[:, 2, jj:jj + 1],
                                                                              scalar2=0.0, op0=ALU.mult, op1=ALU.add),
                         reads=[('us', j), ('c', 'convw')], writes=[('cgs', j)])
                    y3 = yc[:, 0:64].rearrange("p (s t) -> p s t", s=16)
                    u3 = us[j][:, 0:64].rearrange("p (s t) -> p s t", s=16)
                    p3 = pcc[:, jj, :].rearrange("p (s r) -> p s r", s=16)
                    for (tap, sh) in ((1, 1), (0, 2)):
                        S.op('dve', lambda j=j, jj=jj, yc=yc, tap=tap, sh=sh: dve.scalar_tensor_tensor(
                            out=yc[:, 64 + sh:NT], in0=us[j][:, 64:NT - sh], scalar=convw[:, tap, jj:jj + 1], in1=yc[:, 64 + sh:NT],
                            op0=ALU.mult, op1=ALU.add),
                            reads=[('us', j), ('c', 'convw')], writes=[('cgs', j)])
                        S.op('dve', lambda j=j, jj=jj, y3=y3, u3=u3, tap=tap, sh=sh: dve.scalar_tensor_tensor(
                            out=y3[:, :, sh:4], in0=u3[:, :, 0:4 - sh], scalar=convw[:, tap, jj:jj + 1], in1=y3[:, :, sh:4],
                            op0=ALU.mult, op1=ALU.add),
                            reads=[('us', j), ('c', 'convw')], writes=[('cgs', j)])
                    S.op('dve', lambda jj=jj, y3=y3, p3=p3: dve.scalar_tensor_tensor(
                        out=y3[:, :, 0:1], in0=p3[:, :, 1:2], scalar=convw[:, 1, jj:jj + 1], in1=y3[:, :, 0:1],
                        op0=ALU.mult, op1=ALU.add),
                        reads=[('pcc',), ('c', 'convw')], writes=[('cgs', j)])
                    S.op('dve', lambda jj=jj, y3=y3, p3=p3: dve.scalar_tensor_tensor(
                        out=y3[:, :, 0:2], in0=p3[:, :, 0:2], scalar=convw[:, 0, jj:jj + 1], in1=y3[:, :, 0:2],
                        op0=ALU.mult, op1=ALU.add),
                        reads=[('pcc',), ('c', 'convw')], writes=[('cgs', j)])
                    yield
                sl = next_slab()
                for j in range(2):
                    jj = j2 * 2 + j
                    mm_fm(sl, j, RA, MT, 0, xkeys=xk)
                    per_bank('dve', lambda a, b, j=j, jj=jj: dve.tensor_tensor(out=RB[:, jj, a:b], in0=psA[:, a:b], in1=ycs[j][:, a:b], op=ALU.mult),
                             [('cgs', j)], [('aT', jj)])
                    if j == 1:
                        prefetch()
                    yield

        cg_it = conv_gen()

        def filler(n=1):
            for _ in range(n):
                try:
                    next(cg_it)
                except StopIteration:
                    return

        MCD = [psB[:, 1024:1280], psB[:, 1280:1536], psD[:, 0:256], psD[:, 256:512]]
        MCDK = [('ps', 1, 2), ('ps', 'D', 0)]
        for ci in range(1, 9):
            c0 = 128 * ci
            cur = sb_cur[0]
            gla_AT(c0, maskC, 'maskC')
            state_khat(c0, 128, ci - 1)
            filler()
            state_update(khat, ('khat',), 128, ci, ci - 1, Sst, ('Sst',), MCD, MCDK)
            S.op('act', lambda cur=cur: act.copy(out=Sbb[1 - cur], in_=Sst[:]), reads=[('Sst', h_) for h_ in range(4)], writes=[('Sbb', 1 - cur)])

            def of(ci=ci, c0=c0, cur=cur):
                last = None
                for b_ in range(8):
                    pe.matmul(psB[:, b_ * 128:(b_ + 1) * 128], lhsT=vt[:, ci, b_ * 128:(b_ + 1) * 128], rhs=Am[:, b_ // 2, :],
                              start=True, stop=False)
                    last = pe.matmul(psB[:, b_ * 128:(b_ + 1) * 128], lhsT=Sbb[cur][:, b_ // 2, (b_ % 2) * 128:(b_ % 2 + 1) * 128],
                                     rhs=qt[:, b_ // 2, c0:c0 + 128], start=False, stop=True)
                return last
            S.op('pe', of, reads=[('vt', ci), ('Am',), ('Sbb', cur)] + [('qt', h) for h in range(4)], writes=PSB01)
            filler()
            gla_post(c0, 128)
            filler()
            sb_cur[0] = 1 - cur
        S.dma('sp', o_gla_p.rearrange("h d v -> d h v"), Sst[:], reads=[('Sst', h_) for h_ in range(4)])
        filler(100)

        def tru():
            last = None
            for j in range(8):
                pt = psC if j < 4 else psD
                last = pe.transpose(out=pt[0:34, (j % 4) * 128:(j % 4 + 1) * 128], in_=utl[:, j, :], identity=identf[:, :])
            return last
        S.op('pe', tru, reads=[('utl',), ('c', 'identf')], writes=[('ps', 'C', 0), ('ps', 'D', 0)])
        S.op('act', lambda: act.copy(out=rowbuf[0:34, 0:512], in_=psC[0:34, 0:512]), reads=[('ps', 'C', 0)], writes=[('rowbuf',)])
        S.op('act', lambda: act.copy(out=rowbuf[0:34, 512:1024], in_=psD[0:34, 0:512]), reads=[('ps', 'D', 0)], writes=[('rowbuf',)])
        S.dma('sp', o_conv_s, rowbuf[0:32, :], reads=[('rowbuf',)])
        S.dma('sp', o_conv_p, rowbuf[32:34, :], reads=[('rowbuf',)])
        if dbg == 'p1':
            for c in range(16):
                S.dma('pool', dbg_d[:, c * NT:(c + 1) * NT], RB[:, c, :], reads=[('aT', c)])

    norm_alt[0] = False
    resident.clear()
    prefetch()
    S.fence()
    es_p1.close()
    RH = sb("RH", [128, 9, D], F32)
    ws23 = sb("ws23", [128, 2, 16 * 256], BF16)
    WS.append(ws23[:, 0, :].rearrange("p (k n) -> p k n", k=16))
    WS.append(ws23[:, 1, :].rearrange("p (k n) -> p k n", k=16))
    WW.append(ws23[:].rearrange("p a x -> p (a x)").rearrange("p (k n) -> p k n", k=16))
    WSF.append(ws23[:, 0, :])
    WSF.append(ws23[:, 1, :])
    xs2_t = sb("xs2", [128, 2, D], BF16)
    del xs_bufs[2:]
    xs_bufs[0] = xs2_t[:, 0, :]
    xs_bufs[1] = xs2_t[:, 1, :]
    HK = [('h', t) for t in range(9)]
    AK = [('aT', c) for c in range(16)]
    xk = [('xT', t) for t in range(9)]
    tm_slots = [(0, 0), (1, 0), (0, 1), (1, 1), (0, 2), (1, 2)]
    tm_ctr = [0]

    norm_pending = [None]

    def norm_cb_flush():
        if norm_pending[0] is not None:
            norm_pending[0]()
            norm_pending[0] = None

    def make_norm_cb(gi):
        def cb(t):
            norm_cb_flush()
            norm_pending[0] = norm_T(RH[:, t, :], [('h', t)], 128, gi, RA, t * 128, ('xT', t), defer=True)
        return cb

    def _wview(sl, nk):
        if nk * 512 > 4096:
            return WW[sl], [('ws', 2 * sl), ('ws', 2 * sl + 1)]
        return WSF[sl][:, 0:nk * 512].rearrange("p (k n) -> p k n", k=nk), [('ws', sl)]

    def lin_tm_residual(nslab, nk, aT, akeys, defer_last=False, after_tile=None):
        n_plain = 4 if after_tile is None else 2
        for c4 in range(n_plain):
            sl = next_slab()
            wv, wkeys = _wview(sl, nk)
            dl = defer_last and c4 == 0 and nk > 1
            batches = [list(range(0, 6)), list(range(6, 9))] if dl else [[t] for t in range(9)]
            for bt in batches:
                slots = {}
                for t in bt:
                    g, b = tm_slots[tm_ctr[0] % 6]
                    tm_ctr[0] += 1
                    slots[t] = (g, b, PSG[g][:, b * 512:(b + 1) * 512])
                phases = [(0, nk - 1), (nk - 1, nk)] if dl else [(0, nk)]
                for (k0, k1) in phases:
                    for t in bt:
                        g, b, po = slots[t]

                        def fn(t=t, po=po, k0=k0, k1=k1, wv=wv):
                            last = None
                            for kc in range(k0, k1):
                                last = pe.matmul(po, lhsT=aT[:, kc, t * 128:(t + 1) * 128], rhs=wv[:, kc, 0:512],
                                                 start=(kc == 0), stop=(kc == nk - 1))
                            return last
                        S.op('pe', fn, reads=wkeys + akeys[k0:k1], writes=[('ps', g, b)])
                for t in bt:
                    g, b, po = slots[t]
                    S.op('dve', lambda t=t, c4=c4, po=po: dve.tensor_tensor(
                        out=RH[:, t, c4 * 512:(c4 + 1) * 512], in0=po, in1=RH[:, t, c4 * 512:(c4 + 1) * 512], op=ALU.add),
                        reads=[('ps', g, b), ('h', t)], writes=[('h', t)])
            prefetch()
        if after_tile is None:
            return
        slab_hold[0] = slab_state['used']
        sls = [next_slab(), next_slab()]
        views = [_wview(sl, nk) for sl in sls]
        for t in range(9):
            for c4 in (2, 3):
                wv, wkeys = views[c4 - 2]
                g, b = tm_slots[tm_ctr[0] % 6]
                tm_ctr[0] += 1
                po = PSG[g][:, b * 512:(b + 1) * 512]

                def fn(t=t, po=po, wv=wv):
                    last = None
                    for kc in range(nk):
                        last = pe.matmul(po, lhsT=aT[:, kc, t * 128:(t + 1) * 128], rhs=wv[:, kc, 0:512],
                                         start=(kc == 0), stop=(kc == nk - 1))
                    return last
                S.op('pe', fn, reads=wkeys + akeys, writes=[('ps', g, b)])
                S.op('dve', lambda t=t, c4=c4, po=po: dve.tensor_tensor(
                    out=RH[:, t, c4 * 512:(c4 + 1) * 512], in0=po, in1=RH[:, t, c4 * 512:(c4 + 1) * 512], op=ALU.add),
                    reads=[('ps', g, b), ('h', t)], writes=[('h', t)])
            after_tile(t)
        slab_hold[0] = None
        prefetch()

    if stop >= 2:
        for t in range(9):
            S.dma('sp', RH[:, t, :], x_all[t * 128:(t + 1) * 128, :], writes=[('h', t)])
        lin_tm_residual(4, KC, RB, AK, after_tile=make_norm_cb(1) if stop >= 3 else None)
        norm_cb_flush()
        if dbg == 'p2':
            for t in range(9):
                S.dma('sp', dbg_d[:, t * D:(t + 1) * D], RH[:, t, :], reads=[('h', t)])

    def lin_fm_to(dst, dkeys, xT, xkeys, func=None):
        for c8 in range(8):
            sl = next_slab()
            for j in range(2):
                c = c8 * 2 + j
                mm_fm(sl, j, xT, MT, j, xkeys=xkeys)
                S.op('act', lambda j=j, c=c: act.copy(out=dst[:, c, :], in_=PSG[j][:, 0:NT]),
                     reads=pkeys(j, 0, NT), writes=[dkeys[c]])
            prefetch()

    if stop >= 3:
        lin_fm_to(RB, AK, RA, xk)
        S.fence()
        if dbg == 'p3':
            for c in range(16):
                S.dma('pool', dbg_d[:, c * NT:(c + 1) * NT], RB[:, c, :], reads=[('aT', c)])

    if stop >= 4:
        RAf = RA[:].rearrange("p a b -> p (a b)")
        mstage = stgF[:, :]
        memnT = RAf[:, 4096:8192].rearrange("p (c m) -> p c m", c=16)
        mkT = RAf[:, 8192:12288].rearrange("p (c m) -> p c m", c=16)
        mvb = RAf[:, 12288:16384].rearrange("p (m f) -> p m f", m=2)
        xsm = RAf[:, 16384:18432]
        for mc in range(2):
            S.dma('sp', mstage, mem[mc * 128:(mc + 1) * 128, :], writes=[('mstage',)])
            norm_T(mstage, [('mstage',)], 128, 2, memnT, mc * 128, ('memnT', mc), xsb=xsm)
        S.fence()
        ostg = [stgF[:, 0:512], stgF[:, 512:1024]]
        mkt = [RAf[:, 0:512], RAf[:, 512:1024]]
        kv_ctr = [0]
        for (w_i, o_d) in ((0, o_mk), (1, o_mv)):
            for c4 in range(4):
                sl = next_slab()
                for mc in range(2):
                    i = kv_ctr[0]
                    kv_ctr[0] += 1
                    g, b = tm_slots[i % 6]
                    po = PSG[g][:, b * 512:(b + 1) * 512]
                    st_ = i % 2

                    def fn(mc=mc, sl=sl, po=po):
                        last = None
                        for kc in range(KC):
                            last = pe.matmul(po, lhsT=memnT[:, kc, mc * 128:(mc + 1) * 128], rhs=WW[sl][:, kc, 0:512],
                                             start=(kc == 0), stop=(kc == KC - 1))
                        return last
                    S.op('pe', fn, reads=[('ws', 2 * sl), ('ws', 2 * sl + 1), ('memnT', 0), ('memnT', 1)], writes=[('ps', g, b)])
                    S.op('act', lambda po=po, st_=st_: act.copy(out=ostg[st_], in_=po), reads=[('ps', g, b)],
                         writes=[('ostg', st_)])
                    S.dma('sp', o_d[mc * 128:(mc + 1) * 128, c4 * 512:(c4 + 1) * 512], ostg[st_], reads=[('ostg', st_)])
                    if w_i == 1:
                        S.op('dve', lambda po=po, mc=mc, c4=c4: dve.tensor_scalar(out=mvb[:, mc, c4 * 512:(c4 + 1) * 512], in0=po,
                                                                                  scalar1=1.0, scalar2=None, op0=ALU.mult),
                             reads=[('ps', g, b)], writes=[('mvb',)])
                    else:
                        S.op('dve', lambda po=po, st_=st_: dve.tensor_scalar(out=mkt[st_], in0=po, scalar1=1.0, scalar2=None, op0=ALU.mult),
                             reads=[('ps', g, b)], writes=[('mkt', st_)])

                        def trk(st_=st_):
                            last = None
                            for j in range(4):
                                last = pe.transpose(out=pvD[:, j * 128:(j + 1) * 128], in_=mkt[st_][:, j * 128:(j + 1) * 128],
                                                    identity=identb[:, :])
                            return last
                        S.op('pe', trk, reads=[('mkt', st_), ('c', 'identb')], writes=[('ps', 'D', 0)])
                        S.op('act', lambda mc=mc, c4=c4: act.copy(
                            out=mkT[:, c4 * 4:c4 * 4 + 4, mc * 128:(mc + 1) * 128],
                            in_=pvD[:, 0:512].rearrange("p (j m) -> p j m", j=4)),
                            reads=[('ps', 'D', 0)], writes=[('mkT',)])
                prefetch()

        qs = RAf[:, 1536:2560].rearrange("p (c t) -> p c t", c=16)
        S.op('dve', lambda: dve.tensor_scalar(out=qs, in0=RB[:, :, 0:64], scalar1=1.0, scalar2=None, op0=ALU.mult), reads=AK, writes=[('qs',)])
        pTs2 = [RAf[:, 2560:3584].rearrange("p (m t) -> p m t", m=2), RAf[:, 0:1024].rearrange("p (m t) -> p m t", m=2)]
        rden = RAf[:, 3584:4608].bitcast(F32)
        SC = 512 ** -0.5
        st_sets = [[(psA[:, 0:512], ('ps', 0, 0)), (psA[:, 512:1024], ('ps', 0, 1))],
                   [(psA[:, 1024:1536], ('ps', 0, 2)), (psC[:, :], ('ps', 'C', 0))]]
        o_slots = [(psB, 0, ('ps', 1, 0)), (psB, 512, ('ps', 1, 1)), (psB, 1024, ('ps', 1, 2))]
        steps = [(h, a, b) for h in range(4) for (a, b) in MT]
        o_ctr = [0]

        def emit_sf(i):
            h, a, b = steps[i]
            n = b - a
            pT = pTs2[i % 2]
            for mc in range(2):
                pa, pk = st_sets[i % 2][mc]

                def sf(h=h, a=a, b=b, n=n, mc=mc, pa=pa):
                    last = None
                    for dc in range(4):
                        last = pe.matmul(pa[:, 0:n], lhsT=mkT[:, 4 * h + dc, mc * 128:(mc + 1) * 128],
                                         rhs=RB[:, 4 * h + dc, a:b], start=(dc == 0), stop=(dc == 3))
                    return last
                S.op('pe', sf, reads=[('mkT',)] + [('aT', 4 * h + dc) for dc in range(4)], writes=[pk])
            for mc in range(2):
                pa, pk = st_sets[i % 2][mc]
                S.op('act', lambda n=n, mc=mc, pa=pa, pT=pT: act.activation(out=pT[:, mc, 0:n], in_=pa[:, 0:n], func=AF.Exp, scale=SC),
                     reads=[pk], writes=[('pT', i % 2, mc)])

        def emit_rest(i):
            h, a, b = steps[i]
            n = b - a
            pT = pTs2[i % 2]
            ptk = [('pT', i % 2, 0), ('pT', i % 2, 1)]

            def df(n=n, pT=pT):
                last = None
                for mc in range(2):
                    last = pe.matmul(psD[:, 0:n], lhsT=onesb[:, :], rhs=pT[:, mc, 0:n], start=(mc == 0), stop=(mc == 1))
                return last
            S.op('pe', df, reads=ptk + [('c', 'onesb')], writes=[('ps', 'D', 0)])
            S.op('dve', lambda n=n: dve.reciprocal(out=rden[:, 0:n], in_=psD[:, 0:n]), reads=[('ps', 'D', 0)], writes=[('rden',)])
            for dc in range(4):
                pt_, off, pk = o_slots[o_ctr[0] % 3]
                o_ctr[0] += 1

                def of(h=h, dc=dc, n=n, pt_=pt_, off=off, pT=pT):
                    last = None
                    for mc in range(2):
                        last = pe.matmul(pt_[:, off:off + n], lhsT=mvb[:, mc, (4 * h + dc) * 128:(4 * h + dc + 1) * 128],
                                         rhs=pT[:, mc, 0:n], start=(mc == 0), stop=(mc == 1))
                    return last
                S.op('pe', of, reads=ptk + [('mvb',)], writes=[pk])
                S.op('dve', lambda h=h, dc=dc, a=a, b=b, n=n, pt_=pt_, off=off: dve.tensor_tensor(
                    out=RB[:, 4 * h + dc, a:b], in0=pt_[:, off:off + n], in1=rden[:, 0:n], op=ALU.mult),
                    reads=[pk, ('rden',)], writes=[('aT', 4 * h + dc)])

        emit_sf(0)
        for i in range(len(steps)):
            if i + 1 < len(steps):
                emit_sf(i + 1)
            emit_rest(i)
        S.fence()

        NB = 4
        kvb = RAf[:, 4608:4608 + NB * 2048].rearrange("p (n x) -> p n x", n=NB)
        kTb = [RAf[:, 12800 + i * 1024:12800 + (i + 1) * 1024].rearrange("p (c m) -> p c m", c=4) for i in range(2)]
        pTs = RAf[:, 14848:14848 + 16].rearrange("p (i m t) -> p i m t", i=2, m=2)
        rds = RAf[:, 14880:14896].bitcast(F32).rearrange("p (i t) -> p i t", i=2)
        it = 0
        for s_ in range(NSEQ):
            for h in range(4):
                sl = it % NB
                i2 = it % 2
                it += 1
                kb = kvb[:, sl, 0:1024].rearrange("p (m d) -> p m d", m=2)
                vb = kvb[:, sl, 1024:2048].rearrange("p (m d) -> p m d", m=2)
                S.dma('pool', kb, cmk_d[s_, :, h * 512:(h + 1) * 512].rearrange("(m p) d -> p m d", p=128),
                      writes=[('kvb', sl, 0)])
                S.dma('pool', vb, cmv_d[s_, :, h * 512:(h + 1) * 512].rearrange("(m p) d -> p m d", p=128),
                      writes=[('kvb', sl, 1)])
                pz = psC if i2 == 0 else psD
                pzk = ('ps', 'C' if i2 == 0 else 'D', 0)
                pzb = pz[:].bitcast(BF16)

                def trf(kb=kb, pzb=pzb):
                    last = None
                    for mc in range(2):
                        for dc in range(4):
                            last = pe.transpose(out=pzb[:, dc * 256 + mc * 128:dc * 256 + (mc + 1) * 128],
                                                in_=kb[:, mc, dc * 128:(dc + 1) * 128], identity=identb[:, :])
                    return last
                S.op('pe', trf, reads=[('kvb', sl, 0), ('c', 'identb')], writes=[pzk])
                S.op('act', lambda i2=i2, pzb=pzb: act.copy(out=kTb[i2].rearrange("p c m -> p (c m)"), in_=pzb[:, 0:1024]),
                     reads=[pzk], writes=[('kTb', i2)])
                pw = PSG[i2]
                pwk = [('ps', i2, 0), ('ps', i2, 1), ('ps', i2, 2)]

                def scf(i2=i2, h=h, s_=s_, pw=pw):
                    last = None
                    for mc in range(2):
                        for dc in range(4):
                            last = pe.matmul(pw[:, mc * 4:mc * 4 + 4], lhsT=kTb[i2][:, dc, mc * 128:(mc + 1) * 128],
                                             rhs=qs[:, 4 * h + dc, 4 * s_:4 * s_ + 4], start=(dc == 0), stop=(dc == 3))
                    return last
                S.op('pe', scf, reads=[('kTb', i2), ('qs',)], writes=[pwk[0]])
                S.op('act', lambda i2=i2, pw=pw: act.activation(out=pTs[:, i2].rearrange("p m t -> p (m t)"), in_=pw[:, 0:8],
                                                                func=AF.Exp, scale=SC),
                     reads=[pwk[0]], writes=[('pTs', i2)])

                def pvf(i2=i2, pw=pw, vb=vb):
                    last = None
                    for mc in range(2):
                        last = pe.matmul(pw[:, 512:516], lhsT=onesb[:, :], rhs=pTs[:, i2, mc, :], start=(mc == 0), stop=(mc == 1))
                    for dc in range(4):
                        for mc in range(2):
                            last = pe.matmul(pw[:, 1024 + dc * 4:1024 + dc * 4 + 4], lhsT=vb[:, mc, dc * 128:(dc + 1) * 128],
                                             rhs=pTs[:, i2, mc, :], start=(mc == 0), stop=(mc == 1))
                    return last
                S.op('pe', pvf, reads=[('pTs', i2), ('kvb', sl, 1), ('c', 'onesb')], writes=[pwk[1], pwk[2]])
                S.op('dve', lambda i2=i2, pw=pw: dve.reciprocal(out=rds[:, i2, :], in_=pw[:, 512:516]),
                     reads=[pwk[1]], writes=[('rds', i2)])
                S.op('dve', lambda i2=i2, pw=pw, h=h, s_=s_: dve.tensor_tensor(
                    out=RB[:, 4 * h:4 * h + 4, 4 * s_:4 * s_ + 4], in0=pw[:, 1024:1040].rearrange("p (c t) -> p c t", c=4),
                    in1=rds[:, i2, :].unsqueeze(1).broadcast_to([128, 4, 4]), op=ALU.mult),
                    reads=[pwk[2], ('rds', i2)], writes=[('aT', 4 * h + dc) for dc in range(4)])
        S.fence()
        if dbg == 'p4':
            for c in range(16):
                S.dma('pool', dbg_d[:, c * NT:(c + 1) * NT], RB[:, c, :], reads=[('aT', c)])

    if stop >= 5:
        lin_tm_residual(4, KC, RB, AK, after_tile=make_norm_cb(3) if stop >= 6 else None)
        norm_cb_flush()
        if dbg == 'p5':
            for t in range(9):
                S.dma('sp', dbg_d[:, t * D:(t + 1) * D], RH[:, t, :], reads=[('h', t)])

    if stop >= 6:
        S.op('dve', lambda: dve.memset(stat[:, 6:7], 0.0), reads=[],
             writes=AK + [('gs',), ('cc',), ('pcg',), ('gtl',)] + [('act', ci) for ci in range(8)])
        RBf = RB[:].rearrange("p a b -> p (a b)")
        actb = RB[:, 0:8, :]
        gs = RBf[:, 8 * NT:10 * NT].bitcast(F32)
        cc = RBf[:, 10 * NT:12 * NT].bitcast(F32)
        misc = RBf[:, 12 * NT:16 * NT].bitcast(F32)
        pcg = misc[:, 0:256].rearrange("p (j r) -> p j r", j=8)
        gtl = misc[:, 256:256 + 272].rearrange("p (j r) -> p j r", j=8)
        rowb = stgF[0:34, 0:1024]
        rowc = stgF[0:32, 1024:2048]

        def load_cache(gi_):
            f0_, f1_ = FGROUPS[gi_]
            ng_ = f1_ - f0_
            S.dma('sp', rowc[0:32, 0:ng_ * 128], cffn_d[:, f0_ * 128:f1_ * 128], writes=[('rowc',)])

            def trc2():
                last = None
                for j in range(ng_):
                    last = pe.transpose(out=psC[:, j * 32:(j + 1) * 32], in_=rowc[0:32, j * 128:(j + 1) * 128],
                                        identity=identf[0:32, 0:32])
                return last
            S.op('pe', trc2, reads=[('rowc',), ('c', 'identf')], writes=[('ps', 'C', 0)])
            S.op('act', lambda: act.copy(out=pcg.rearrange("p j r -> p (j r)")[:, 0:ng_ * 32], in_=psC[:, 0:ng_ * 32]),
                 reads=[('ps', 'C', 0)], writes=[('pcg',)])

        load_cache(0)
        for gi_, (f0, f1) in enumerate(FGROUPS):
            ng = f1 - f0
            for ci in range(ng):
                fc = f0 + ci
                sl = next_slab()
                mm_fm(sl, 0, RA, MT, 0, xkeys=xk)
                prefetch()
                S.op('act', lambda: act.copy(out=gs, in_=psA[:, 0:NT]), reads=pkeys(0, 0, NT), writes=[('gs',)])
                sl = next_slab()
                mm_fm(sl, 0, RA, MT, 1, xkeys=xk)
                prefetch()
                S.op('act', lambda ci=ci: act.copy(out=gtl[:, ci, 0:32].rearrange("p (s r) -> p s r", s=16),
                                                  in_=gs[:, 0:64].rearrange("p (s t) -> p s t", s=16)[:, :, 2:4]),
                     reads=[('gs',)], writes=[('gtl',)])
                S.op('act', lambda ci=ci: act.copy(out=gtl[:, ci, 32:34], in_=gs[:, NT - 2:NT]), reads=[('gs',)], writes=[('gtl',)])
                S.op('dve', lambda: dve.tensor_scalar(out=gs[:, 64:128], in0=gs[:, 64:128], scalar1=ovm[:, 0:1], scalar2=None,
                                                      op0=ALU.mult), reads=[('gs',), ('c', 'ovm')], writes=[('gs',)])
                S.op('dve', lambda fc=fc: dve.tensor_scalar(out=cc, in0=gs, scalar1=fcw[:, 2, fc:fc + 1], scalar2=fcb[:, fc:fc + 1],
                                                            op0=ALU.mult, op1=ALU.add),
                     reads=[('gs',), ('c', 'fcw'), ('c', 'fcb')], writes=[('cc',)])
                y3 = cc[:, 0:64].rearrange("p (s t) -> p s t", s=16)
                u3 = gs[:, 0:64].rearrange("p (s t) -> p s t", s=16)
                p3 = pcg[:, ci, :].rearrange("p (s r) -> p s r", s=16)
                for (tap, sh) in ((1, 1), (0, 2)):
                    S.op('dve', lambda fc=fc, tap=tap, sh=sh: dve.scalar_tensor_tensor(
                        out=cc[:, 64 + sh:NT], in0=gs[:, 64:NT - sh], scalar=fcw[:, tap, fc:fc + 1], in1=cc[:, 64 + sh:NT],
                        op0=ALU.mult, op1=ALU.add), reads=[('gs',), ('c', 'fcw')], writes=[('cc',)])
                    S.op('dve', lambda fc=fc, tap=tap, sh=sh, y3=y3, u3=u3: dve.scalar_tensor_tensor(
                        out=y3[:, :, sh:4], in0=u3[:, :, 0:4 - sh], scalar=fcw[:, tap, fc:fc + 1], in1=y3[:, :, sh:4],
                        op0=ALU.mult, op1=ALU.add), reads=[('gs',), ('c', 'fcw')], writes=[('cc',)])
                S.op('dve', lambda fc=fc, y3=y3, p3=p3: dve.scalar_tensor_tensor(
                    out=y3[:, :, 0:1], in0=p3[:, :, 1:2], scalar=fcw[:, 1, fc:fc + 1], in1=y3[:, :, 0:1],
                    op0=ALU.mult, op1=ALU.add), reads=[('pcg',), ('c', 'fcw')], writes=[('cc',)])
                S.op('dve', lambda fc=fc, y3=y3, p3=p3: dve.scalar_tensor_tensor(
                    out=y3[:, :, 0:2], in0=p3[:, :, 0:2], scalar=fcw[:, 0, fc:fc + 1], in1=y3[:, :, 0:2],
                    op0=ALU.mult, op1=ALU.add), reads=[('pcg',), ('c', 'fcw')], writes=[('cc',)])
                S.op('act', lambda: act.activation(out=cc, in_=cc, func=AF.Silu), reads=[('cc',)], writes=[('cc',)])
                S.op('dve', lambda ci=ci: dve.tensor_tensor(out=actb[:, ci, :], in0=psB[:, 0:NT], in1=cc, op=ALU.mult),
                     reads=pkeys(1, 0, NT) + [('cc',)], writes=[('act', ci)])
            if gi_ + 1 < len(FGROUPS):
                load_cache(gi_ + 1)
            lin_tm_residual(4, ng, actb, [('act', ci) for ci in range(ng)], defer_last=True)

            def trg(ng=ng):
                last = None
                for j in range(ng):
                    pt = psC if j < 4 else psD
                    last = pe.transpose(out=pt[0:34, (j % 4) * 128:(j % 4 + 1) * 128], in_=gtl[:, j, :], identity=identf[:, :])
                return last
            S.op('pe', trg, reads=[('gtl',), ('c', 'identf')], writes=[('ps', 'C', 0), ('ps', 'D', 0)])
            n1 = min(ng, 4) * 128
            S.op('act', lambda n1=n1: act.copy(out=rowb[0:34, 0:n1], in_=psC[0:34, 0:n1]), reads=[('ps', 'C', 0)], writes=[('rowb',)])
            if ng > 4:
                S.op('act', lambda ng=ng: act.copy(out=rowb[0:34, 512:ng * 128], in_=psD[0:34, 0:(ng - 4) * 128]),
                     reads=[('ps', 'D', 0)], writes=[('rowb',)])
            S.dma('sp', o_ffn_s[:, f0 * 128:f1 * 128], rowb[0:32, 0:ng * 128], reads=[('rowb',)])
            S.dma('sp', o_ffn_p[:, f0 * 128:f1 * 128], rowb[32:34, 0:ng * 128], reads=[('rowb',)])
        S.fence()

    if stop >= 7:
        gbc = stgF[:, :]
        with nc.allow_non_contiguous_dma(reason="gain broadcast"):
            S.dma('sp', gbc, gains_d[4:5, :].partition_broadcast(128).rearrange("p o d -> p (o d)"), writes=[('gbc',)])
        for t in range(9):
            s2 = t % 2
            ss = stat[:, 2 * s2:2 * s2 + 1]
            rs = stat[:, 2 * s2 + 1:2 * s2 + 2]
            junk = xs_bufs[s2]
            S.op('act', lambda t=t, junk=junk, ss=ss: act.activation(out=junk, in_=RH[:, t, :], func=AF.Square, accum_out=ss),
                 reads=[('h', t)], writes=[('junk', s2), ('st', s2)])
            S.op('act', lambda ss=ss: act.activation(out=ss, in_=ss, func=AF.Sqrt, scale=1.0 / D, bias=EPS),
                 reads=[('st', s2)], writes=[('st', s2)])
            S.op('dve', lambda ss=ss, rs=rs: dve.reciprocal(out=rs, in_=ss), reads=[('st', s2)], writes=[('rs', s2)])
            S.op('dve', lambda t=t, rs=rs: dve.scalar_tensor_tensor(out=RH[:, t, :], in0=RH[:, t, :], scalar=rs, in1=gbc,
                                                                    op0=ALU.mult, op1=ALU.mult),
                 reads=[('h', t), ('rs', s2), ('gbc',)], writes=[('h', t)])
            S.dma('sp', y_all[t * 128:(t + 1) * 128, :], RH[:, t, :], reads=[('h', t)])

    S.finish()
    es.close()
    return nc


def make_consts():
    c = np.zeros((128, 128 * 3 + 17), np.float32)
    c[:, 0:128] = np.eye(128, dtype=np.float32)
    j = np.arange(128)[:, None]
    i = np.arange(128)[None, :]
    c[:, 128:256] = (j <= i).astype(np.float32)
    m0 = np.zeros((128, 128), np.float32)
    samp = (j < 64) & (i < 64) & ((j // 4) == (i // 4)) & (j <= i)
    ovl = (j >= 64) & (i >= 64) & (j <= i)
    m0[samp | ovl] = 1.0
    c[:, 256:384] = m0
    for s in range(16):
        c[4 * s:4 * s + 4, 384 + s] = 1.0
    c[64:128, 384 + 16] = 1.0
    return c


def core_inputs(inp, c):
    seq, half = c // 2, c % 2
    xp = inp["x_prompt"][seq]
    pos0 = half * 1024
    x_all = np.zeros((NT, D), np.float32)
    x_all[0:64] = inp["x_sample"][16 * c:16 * c + 16].reshape(64, D)
    if half == 1:
        x_all[64:128] = xp[pos0 - 64:pos0]
    x_all[128:] = xp[pos0:pos0 + 1024]
    x_pre = np.zeros((1024, D), np.float32)
    if half == 1:
        x_pre[0:NPRE] = xp[0:NPRE]
    m = {
        "x_all": x_all, "x_pre": x_pre, "mem": np.ascontiguousarray(inp["mem_prompt"][seq]),
        "ovm": np.full((128, 1), float(half), np.float32),
        "cache_conv": np.ascontiguousarray(inp["cache_conv"][0, 16 * c:16 * c + 16].reshape(32, 1024)),
        "state_gla": np.ascontiguousarray(inp["state_gla"][0, 16 * c:16 * c + 16]),
        "cache_ffn": np.ascontiguousarray(inp["cache_ffn"][0, 16 * c:16 * c + 16].reshape(32, DFF)),
        "cache_mem_k": np.ascontiguousarray(inp["cache_mem_k"][0, 16 * c:16 * c + 16].reshape(16, 256, D)),
        "cache_mem_v": np.ascontiguousarray(inp["cache_mem_v"][0, 16 * c:16 * c + 16].reshape(16, 256, D)),
        "w_in": inp["w_in"][0], "w_out": inp["w_out"][0], "w_xq": inp["w_xq"][0], "w_xk": inp["w_xk"][0],
        "w_xv": inp["w_xv"][0], "w_xo": inp["w_xo"][0], "w_ffn_gate": inp["w_ffn_gate"][0],
        "w_ffn_up": inp["w_ffn_up"][0], "w_ffn_down": inp["w_ffn_down"][0],
        "gains": np.stack([inp["norm_mix"][0], inp["norm_x"][0], inp["norm_mem"][0], inp["norm_ffn"][0],
                           inp["norm_final"]]).astype(np.float32),
        "conv_w": inp["conv_w"][0], "w_gate2": inp["w_gate2"][0], "b_gate": inp["b_gate"],
        "gla_norm": inp["gla_norm"], "ffn_conv_w": inp["ffn_conv_w"][0], "ffn_conv_b": inp["ffn_conv_b"],
        "consts": make_consts(),
    }
    return {k: np.ascontiguousarray(np.asarray(v, dtype=np.float32)) for k, v in m.items()}


_NC_CACHE = {}


def kernel(**inputs):
    inp = {k: np.asarray(v) for k, v in inputs.items()}
    if 'nc' not in _NC_CACHE:
        _NC_CACHE['nc'] = build()
    nc = _NC_CACHE['nc']
    in_maps = [core_inputs(inp, c) for c in range(8)]
    res = run_bass_kernel_spmd(nc, in_maps, core_ids=list(range(8)))
    R = res.results
    y_prompt = np.zeros((4, 2048, D), np.float32)
    y_sample = np.zeros((128, 4, D), np.float32)
    conv_p = np.zeros((1, 4, 2, 1024), np.float32)
    gla_p = np.zeros((1, 4, 4, 128, 256), np.float32)
    ffn_p = np.zeros((1, 4, 2, DFF), np.float32)
    mk_p = np.zeros((1, 4, 256, 4, 512), np.float32)
    mv_p = np.zeros((1, 4, 256, 4, 512), np.float32)
    conv_s = np.zeros((1, 128, 2, 1024), np.float32)
    gla_s = np.zeros((1, 128, 4, 128, 256), np.float32)
    ffn_s = np.zeros((1, 128, 2, DFF), np.float32)
    for c in range(8):
        seq, half = c // 2, c % 2
        r = R[c]
        y_prompt[seq, half * 1024:(half + 1) * 1024] = r["y_all"][128:NT]
        y_sample[16 * c:16 * c + 16] = r["y_all"][0:64].reshape(16, 4, D)
        conv_s[0, 16 * c:16 * c + 16] = r["o_conv_s"].reshape(16, 2, 1024)
        gla_s[0, 16 * c:16 * c + 16] = r["o_gla_s"]
        ffn_s[0, 16 * c:16 * c + 16] = r["o_ffn_s"].reshape(16, 2, DFF)
        if half == 1:
            conv_p[0, seq] = r["o_conv_p"]
            gla_p[0, seq] = r["o_gla_p"]
            ffn_p[0, seq] = r["o_ffn_p"]
        else:
            mk_p[0, seq] = r["o_mk"].reshape(256, 4, 512)
            mv_p[0, seq] = r["o_mv"].reshape(256, 4, 512)
    return (y_prompt, y_sample, conv_p, gla_p, ffn_p, mk_p, mv_p, conv_s, gla_s, ffn_s)
```
